# Optimizing a Trainium2 kernel written in Bass

```python
import jax, jax.numpy as jnp
from jax import lax
import numpy as np

D_MODEL = 2048
BATCH = 8
SEQ = 4096
DEPTH = 4

N_MIXERS = 2
N_CONV_LAYERS = (DEPTH + N_MIXERS - 1) // N_MIXERS
N_ATTN_LAYERS = DEPTH // N_MIXERS
CONV_WIDTH = 31
HEAD_DIM = 128
N_HEADS = D_MODEL // HEAD_DIM
N_KV_HEADS = 4
GROUP = N_HEADS // N_KV_HEADS
QKV_DIM = (N_HEADS + 2 * N_KV_HEADS) * HEAD_DIM
ROPE_THETA = 10000.0
ROPE_AXIS_DIM = HEAD_DIM // 2
GRID_W = 64
Q_BLOCK = 128
D_FF = 4 * D_MODEL
ALPHA = (2 * DEPTH) ** 0.25
BETA = (8 * DEPTH) ** -0.25
LN_EPS = 1e-5
RMS_EPS = 1e-6

kernel_name = "hybrid_conv_gqa_axialrope_deepnorm_adaln"


def layer_norm(x, g, b):
    xf = x.astype(jnp.float32)
    mu = jnp.mean(xf, axis=-1, keepdims=True)
    xc = xf - mu
    var = jnp.mean(xc * xc, axis=-1, keepdims=True)
    y = xc * lax.rsqrt(var + LN_EPS) * g.astype(jnp.float32) + b.astype(jnp.float32)
    return y.astype(x.dtype)


def rms_norm(x, g):
    xf = x.astype(jnp.float32)
    y = xf * lax.rsqrt(jnp.mean(xf * xf, axis=-1, keepdims=True) + RMS_EPS) * g.astype(jnp.float32)
    return y.astype(x.dtype)


def modulation(c_act, w, b):
    m = c_act @ w + b
    shift, scale, gate = jnp.split(m, 3, axis=-1)
    return shift[:, None, :], scale[:, None, :], gate[:, None, :]


def axial_rope_tables(seq_len, dtype):
    rows = seq_len // GRID_W
    row = jnp.repeat(jnp.arange(rows, dtype=jnp.int32), GRID_W)
    col = jnp.tile(jnp.arange(GRID_W, dtype=jnp.int32), rows)
    half = ROPE_AXIS_DIM // 2
    inv_freq = ROPE_THETA ** (-jnp.arange(half, dtype=jnp.float32) / half)
    ang_r = row.astype(jnp.float32)[:, None] * inv_freq[None, :]
    ang_c = col.astype(jnp.float32)[:, None] * inv_freq[None, :]
    return (jnp.cos(ang_r).astype(dtype), jnp.sin(ang_r).astype(dtype),
            jnp.cos(ang_c).astype(dtype), jnp.sin(ang_c).astype(dtype))


def rope_half(x, cos, sin):
    x1, x2 = jnp.split(x, 2, axis=-1)
    cos = cos[None, :, None, :]
    sin = sin[None, :, None, :]
    return jnp.concatenate([x1 * cos - x2 * sin, x2 * cos + x1 * sin], axis=-1)


def apply_axial_rope(x, tables):
    cos_r, sin_r, cos_c, sin_c = tables
    xr = x[..., :ROPE_AXIS_DIM]
    xc = x[..., ROPE_AXIS_DIM:]
    return jnp.concatenate([rope_half(xr, cos_r, sin_r), rope_half(xc, cos_c, sin_c)], axis=-1)


def conformer_conv(h, w_in, b_in, dw, dw_b, ln_g, ln_b, w_out):
    d = h.shape[-1]
    u = h @ w_in + b_in
    a, g = jnp.split(u, 2, axis=-1)
    u = a * jax.nn.sigmoid(g)
    pad = CONV_WIDTH // 2
    u = lax.conv_general_dilated(
        u, dw[:, None, :].astype(u.dtype), window_strides=(1,), padding=[(pad, pad)],
        dimension_numbers=("NWC", "WIO", "NWC"), feature_group_count=d) + dw_b
    u = jax.nn.silu(layer_norm(u, ln_g, ln_b))
    return u @ w_out


def gqa_axial(h, w_qkv, q_norm, k_norm, w_out, tables):
    bsz, seq, _ = h.shape
    qkv = h @ w_qkv
    q, k, v = jnp.split(qkv, [N_HEADS * HEAD_DIM, (N_HEADS + N_KV_HEADS) * HEAD_DIM], axis=-1)
    q = q.reshape(bsz, seq, N_HEADS, HEAD_DIM)
    k = k.reshape(bsz, seq, N_KV_HEADS, HEAD_DIM)
    v = v.reshape(bsz, seq, N_KV_HEADS, HEAD_DIM)
    q = apply_axial_rope(rms_norm(q, q_norm), tables)
    k = apply_axial_rope(rms_norm(k, k_norm), tables)
    n_blocks = seq // Q_BLOCK
    qb = q.reshape(bsz, n_blocks, Q_BLOCK, N_KV_HEADS, GROUP, HEAD_DIM).transpose(1, 0, 2, 3, 4, 5)
    scale = HEAD_DIM ** -0.5

    def attend(q_blk):
        s = jnp.einsum("bqkgd,bskd->bkgqs", q_blk, k).astype(jnp.float32) * scale
        p = jax.nn.softmax(s, axis=-1).astype(v.dtype)
        return jnp.einsum("bkgqs,bskd->bqkgd", p, v)

    o = lax.map(attend, qb)
    o = o.transpose(1, 0, 2, 3, 4, 5).reshape(bsz, seq, N_HEADS * HEAD_DIM)
    return o @ w_out


def sq_relu_mlp(h, w_in, w_out):
    u = jax.nn.relu(h @ w_in)
    return (u * u) @ w_out


def setup_inputs(seed: int = 0) -> dict:
    key = jax.random.key(seed)
    ks = jax.random.split(key, 22)
    f32 = jnp.float32
    nrm = lambda k, shape, s: jax.random.normal(k, shape, f32) * s
    D = D_MODEL
    return {
        "x": nrm(ks[0], (BATCH, SEQ, D), 1.0),
        "c": nrm(ks[1], (BATCH, D), 1.0),
        "mod_w": nrm(ks[2], (DEPTH, 2, D, 3 * D), 0.1 * D ** -0.5),
        "mod_b": nrm(ks[3], (DEPTH, 2, 3 * D), 0.01),
        "ln_g": 1.0 + nrm(ks[4], (DEPTH, 2, D), 0.02),
        "ln_b": nrm(ks[5], (DEPTH, 2, D), 0.02),
        "conv_w_in": nrm(ks[6], (N_CONV_LAYERS, D, 2 * D), D ** -0.5),
        "conv_b_in": nrm(ks[7], (N_CONV_LAYERS, 2 * D), 0.02),
        "conv_dw": nrm(ks[8], (N_CONV_LAYERS, CONV_WIDTH, D), CONV_WIDTH ** -0.5),
        "conv_dw_b": nrm(ks[9], (N_CONV_LAYERS, D), 0.02),
        "conv_ln_g": 1.0 + nrm(ks[10], (N_CONV_LAYERS, D), 0.02),
        "conv_ln_b": nrm(ks[11], (N_CONV_LAYERS, D), 0.02),
        "conv_w_out": nrm(ks[12], (N_CONV_LAYERS, D, D), BETA * D ** -0.5),
        "attn_w_qkv": nrm(ks[13], (N_ATTN_LAYERS, D, QKV_DIM), D ** -0.5),
        "attn_q_norm": 1.0 + nrm(ks[14], (N_ATTN_LAYERS, HEAD_DIM), 0.02),
        "attn_k_norm": 1.0 + nrm(ks[15], (N_ATTN_LAYERS, HEAD_DIM), 0.02),
        "attn_w_out": nrm(ks[16], (N_ATTN_LAYERS, N_HEADS * HEAD_DIM, D), BETA * (N_HEADS * HEAD_DIM) ** -0.5),
        "mlp_w_in": nrm(ks[17], (DEPTH, D, D_FF), D ** -0.5),
        "mlp_w_out": nrm(ks[18], (DEPTH, D_FF, D), BETA * D_FF ** -0.5),
    }


def reference(x, c, mod_w, mod_b, ln_g, ln_b, conv_w_in, conv_b_in, conv_dw, conv_dw_b,
              conv_ln_g, conv_ln_b, conv_w_out, attn_w_qkv, attn_q_norm, attn_k_norm,
              attn_w_out, mlp_w_in, mlp_w_out):
    c_act = jax.nn.silu(c)
    tables = axial_rope_tables(x.shape[1], x.dtype)
    for i in range(DEPTH):
        shift, scale, gate = modulation(c_act, mod_w[i, 0], mod_b[i, 0])
        h = x * (1 + scale) + shift
        j = i // N_MIXERS
        if i % N_MIXERS == 0:
            y = conformer_conv(h, conv_w_in[j], conv_b_in[j], conv_dw[j], conv_dw_b[j],
                               conv_ln_g[j], conv_ln_b[j], conv_w_out[j])
        else:
            y = gqa_axial(h, attn_w_qkv[j], attn_q_norm[j], attn_k_norm[j], attn_w_out[j], tables)
        x = layer_norm(ALPHA * x + (1 + gate) * y, ln_g[i, 0], ln_b[i, 0])
        shift, scale, gate = modulation(c_act, mod_w[i, 1], mod_b[i, 1])
        h = x * (1 + scale) + shift
        y = sq_relu_mlp(h, mlp_w_in[i], mlp_w_out[i])
        x = layer_norm(ALPHA * x + (1 + gate) * y, ln_g[i, 1], ln_b[i, 1])
    return x
```

```python
import numpy as np
from contextlib import ExitStack
import concourse.bass as bass
import concourse.mybir as mybir
from concourse.bass_utils import run_bass_kernel_spmd

F32 = mybir.dt.float32
BF16 = mybir.dt.bfloat16
AF = mybir.ActivationFunctionType
ALU = mybir.AluOpType

D = 2048
KC = 16
T = 512
DEPTH = 4
DFF = 8192
ALPHA = (2 * DEPTH) ** 0.25
LN_EPS = 1e-5
RMS_EPS = 1e-6
CW = 31
UP = 16

V_C = 0
V_MODB = V_C + 16
V_LNG = V_MODB + 8 * 48
V_LNB = V_LNG + 8 * 16
V_CBIN = V_LNB + 8 * 16
V_CDW = V_CBIN + 2 * 32
V_CDWB = V_CDW + 2 * 31 * 16
V_CLNG = V_CDWB + 2 * 16
V_CLNB = V_CLNG + 2 * 16
V_QN = V_CLNB + 2 * 16
V_KN = V_QN + 2
NV = V_KN + 2


_UC = [0]


def _u(name):
    _UC[0] += 1
    return f"{name}_{_UC[0]}"


class Buf:
    __slots__ = ("name", "w", "r")

    def __init__(self, name=""):
        self.name = name
        self.w = None
        self.r = {}


class Prog:
    COMPUTE = ("pe", "act", "dve", "pool")
    ENG = ("pe", "act", "dve", "pool", "sp")

    def __init__(self, nc, stack, n_dma_sems=14):
        self.nc = nc
        self.stack = stack
        self.ops = {e: [] for e in self.ENG}
        self.sems = []
        self.own = {}
        self.cnt = {}
        self.n_own = 0
        self.sig = {}
        for e in self.COMPUTE:
            self._new_own(e)
        self.seen = {e: {} for e in self.ENG}
        self.dma_sems = {}
        self.dma_rr = {}
        self.dma_val = {}
        for q in ("sp", "pool"):
            lst = []
            for i in range(n_dma_sems):
                lst.append(len(self.sems))
                self.dma_val[len(self.sems)] = 0
                self.sems.append(stack.enter_context(nc.semaphore(f"d_{q}{i}")))
            self.dma_sems[q] = lst
            self.dma_rr[q] = 0
        self.nops = 0

    def _new_own(self, e):
        self.own[e] = len(self.sems)
        self.sems.append(self.stack.enter_context(self.nc.semaphore(f"s_{e}{self.n_own}")))
        self.n_own += 1
        self.cnt[e] = 0

    def _deps(self, e, reads, writes, extra=()):
        need = {}
        for b in reads:
            if b.w is not None:
                s, v = b.w
                if need.get(s, 0) < v:
                    need[s] = v
        for b in writes:
            if b.w is not None:
                s, v = b.w
                if need.get(s, 0) < v:
                    need[s] = v
            for s, v in b.r.items():
                if need.get(s, 0) < v:
                    need[s] = v
        for s, v in extra:
            if need.get(s, 0) < v:
                need[s] = v
        own = self.own.get(e)
        seen = self.seen[e]
        for s, v in need.items():
            if s == own and e == "pe":
                continue
            if seen.get(s, 0) >= v:
                continue
            seen[s] = v
            self.ops[e].append(("wait", s, v))

    def _mark(self, tok, reads, writes):
        s, v = tok
        for b in reads:
            if b.r.get(s, 0) < v:
                b.r[s] = v
        for b in writes:
            b.w = tok
            b.r = {}

    def op(self, e, fn, reads=(), writes=(), signal=True):
        self._deps(e, reads, writes)
        tok = (self.own[e], self.cnt[e] + 1)
        if signal:
            self.cnt[e] += 1
        self.ops[e].append(("op", fn, self.own[e] if signal else None))
        if signal:
            self.sig[self.own[e]] = self.cnt[e]
        self._mark(tok, reads, writes)
        self.nops += 1
        if signal and self.cnt[e] >= 28000:
            self._new_own(e)

    def dma(self, q, fn, reads=(), writes=()):
        slot = self.dma_rr[q]
        self.dma_rr[q] = (slot + 1) % len(self.dma_sems[q])
        s = self.dma_sems[q][slot]
        prev = self.dma_val[s]
        self._deps(q, reads, writes, extra=((s, prev),) if prev else ())
        tok = (s, prev + 16)
        self.dma_val[s] = prev + 16
        self.ops[q].append(("dma", fn, s))
        self._mark(tok, reads, writes)
        self.nops += 1

    def barrier(self):
        toks = [(s, v) for s, v in self.sig.items() if v > 0]
        for s, v in self.dma_val.items():
            if v > 0:
                toks.append((s, v))
        for e in self.ENG:
            seen = self.seen[e]
            for s, v in toks:
                if s == self.own.get(e):
                    continue
                if seen.get(s, 0) >= v:
                    continue
                seen[s] = v
                self.ops[e].append(("wait", s, v))

    def _replay(self, e, eng):
        for item in self.ops[e]:
            if item[0] == "wait":
                eng.wait_ge(self.sems[item[1]], item[2])
            elif item[0] == "op":
                ins = item[1](eng)
                if item[2] is not None:
                    ins.then_inc(self.sems[item[2]], 1)
            else:
                item[1](eng).then_inc(self.sems[item[2]], 16)
        self.ops[e] = []

    def emit(self):
        with self.nc.Block() as block:
            @block.tensor
            def _(eng):
                self._replay("pe", eng)

            @block.scalar
            def _(eng):
                self._replay("act", eng)

            @block.vector
            def _(eng):
                self._replay("dve", eng)

            @block.gpsimd
            def _(eng):
                self._replay("pool", eng)

            @block.sync
            def _(eng):
                self._replay("sp", eng)


class Ring:
    def __init__(self, nc, st, name, n, shape, dt):
        self.t = [st.enter_context(nc.sbuf_tensor(_u(f"{name}{i}"), shape, dt)) for i in range(n)]
        self.b = [Buf(f"{name}{i}") for i in range(n)]
        self.i = 0
        self.n = n

    def next(self):
        i = self.i
        self.i = (i + 1) % self.n
        return self.t[i], self.b[i]


class G:
    pass


def build(S=4096, depth=DEPTH, stop_mixer=False, dbg=False):
    NT = S // T
    nc = bass.Bass("TRN2", target_bir_lowering=False)
    g = G()
    g.nc = nc
    g.S, g.NT = S, NT
    dt = nc.dram_tensor
    xT = dt("xT", [D, S], F32, kind="ExternalInput").ap()
    vecs = dt("vecs", [128, NV], F32, kind="ExternalInput").ap()
    cst = dt("cst", [128, 256], F32, kind="ExternalInput").ap()
    rope = dt("rope", [2, 128, S], F32, kind="ExternalInput").ap()
    mod_w = dt("mod_w", [8, D, 3 * D], F32, kind="ExternalInput").ap()
    conv_w_in = dt("conv_w_in", [2, D, 2 * D], F32, kind="ExternalInput").ap()
    conv_w_out = dt("conv_w_out", [2, D, D], F32, kind="ExternalInput").ap()
    attn_w_qkv = dt("attn_w_qkv", [2, D, 3072], F32, kind="ExternalInput").ap()
    attn_w_out = dt("attn_w_out", [2, D, D], F32, kind="ExternalInput").ap()
    mlp_w_in = dt("mlp_w_in", [4, D, DFF], F32, kind="ExternalInput").ap()
    mlp_w_out = dt("mlp_w_out", [4, DFF, D], F32, kind="ExternalInput").ap()
    yT = dt("yT", [D, S], F32, kind="ExternalOutput").ap()
    XS = dt("XS", [D, S], F32).ap()
    UPD = dt("UPD", [D, S + 2 * UP], BF16, kind="ExternalOutput" if dbg else "Internal").ap()
    VC = dt("VC", [D, S], F32, kind="ExternalOutput" if dbg else "Internal").ap()
    n_conv = (depth + 1) // 2
    n_attn = depth // 2
    WB = {}
    for j in range(n_conv):
        WB["cin", j] = dt(f"wb_cin{j}", [8, 128, KC * 512], BF16).ap()
        WB["cout", j] = dt(f"wb_cout{j}", [4, 128, KC * 512], BF16).ap()
    for j in range(n_attn):
        WB["qkv", j] = dt(f"wb_qkv{j}", [6, 128, KC * 512], BF16).ap()
        WB["aout", j] = dt(f"wb_aout{j}", [4, 128, KC * 512], BF16).ap()
    for i in range(depth):
        WB["min", i] = dt(f"wb_min{i}", [16, 128, KC * 512], BF16).ap()
        WB["mout", i] = dt(f"wb_mout{i}", [16, 128, 32 * 256], BF16).ap()

    def xtile(ap, j):
        return ap.rearrange("(kc p) s -> p kc s", p=128)[:, :, j * T:(j + 1) * T]

    with ExitStack() as st:
        P = Prog(nc, st)
        g.P = P
        vec = st.enter_context(nc.sbuf_tensor(_u("vec"), [128, NV], F32))
        cst32 = st.enter_context(nc.sbuf_tensor(_u("cst32"), [128, 256], F32))
        ident_bf = st.enter_context(nc.sbuf_tensor(_u("ident_bf"), [128, 128], BF16))
        ones_bf = st.enter_context(nc.sbuf_tensor(_u("ones_bf"), [128, 128], BF16))
        cact = st.enter_context(nc.sbuf_tensor(_u("cact"), [128, 16], F32))
        modv = st.enter_context(nc.sbuf_tensor(_u("modv"), [128, 8, 48], F32))
        gqk = st.enter_context(nc.sbuf_tensor(_u("gqk"), [128, 4], F32))
        Bconst = Buf("const")
        Bmodv = Buf("modv")
        ps = [st.enter_context(nc.psum_tensor(f"ps{i}", [128, 512], F32)) for i in range(8)]
        psb = [Buf(f"ps{i}") for i in range(8)]
        rotT = cst32[:, 128:256]

        P.dma("sp", lambda e: e.dma_start(out=vec[:], in_=vecs), writes=[Bconst])
        P.dma("sp", lambda e: e.dma_start(out=cst32[:], in_=cst), writes=[Bconst])
        P.op("dve", lambda e: e.tensor_copy(out=ident_bf[:], in_=cst32[:, 0:128]), reads=[Bconst], writes=[Bconst])
        P.op("dve", lambda e: e.memset(ones_bf[:], 1.0), writes=[Bconst])
        P.op("act", lambda e: e.activation(out=cact[:], in_=vec[:, V_C:V_C + 16], func=AF.Silu), reads=[Bconst], writes=[Bconst])
        P.op("dve", lambda e: e.tensor_scalar(out=gqk[:], in0=vec[:, V_QN:V_QN + 4], scalar1=float(128.0 ** 0.5), scalar2=None, op0=ALU.mult),
             reads=[Bconst], writes=[Bconst])

        ones32 = st.enter_context(nc.sbuf_tensor(_u("ones32"), [128, 1], F32))
        P.op("dve", lambda e: e.memset(ones32[:], 1.0), writes=[Bconst])
        MODB = 7

        def cast_k2048(w, wb, nblk):
            wv = w.rearrange("(kc p) n -> p kc n", p=128)
            return [(wb[nb].rearrange("p (kc n) -> p kc n", kc=KC), wv[:, :, nb * 512:(nb + 1) * 512]) for nb in range(nblk)]

        def cast_mout(w, wb):
            wv = w.rearrange("(kc p) n -> p kc n", p=128)
            out = []
            for nb2 in range(8):
                for kh in range(2):
                    out.append((wb[nb2 * 2 + kh].rearrange("p (kc n) -> p kc n", kc=32),
                                wv[:, kh * 32:(kh + 1) * 32, nb2 * 256:(nb2 + 1) * 256]))
            return out

        def layer_casts(i):
            j = i // 2
            if i % 2 == 0:
                return cast_k2048(conv_w_in[j], WB["cin", j], 8) + cast_k2048(conv_w_out[j], WB["cout", j], 4)
            return cast_k2048(attn_w_qkv[j], WB["qkv", j], 6) + cast_k2048(attn_w_out[j], WB["aout", j], 4)

        def mlp_casts(i):
            return cast_k2048(mlp_w_in[i], WB["min", i], 16) + cast_mout(mlp_w_out[i], WB["mout", i])

        def emit_cast(c):
            dst, srcv = c
            P.dma("pool", lambda e: e.dma_start(out=dst, in_=srcv))

        bg_cast = []
        cast_mark = {}
        for i in range(depth):
            if i > 0:
                bg_cast += layer_casts(i)
            cast_mark["layer", i] = len(bg_cast)
            bg_cast += mlp_casts(i)
            cast_mark["mlp", i] = len(bg_cast)
        cast_mark["layer", depth] = len(bg_cast)
        bg_cast_pos = [0]

        def bg_cast_until(idx):
            bg_cast_run(max(0, idx - bg_cast_pos[0]))

        def bg_cast_run(n):
            for _ in range(n):
                if bg_cast_pos[0] < len(bg_cast):
                    emit_cast(bg_cast[bg_cast_pos[0]])
                    bg_cast_pos[0] += 1

        def mod_task(ls, nb, stgr, accr, q, eng, maccA):
            tl, tb = stgr.next()
            acc, accb = maccA.next()
            wv = mod_w[ls].rearrange("(kc p) n -> p kc n", p=128)
            P.dma(q, lambda e: e.dma_start(out=tl[:], in_=wv[:, :, nb * 128:(nb + 1) * 128]), writes=[tb])
            P.op(eng, lambda e: e.tensor_scalar(out=acc[:], in0=tl[:, 0, :], scalar1=cact[:, 0:1], scalar2=None, op0=ALU.mult),
                 reads=[tb, Bconst], writes=[accb])
            for kc in range(1, KC):
                if eng == "dve":
                    P.op(eng, lambda e, kc=kc: e.scalar_tensor_tensor(out=acc[:], in0=tl[:, kc, :], scalar=cact[:, kc:kc + 1], in1=acc[:],
                                                                     op0=ALU.mult, op1=ALU.add),
                         reads=[tb, accb, Bconst], writes=[accb])
                else:
                    tm, tmb = accr.next()
                    P.op(eng, lambda e, kc=kc, tm=tm: e.tensor_scalar(out=tm[:], in0=tl[:, kc, :], scalar1=cact[:, kc:kc + 1], scalar2=None, op0=ALU.mult),
                         reads=[tb, Bconst], writes=[tmb])
                    P.op(eng, lambda e, tm=tm: e.tensor_tensor(out=acc[:], in0=acc[:], in1=tm[:], op=ALU.add),
                         reads=[tmb, accb], writes=[accb])
            def pe_part():
                col = ls * 48 + nb
                P.op("pe", lambda e, col=col: e.matmul(ps[MODB][:, col:col + 1], acc[:], ones32[:, 0:1], start=True, stop=True),
                     reads=[accb, Bconst], writes=[psb[MODB]])
                if nb == 47:
                    fin_part()

            def fin_part():
                P.op("dve", lambda e: e.tensor_tensor(out=modv[:, ls, :], in0=ps[MODB][:, ls * 48:(ls + 1) * 48],
                                                      in1=vec[:, V_MODB + ls * 48:V_MODB + (ls + 1) * 48], op=ALU.add),
                     reads=[psb[MODB], Bconst], writes=[Bmodv])
                P.op("dve", lambda e: e.tensor_scalar(out=modv[:, ls, 16:32], in0=modv[:, ls, 16:32], scalar1=1.0, scalar2=None, op0=ALU.add),
                     reads=[Bmodv], writes=[Bmodv])
                P.op("dve", lambda e: e.tensor_scalar(out=modv[:, ls, 32:48], in0=modv[:, ls, 32:48], scalar1=1.0, scalar2=float(1.0 / ALPHA),
                                                      op0=ALU.add, op1=ALU.mult),
                     reads=[Bmodv], writes=[Bmodv])

            mod_pend.append(pe_part)
            while len(mod_pend) > 3:
                mod_pend.pop(0)()

        mod_pend = []

        def mod_flush():
            while mod_pend:
                mod_pend.pop(0)()

        bg_mod = []
        bg_mod_pos = [0]

        def bg_mod_run(n):
            for _ in range(n):
                if bg_mod_pos[0] < len(bg_mod):
                    bg_mod[bg_mod_pos[0]]()
                    bg_mod_pos[0] += 1

        def mv(ls, part, n):
            return modv[:, ls, part * 16 + n:part * 16 + n + 1]

        def vcol(base, n):
            return vec[:, base + n:base + n + 1]

        def make_h(src, j, ls, h, hb, stg, w=2):
            sv = xtile(src, j)
            for kc2 in range(0, KC, w):
                tl, tb = stg.next()
                P.dma("sp", lambda e, tl=tl, kc2=kc2: e.dma_start(out=tl[:], in_=sv[:, kc2:kc2 + w, :]), writes=[tb])
                for q in range(w):
                    kc = kc2 + q
                    P.op("dve", lambda e, tl=tl, q=q, kc=kc: e.tensor_scalar(
                        out=h[:, kc, :], in0=tl[:, q, :], scalar1=mv(ls, 1, kc), scalar2=mv(ls, 0, kc),
                        op0=ALU.mult, op1=ALU.add), reads=[tb, Bmodv], writes=[hb[kc]])

        class Epi:
            def __init__(self, X, Xb, ls, tmpr, statr, s1, s2):
                self.X, self.Xb, self.ls = X, Xb, ls
                self.tmpr, self.statr = tmpr, statr
                self.s1, self.s2 = s1, s2

            def chunk(self, n, pt, ptb):
                X, Xb, ls = self.X, self.Xb, self.ls
                P.op("dve", lambda e: e.scalar_tensor_tensor(out=X[:, n, :], in0=pt[:], scalar=mv(ls, 2, n), in1=X[:, n, :],
                                                             op0=ALU.mult, op1=ALU.add),
                     reads=[ptb, Xb[n], Bmodv], writes=[Xb[n]])
                zs, zsb = self.tmpr.next()
                zb, zbb = self.tmpr.next()
                P.op("act", lambda e: e.activation(out=zs[:], in_=X[:, n, :], func=AF.Square), reads=[Xb[n]], writes=[zsb])
                P.op("dve", lambda e: e.tensor_copy(out=zb[:], in_=X[:, n, :]), reads=[Xb[n]], writes=[zbb])
                P.op("pe", lambda e: e.matmul(ps[self.s1][:], ones_bf[:], zb[:], start=(n == 0), stop=(n == KC - 1)),
                     reads=[zbb, Bconst], writes=[psb[self.s1]], signal=True)
                P.op("pe", lambda e: e.matmul(ps[self.s2][:], ones_bf[:], zs[:], start=(n == 0), stop=(n == KC - 1)),
                     reads=[zsb, Bconst], writes=[psb[self.s2]], signal=True)

            def stats(self, eps):
                mean, meanb = self.statr.next()
                rstd, rstdb = self.statr.next()
                P.op("dve", lambda e: e.tensor_scalar(out=mean[:], in0=ps[self.s1][:], scalar1=float(1.0 / D), scalar2=None, op0=ALU.mult),
                     reads=[psb[self.s1]], writes=[meanb])
                P.op("dve", lambda e: e.tensor_tensor(out=rstd[:], in0=mean[:], in1=mean[:], op=ALU.mult), reads=[meanb], writes=[rstdb])
                P.op("dve", lambda e: e.scalar_tensor_tensor(out=rstd[:], in0=ps[self.s2][:], scalar=float(1.0 / D), in1=rstd[:],
                                                             op0=ALU.mult, op1=ALU.subtract),
                     reads=[psb[self.s2], rstdb], writes=[rstdb])
                P.op("act", lambda e: e.activation(out=rstd[:], in_=rstd[:], func=AF.Sqrt, bias=float(eps), scale=1.0),
                     reads=[rstdb], writes=[rstdb])
                P.op("dve", lambda e: e.reciprocal(out=rstd[:], in_=rstd[:]), reads=[rstdb], writes=[rstdb])
                return mean, meanb, rstd, rstdb

            def finish(self, gbase, bbase):
                X, Xb = self.X, self.Xb
                mean, meanb, rstd, rstdb = self.stats(LN_EPS / (ALPHA * ALPHA))
                for n in range(KC):
                    P.op("dve", lambda e, n=n: e.tensor_tensor(out=X[:, n, :], in0=X[:, n, :], in1=mean[:], op=ALU.subtract),
                         reads=[Xb[n], meanb], writes=[Xb[n]])
                    P.op("dve", lambda e, n=n: e.tensor_tensor(out=X[:, n, :], in0=X[:, n, :], in1=rstd[:], op=ALU.mult),
                         reads=[Xb[n], rstdb], writes=[Xb[n]])
                    P.op("act", lambda e, n=n: e.activation(out=X[:, n, :], in_=X[:, n, :], func=AF.Identity,
                                                            scale=vcol(gbase, n), bias=vcol(bbase, n)),
                         reads=[Xb[n], Bconst], writes=[Xb[n]])

        def stream(steps, ring, depth_pf=None):
            n = len(steps)
            pf = ring.n - 1 if depth_pf is None else depth_pf
            loaded = {}

            def load(i):
                ap = steps[i][0]
                if ap is None:
                    loaded[i] = (None, None)
                    return
                tl, tb = ring.next()
                P.dma("sp", lambda e: e.dma_start(out=tl[:].rearrange("p a b -> p (a b)"), in_=ap), writes=[tb])
                loaded[i] = (tl, tb)

            for i in range(min(pf, n)):
                load(i)
            for i in range(n):
                if i + pf < n:
                    load(i + pf)
                tl, tb = loaded.pop(i)
                steps[i][1](tl, tb)

        def store_x(X, Xb, dst, j):
            P.dma("sp", lambda e: e.dma_start(out=xtile(dst, j), in_=X[:]), reads=list(Xb))

        def load_x(X, Xb, src, j):
            P.dma("sp", lambda e: e.dma_start(out=X[:], in_=xtile(src, j)), writes=list(Xb))

        def mlp_pass(i, src, dst):
            ls = 2 * i + 1
            with ExitStack() as ph:
                X = ph.enter_context(nc.sbuf_tensor(_u("mX"), [128, KC, T], F32))
                Xb = [Buf(f"mX{n}") for n in range(KC)]
                h = ph.enter_context(nc.sbuf_tensor(_u("mh"), [128, KC, T], BF16))
                hb = [Buf(f"mh{n}") for n in range(KC)]
                uT = ph.enter_context(nc.sbuf_tensor(_u("muT"), [128, 64, T], BF16))
                ub = [Buf(f"mu{n}") for n in range(64)]
                stg = Ring(nc, ph, "mstg", 2, [128, 1, T], F32)
                wr = Ring(nc, ph, "mw", 3, [128, KC, 512], BF16)
                tmpr = Ring(nc, ph, "mtmp", 4, [128, T], BF16)
                rr = Ring(nc, ph, "mrl", 3, [128, T], BF16)
                statr = Ring(nc, ph, "mstat", 2, [128, T], F32)
                steps = []
                pin = [2, 3, 6]
                pin_i = [0]
                make_h(src, 0, ls, h, hb, stg, 1)
                for j in range(NT):
                    epi = Epi(X, Xb, ls, tmpr, statr, 0, 1)
                    for nb in range(16):
                        def c_in(tl, tb, nb=nb):
                            bg_cast_run(1)
                            if i == 0:
                                bg_mod_run(1)
                            for hc in range(4):
                                pi = pin[pin_i[0] % 3]
                                pin_i[0] += 1
                                for kc in range(KC):
                                    P.op("pe", lambda e, kc=kc, hc=hc, pi=pi: e.matmul(
                                        ps[pi][:], tl[:, kc, hc * 128:(hc + 1) * 128], h[:, kc, :], start=(kc == 0), stop=(kc == KC - 1)),
                                        reads=[tb, hb[kc]], writes=[psb[pi]], signal=(kc == KC - 1))
                                r, rb = rr.next()
                                P.op("act", lambda e, pi=pi, r=r: e.activation(out=r[:], in_=ps[pi][:], func=AF.Relu),
                                     reads=[psb[pi]], writes=[rb])
                                u = nb * 4 + hc
                                eng = "dve"
                                P.op(eng, lambda e, r=r, u=u: e.tensor_tensor(out=uT[:, u, :], in0=r[:], in1=r[:], op=ALU.mult),
                                     reads=[rb], writes=[ub[u]])
                        steps.append((WB["min", i][nb], c_in))
                    for nb2 in range(8):
                        for kh in range(2):
                            def c_out(tl, tb, nb2=nb2, kh=kh, j=j, epi=epi):
                                if i == 0:
                                    bg_mod_run(1)
                                if nb2 == 0 and kh == 0:
                                    load_x(X, Xb, src, j)
                                    if j + 1 < NT:
                                        make_h(src, j + 1, ls, h, hb, stg, 1)
                                wv = tl[:].rearrange("p a b -> p (a b)").rearrange("p (kc n) -> p kc n", kc=32)
                                for half in range(2):
                                    pi = 4 + half
                                    for kk in range(32):
                                        kc = kh * 32 + kk
                                        P.op("pe", lambda e, kk=kk, kc=kc, half=half, pi=pi: e.matmul(
                                            ps[pi][:], wv[:, kk, half * 128:(half + 1) * 128], uT[:, kc, :],
                                            start=(kc == 0), stop=(kc == 63)),
                                            reads=[tb, ub[kc]], writes=[psb[pi]], signal=(kk == 31))
                                if kh == 1:
                                    for half in range(2):
                                        epi.chunk(nb2 * 2 + half, ps[4 + half], psb[4 + half])
                                if nb2 == 7 and kh == 1:
                                    epi.finish(V_LNG + ls * 16, V_LNB + ls * 16)
                                    store_x(X, Xb, dst, j)
                            steps.append((WB["mout", i][nb2 * 2 + kh], c_out))
                stream(steps, wr)
                if i == 0:
                    bg_mod_run(10 ** 6)
                    mod_flush()
                bg_cast_until(cast_mark["layer", i + 1])
                P.barrier()
                P.emit()

        def conv_layer(i, src, dst):
            j_ = i // 2
            ls = 2 * i
            with ExitStack() as ph:
                h = ph.enter_context(nc.sbuf_tensor(_u("ch"), [128, KC, T], BF16))
                hb = [Buf(f"ch{n}") for n in range(KC)]
                stg = Ring(nc, ph, "cstg", 2, [128, 2, T], F32)
                wr = Ring(nc, ph, "cw", 4, [128, KC, 512], BF16)
                sgr = Ring(nc, ph, "csg", 3, [128, T], F32)
                ur = Ring(nc, ph, "cu", 2, [128, KC, T], BF16)
                pin_i = [0]
                steps = []
                make_h(src, 0, ls, h, hb, stg)
                for j in range(NT):
                    u, ubuf = ur.next()
                    for nb in range(4):
                        hold = {}

                        def c_a(tl, tb, hold=hold):
                            hold["a"] = (tl, tb)
                            bg_cast_run(1)
                            bg_mod_run(1)

                        def c_g(tl, tb, nb=nb, hold=hold, u=u, ubuf=ubuf, j=j):
                            ta, tab = hold["a"]
                            for hc in range(4):
                                n = nb * 4 + hc
                                pa = (pin_i[0] % 3) * 2
                                pin_i[0] += 1
                                pg = pa + 1
                                for (wt, wtb, pi) in ((ta, tab, pa), (tl, tb, pg)):
                                    for kc in range(KC):
                                        P.op("pe", lambda e, kc=kc, hc=hc, pi=pi, wt=wt: e.matmul(
                                            ps[pi][:], wt[:, kc, hc * 128:(hc + 1) * 128], h[:, kc, :], start=(kc == 0), stop=(kc == KC - 1)),
                                            reads=[wtb, hb[kc]], writes=[psb[pi]], signal=(kc == KC - 1))
                                sg, sgb = sgr.next()
                                P.op("act", lambda e, pg=pg, sg=sg, n=n: e.activation(out=sg[:], in_=ps[pg][:], func=AF.Sigmoid,
                                                                                   bias=vcol(V_CBIN + j_ * 32 + 16, n), scale=1.0),
                                     reads=[psb[pg], Bconst], writes=[sgb])
                                P.op("dve", lambda e, pa=pa, sg=sg, n=n, u=u: e.scalar_tensor_tensor(
                                    out=u[:, n, :], in0=ps[pa][:], scalar=vcol(V_CBIN + j_ * 32, n), in1=sg[:], op0=ALU.add, op1=ALU.mult),
                                    reads=[psb[pa], sgb, Bconst], writes=[ubuf])
                            if nb == 3:
                                P.dma("sp", lambda e, u=u, j=j: e.dma_start(
                                    out=UPD.rearrange("(kc p) s -> p kc s", p=128)[:, :, UP + j * T:UP + (j + 1) * T], in_=u[:]), reads=[ubuf])
                                if j + 1 < NT:
                                    make_h(src, j + 1, ls, h, hb, stg)
                        steps.append((WB["cin", j_][nb], c_a))
                        steps.append((WB["cin", j_][4 + nb], c_g))
                stream(steps, wr, depth_pf=2)
                bg_mod_run(max(0, 48 - bg_mod_pos[0]))
                mod_flush()
                P.barrier()
                P.emit()
            with ExitStack() as ph:
                urow = Ring(nc, ph, "c2u", 2, [128, S + 2 * UP], BF16)
                vrow = Ring(nc, ph, "c2v", 2, [128, S], F32)
                dgr = Ring(nc, ph, "c2d", 2, [128, CW, 128], BF16)
                pin_i = [0]
                for c in range(KC):
                    ur_, urb = urow.next()
                    vr_, vrb = vrow.next()
                    dg, dgb = dgr.next()
                    P.dma("sp", lambda e, ur_=ur_, c=c: e.dma_start(out=ur_[:], in_=UPD[c * 128:(c + 1) * 128, :]), writes=[urb])
                    for tp in range(CW):
                        P.op("act", lambda e, dg=dg, tp=tp, c=c: e.activation(
                            out=dg[:, tp, :], in_=ident_bf[:], func=AF.Identity, scale=vcol(V_CDW + j_ * CW * 16 + tp * 16, c)),
                            reads=[Bconst], writes=[dgb])
                    bg_cast_run(3)
                    for j in range(NT):
                        pi = pin_i[0] % 7
                        pin_i[0] += 1
                        for tp in range(CW):
                            o = UP + j * T + tp - (CW // 2)
                            P.op("pe", lambda e, dg=dg, tp=tp, o=o, pi=pi, ur_=ur_: e.matmul(
                                ps[pi][:], dg[:, tp, :], ur_[:, o:o + T], start=(tp == 0), stop=(tp == CW - 1)),
                                reads=[dgb, urb], writes=[psb[pi]], signal=(tp == CW - 1))
                        eng = "act" if j % 2 == 0 else "dve"
                        if eng == "act":
                            P.op("act", lambda e, pi=pi, vr_=vr_, j=j, c=c: e.activation(
                                out=vr_[:, j * T:(j + 1) * T], in_=ps[pi][:], func=AF.Identity, bias=vcol(V_CDWB + j_ * 16, c), scale=1.0),
                                reads=[psb[pi], Bconst], writes=[vrb])
                        else:
                            P.op("dve", lambda e, pi=pi, vr_=vr_, j=j, c=c: e.tensor_scalar(
                                out=vr_[:, j * T:(j + 1) * T], in0=ps[pi][:], scalar1=vcol(V_CDWB + j_ * 16, c), scalar2=None, op0=ALU.add),
                                reads=[psb[pi], Bconst], writes=[vrb])
                    P.dma("sp", lambda e, vr_=vr_, c=c: e.dma_start(out=VC[c * 128:(c + 1) * 128, :], in_=vr_[:]), reads=[vrb])
                mod_flush()
                P.barrier()
                P.emit()
            with ExitStack() as ph:
                X = ph.enter_context(nc.sbuf_tensor(_u("cX"), [128, KC, T], F32))
                Xb = [Buf(f"cX{n}") for n in range(KC)]
                sTr = Ring(nc, ph, "csT", 2, [128, KC, T], BF16)
                vstg = Ring(nc, ph, "cvst", 3, [128, 2, T], F32)
                wr = Ring(nc, ph, "c3w", 3, [128, KC, 512], BF16)
                tmpr = Ring(nc, ph, "ctmp", 4, [128, T], BF16)
                statr = Ring(nc, ph, "cstat", 4, [128, T], F32)
                steps = []
                pin_i = [0]
                vt = xtile(VC, 0)
                Bdummy = Buf("vdummy")

                def prep1(j):
                    sv = xtile(VC, j)
                    for k2 in range(0, KC, 2):
                        tl, tb = vstg.next()
                        P.dma("sp", lambda e, tl=tl, k2=k2: e.dma_start(out=tl[:], in_=sv[:, k2:k2 + 2, :]), writes=[tb])
                        for q in range(2):
                            n = k2 + q
                            zs, zsb = tmpr.next()
                            zb, zbb = tmpr.next()
                            P.op("act", lambda e, tl=tl, q=q, zs=zs: e.activation(out=zs[:], in_=tl[:, q, :], func=AF.Square), reads=[tb], writes=[zsb])
                            P.op("dve", lambda e, tl=tl, q=q, zb=zb: e.tensor_copy(out=zb[:], in_=tl[:, q, :]), reads=[tb], writes=[zbb])
                            P.op("pe", lambda e, n=n, zb=zb: e.matmul(ps[2][:], ones_bf[:], zb[:], start=(n == 0), stop=(n == KC - 1)),
                                 reads=[zbb, Bconst], writes=[psb[2]])
                            P.op("pe", lambda e, n=n, zs=zs: e.matmul(ps[3][:], ones_bf[:], zs[:], start=(n == 0), stop=(n == KC - 1)),
                                 reads=[zsb, Bconst], writes=[psb[3]])

                def prep2(j, sT, sb):
                    vepi = Epi(None, None, ls, tmpr, statr, 2, 3)
                    mean, meanb, rstd, rstdb = vepi.stats(LN_EPS)
                    sv = xtile(VC, j)
                    for k2 in range(0, KC, 2):
                        tl, tb = vstg.next()
                        P.dma("sp", lambda e, tl=tl, k2=k2: e.dma_start(out=tl[:], in_=sv[:, k2:k2 + 2, :]), writes=[tb])
                        for q in range(2):
                            n = k2 + q
                            P.op("dve", lambda e, tl=tl, q=q: e.tensor_tensor(out=tl[:, q, :], in0=tl[:, q, :], in1=mean[:], op=ALU.subtract),
                                 reads=[tb, meanb], writes=[tb])
                            P.op("dve", lambda e, tl=tl, q=q: e.tensor_tensor(out=tl[:, q, :], in0=tl[:, q, :], in1=rstd[:], op=ALU.mult),
                                 reads=[tb, rstdb], writes=[tb])
                            P.op("act", lambda e, tl=tl, q=q, n=n: e.activation(out=sT[:, n, :], in_=tl[:, q, :], func=AF.Silu,
                                                                               scale=vcol(V_CLNG + j_ * 16, n), bias=vcol(V_CLNB + j_ * 16, n)),
                                 reads=[tb, Bconst], writes=[sb[n]])

                sTb = {}
                for j in range(NT):
                    sTb[j] = sTr.next() + ([Buf(f"cs{j}_{n}") for n in range(KC)],)
                slot_bufs = {}
                for j in range(NT):
                    slot_bufs.setdefault(j % 2, sTb[j][2])
                    sTb[j] = (sTb[j][0], sTb[j][1], slot_bufs[j % 2])
                prep1(0)
                prep2(0, sTb[0][0], sTb[0][2])
                for j in range(NT):
                    epi = Epi(X, Xb, ls, tmpr, statr, 0, 1)
                    sT, _, sb = sTb[j]
                    for nb in range(4):
                        def c_o(tl, tb, nb=nb, j=j, epi=epi, sT=sT, sb=sb):
                            bg_cast_run(1)
                            if nb == 0:
                                load_x(X, Xb, src, j)
                            if nb == 1 and j + 1 < NT:
                                prep1(j + 1)
                            if nb == 2 and j + 1 < NT:
                                prep2(j + 1, sTb[j + 1][0], sTb[j + 1][2])
                            for hc in range(4):
                                n = nb * 4 + hc
                                pi = 4 + (pin_i[0] % 3)
                                pin_i[0] += 1
                                for kc in range(KC):
                                    P.op("pe", lambda e, kc=kc, hc=hc, pi=pi: e.matmul(
                                        ps[pi][:], tl[:, kc, hc * 128:(hc + 1) * 128], sT[:, kc, :], start=(kc == 0), stop=(kc == KC - 1)),
                                        reads=[tb, sb[kc]], writes=[psb[pi]], signal=(kc == KC - 1))
                                epi.chunk(n, ps[pi], psb[pi])
                            if nb == 3:
                                epi.finish(V_LNG + ls * 16, V_LNB + ls * 16)
                                store_x(X, Xb, dst, j)
                        steps.append((WB["cout", j_][nb], c_o))
                stream(steps, wr)
                mod_flush()
                bg_cast_until(cast_mark["mlp", i])
                P.barrier()
                P.emit()

        def attn_layer(i, src, dst):
            j_ = i // 2
            ls = 2 * i
            NKC = S // 128
            with ExitStack() as al:
                KT = al.enter_context(nc.sbuf_tensor(_u("KT"), [128, 4, S], BF16))
                Vs = al.enter_context(nc.sbuf_tensor(_u("Vs"), [128, NKC, 512], BF16))
                KTb = [[Buf(f"KT{kv}_{j}") for j in range(NT)] for kv in range(4)]
                Vsb = [Buf(f"Vs{c}") for c in range(NKC)]
                rope_rings = [None, None]
                gen_i = [0]

                gen_banks = [[4, 5, 6, 7]]

                def gbank():
                    gb = gen_banks[0]
                    pi = gb[gen_i[0] % len(gb)]
                    gen_i[0] += 1
                    return pi

                def load_rope(j):
                    ct, cb = rope_rings[0].next()
                    sn, snb = rope_rings[1].next()
                    P.dma("sp", lambda e: e.dma_start(out=ct[:], in_=rope[0][:, j * T:(j + 1) * T]), writes=[cb])
                    P.dma("sp", lambda e: e.dma_start(out=sn[:], in_=rope[1][:, j * T:(j + 1) * T]), writes=[snb])
                    return ct, cb, sn, snb

                def norm_rope_stages(proj_fn, gcol, rp, out_ap, out_bufs, sqr, f32r):
                    ct, cb, sn, snb = rp
                    stt = {}

                    def A():
                        pi = gbank()
                        proj_fn(pi)
                        sq, sqb = sqr.next()
                        P.op("act", lambda e: e.activation(out=sq[:], in_=ps[pi][:], func=AF.Square), reads=[psb[pi]], writes=[sqb])
                        stt["pi"], stt["sq"], stt["sqb"] = pi, sq, sqb

                    def B():
                        pi, sq, sqb = stt["pi"], stt["sq"], stt["sqb"]
                        p2 = gbank()
                        P.op("pe", lambda e: e.matmul(ps[p2][:], ones_bf[:], sq[:], start=True, stop=True),
                             reads=[sqb, Bconst], writes=[psb[p2]])
                        rs, rsb = f32r.next()
                        P.op("act", lambda e: e.activation(out=rs[:], in_=ps[p2][:], func=AF.Ln, bias=float(128.0 * RMS_EPS), scale=1.0),
                             reads=[psb[p2]], writes=[rsb])
                        P.op("act", lambda e: e.activation(out=rs[:], in_=rs[:], func=AF.Exp, scale=-0.5), reads=[rsb], writes=[rsb])
                        qn, qnb = f32r.next()
                        P.op("dve", lambda e: e.scalar_tensor_tensor(out=qn[:], in0=ps[pi][:], scalar=gcol, in1=rs[:], op0=ALU.mult, op1=ALU.mult),
                             reads=[psb[pi], rsb, Bconst], writes=[qnb])
                        stt["qn"], stt["qnb"], stt["rs"], stt["rsb"] = qn, qnb, rs, rsb

                    def C():
                        qn, qnb, rs, rsb = stt["qn"], stt["qnb"], stt["rs"], stt["rsb"]
                        p3 = gbank()
                        P.op("pe", lambda e: e.matmul(ps[p3][:], rotT, qn[:], start=True, stop=True),
                             reads=[qnb, Bconst], writes=[psb[p3]])
                        P.op("dve", lambda e: e.tensor_tensor(out=rs[:], in0=ps[p3][:], in1=sn[:], op=ALU.mult),
                             reads=[psb[p3], snb], writes=[rsb])
                        P.op("dve", lambda e: e.tensor_tensor(out=qn[:], in0=qn[:], in1=ct[:], op=ALU.mult),
                             reads=[qnb, cb], writes=[qnb])
                        P.op("dve", lambda e: e.tensor_tensor(out=out_ap, in0=qn[:], in1=rs[:], op=ALU.add),
                             reads=[qnb, rsb], writes=out_bufs)

                    return [A, B, C]

                with ExitStack() as ph:
                    h = ph.enter_context(nc.sbuf_tensor(_u("ah"), [128, KC, T], BF16))
                    hb = [Buf(f"ah{n}") for n in range(KC)]
                    stg = Ring(nc, ph, "astg", 2, [128, 2, T], F32)
                    rope_rings[0] = Ring(nc, ph, "cosr1", 2, [128, T], F32)
                    rope_rings[1] = Ring(nc, ph, "sinr1", 2, [128, T], F32)
                    sqr1 = Ring(nc, ph, "sqr1", 4, [128, T], BF16)
                    f32r1 = Ring(nc, ph, "f32r1", 8, [128, T], F32)
                    wk = ph.enter_context(nc.sbuf_tensor(_u("awk"), [128, KC, 512], BF16))
                    wv_ = ph.enter_context(nc.sbuf_tensor(_u("awv"), [128, KC, 512], BF16))
                    Bwk, Bwv = Buf("wk"), Buf("wv")
                    P.dma("sp", lambda e: e.dma_start(out=wk[:].rearrange("p a b -> p (a b)"), in_=WB["qkv", j_][4]), writes=[Bwk])
                    P.dma("sp", lambda e: e.dma_start(out=wv_[:].rearrange("p a b -> p (a b)"), in_=WB["qkv", j_][5]), writes=[Bwv])
                    def vproj(j, tc4):
                        pi = 6 + (tc4 % 2)
                        for kc in range(KC):
                            P.op("pe", lambda e, kc=kc, tc4=tc4, pi=pi: e.matmul(ps[pi][:], h[:, kc, tc4 * 128:(tc4 + 1) * 128], wv_[:, kc, :],
                                                                                  start=(kc == 0), stop=(kc == KC - 1)),
                                 reads=[Bwv, hb[kc]], writes=[psb[pi]], signal=(kc == KC - 1))
                        c = j * 4 + tc4
                        if tc4 % 2 == 0:
                            P.op("act", lambda e, c=c, pi=pi: e.activation(out=Vs[:, c, :], in_=ps[pi][:], func=AF.Copy), reads=[psb[pi]], writes=[Vsb[c]])
                        else:
                            P.op("dve", lambda e, c=c, pi=pi: e.tensor_copy(out=Vs[:, c, :], in_=ps[pi][:]), reads=[psb[pi]], writes=[Vsb[c]])

                    for j in range(NT):
                        make_h(src, j, ls, h, hb, stg)
                        rp = load_rope(j)
                        stgs = []
                        for kv in range(4):
                            def proj(pi, kv=kv):
                                for kc in range(KC):
                                    P.op("pe", lambda e, kc=kc: e.matmul(ps[pi][:], wk[:, kc, kv * 128:(kv + 1) * 128], h[:, kc, :],
                                                                          start=(kc == 0), stop=(kc == KC - 1)),
                                         reads=[Bwk, hb[kc]], writes=[psb[pi]], signal=(kc == KC - 1))
                            stgs.append(norm_rope_stages(proj, gqk[:, 2 + j_:3 + j_], rp, KT[:, kv, j * T:(j + 1) * T], [KTb[kv][j]], sqr1, f32r1))
                        gen_banks[0] = [0, 1, 2, 3]
                        gen_i[0] = 0
                        for kv in range(4):
                            stgs[kv][0]()
                        vproj(j, 0)
                        vproj(j, 1)
                        gen_banks[0] = [4, 5]
                        gen_i[0] = 0
                        for kv in range(4):
                            stgs[kv][1]()
                        vproj(j, 2)
                        vproj(j, 3)
                        for kv in range(4):
                            stgs[kv][2]()
                    P.barrier()
                    P.emit()

                with ExitStack() as ph:
                    X = ph.enter_context(nc.sbuf_tensor(_u("aX"), [128, KC, T], F32))
                    Xb = [Buf(f"aX{n}") for n in range(KC)]
                    h = ph.enter_context(nc.sbuf_tensor(_u("a2h"), [128, KC, T], BF16))
                    hb = [Buf(f"a2h{n}") for n in range(KC)]
                    stg = Ring(nc, ph, "a2stg", 2, [128, 1, T], F32)
                    qr = Ring(nc, ph, "a2q", 2, [128, T], BF16)
                    OT = ph.enter_context(nc.sbuf_tensor(_u("a2O"), [128, KC, T], BF16))
                    Ob = [Buf(f"a2O{n}") for n in range(KC)]
                    pr = Ring(nc, ph, "a2p", 6, [128, T], BF16)
                    wqr = Ring(nc, ph, "a2wq", 2, [128, KC, 512], BF16)
                    tmpr = Ring(nc, ph, "a2tmp", 4, [128, T], BF16)
                    statr = Ring(nc, ph, "a2stat", 2, [128, T], F32)
                    rcr = Ring(nc, ph, "a2rc", 2, [128, T], F32)
                    rope_rings[0] = Ring(nc, ph, "cosr2", 1, [128, T], F32)
                    rope_rings[1] = Ring(nc, ph, "sinr2", 1, [128, T], F32)
                    sqr2 = Ring(nc, ph, "sqr2", 2, [128, T], BF16)
                    f32r2 = Ring(nc, ph, "f32r2", 2, [128, T], F32)
                    scale = float(128.0 ** -0.5)

                    def make_h1(j):
                        sv = xtile(src, j)
                        for kc in range(KC):
                            tl, tb = stg.next()
                            P.dma("sp", lambda e, tl=tl, kc=kc: e.dma_start(out=tl[:], in_=sv[:, kc:kc + 1, :]), writes=[tb])
                            P.op("dve", lambda e, tl=tl, kc=kc: e.tensor_scalar(
                                out=h[:, kc, :], in0=tl[:, 0, :], scalar1=mv(ls, 1, kc), scalar2=mv(ls, 0, kc),
                                op0=ALU.mult, op1=ALU.add), reads=[tb, Bmodv], writes=[hb[kc]])

                    units = [(j, hd) for j in range(NT) for hd in range(16)]
                    wq_cur = {}
                    rope_cur = {}

                    def prep(u):
                        j, hd = units[u]
                        if hd == 0:
                            rope_cur[j] = load_rope(j)
                        if hd % 4 == 0:
                            tl, tb = wqr.next()
                            blk = WB["qkv", j_][hd // 4]
                            P.dma("sp", lambda e: e.dma_start(out=tl[:].rearrange("p a b -> p (a b)"), in_=blk), writes=[tb])
                            wq_cur[(j, hd // 4)] = (tl, tb)
                        tl, tb = wq_cur[(j, hd // 4)]
                        q, qb = qr.next()

                        def proj(pi):
                            hh = hd % 4
                            for kc in range(KC):
                                P.op("pe", lambda e, kc=kc: e.matmul(ps[pi][:], tl[:, kc, hh * 128:(hh + 1) * 128], h[:, kc, :],
                                                                      start=(kc == 0), stop=(kc == KC - 1)),
                                     reads=[tb, hb[kc]], writes=[psb[pi]], signal=(kc == KC - 1))
                        return norm_rope_stages(proj, gqk[:, j_:j_ + 1], rope_cur[j], q[:], [qb], sqr2, f32r2), (q, qb)

                    gen_banks[0] = [6, 7]
                    gen_i[0] = 0

                    def run_tile(j, qs):
                        us = [u for u in range(len(units)) if units[u][0] == j]
                        its = [(u, kc) for u in us for kc in range(NKC)]
                        pend = {}
                        prev_p = [None]
                        hk = [0, max(1, NKC // 4), max(2, NKC // 2)]

                        def qk(i):
                            u, kc = its[i]
                            q, qb = qs[u]
                            kv = units[u][1] // 4
                            bk = i % 3
                            P.op("pe", lambda e: e.matmul(ps[bk][:], KT[:, kv, kc * 128:(kc + 1) * 128], q[:], start=True, stop=True),
                                 reads=[KTb[kv][kc // 4], qb], writes=[psb[bk]])
                            p, pb = pr.next()
                            P.op("act", lambda e: e.activation(out=p[:], in_=ps[bk][:], func=AF.Exp, scale=scale),
                                 reads=[psb[bk]], writes=[pb])
                            pend[i] = (p, pb)

                        for i0 in range(min(3, len(its))):
                            qk(i0)
                        for i in range(len(its)):
                            u, kc = its[i]
                            hd = units[u][1]
                            kv = hd // 4
                            OACC, SUMS = 3 + (hd % 2), 5
                            if hd < 15:
                                if kc == hk[0]:
                                    st_, qq = prep(u + 1)
                                    qs[u + 1] = qq
                                    qs[("st", u + 1)] = st_
                                    st_[0]()
                                elif kc == hk[1]:
                                    qs[("st", u + 1)][1]()
                                elif kc == hk[2]:
                                    qs[("st", u + 1)][2]()
                            elif kc == hk[0] and j + 1 < NT:
                                make_h1(j + 1)
                            p, pb = pend.pop(i)
                            P.op("pe", lambda e, kc=kc, kv=kv, p=p, OACC=OACC: e.matmul(ps[OACC][:], Vs[:, kc, kv * 128:(kv + 1) * 128], p[:],
                                                                              start=(kc == 0), stop=(kc == NKC - 1)),
                                 reads=[Vsb[kc], pb], writes=[psb[OACC]], signal=False)
                            P.op("pe", lambda e, kc=kc, p=p, SUMS=SUMS: e.matmul(ps[SUMS][:], ones_bf[:], p[:], start=(kc == 0), stop=(kc == NKC - 1)),
                                 reads=[pb, Bconst], writes=[psb[SUMS]], signal=True)
                            if i + 3 < len(its):
                                qk(i + 3)
                            if kc == NKC - 1:
                                rc, rcb = rcr.next()
                                P.op("dve", lambda e, rc=rc, SUMS=SUMS: e.tensor_copy(out=rc[:], in_=ps[SUMS][:]), reads=[psb[SUMS]], writes=[rcb])
                                P.op("dve", lambda e, rc=rc: e.reciprocal(out=rc[:], in_=rc[:]), reads=[rcb], writes=[rcb])
                                P.op("dve", lambda e, rc=rc, hd=hd, OACC=OACC: e.tensor_tensor(out=OT[:, hd, :], in0=ps[OACC][:], in1=rc[:], op=ALU.mult),
                                     reads=[psb[OACC], rcb], writes=[Ob[hd]])

                    def out_proj(j):
                        epi = Epi(X, Xb, ls, tmpr, statr, 0, 1)
                        load_x(X, Xb, src, j)
                        for nb in range(4):
                            tl, tb = wqr.next()
                            blk = WB["aout", j_][nb]
                            P.dma("sp", lambda e, tl=tl, blk=blk: e.dma_start(out=tl[:].rearrange("p a b -> p (a b)"), in_=blk), writes=[tb])
                            for hc in range(4):
                                n = nb * 4 + hc
                                pi = gbank()
                                for kc in range(KC):
                                    P.op("pe", lambda e, kc=kc, hc=hc, pi=pi, tl=tl: e.matmul(
                                        ps[pi][:], tl[:, kc, hc * 128:(hc + 1) * 128], OT[:, kc, :], start=(kc == 0), stop=(kc == KC - 1)),
                                        reads=[tb, Ob[kc]], writes=[psb[pi]], signal=(kc == KC - 1))
                                epi.chunk(n, ps[pi], psb[pi])
                        epi.finish(V_LNG + ls * 16, V_LNB + ls * 16)
                        store_x(X, Xb, dst, j)

                    make_h1(0)
                    for j in range(NT):
                        u0 = j * 16
                        st_, qq = prep(u0)
                        for s_ in st_:
                            s_()
                        qs = {u0: qq}
                        run_tile(j, qs)
                        out_proj(j)
                    bg_cast_until(cast_mark["mlp", i])
                    P.barrier()
                    P.emit()

        with ExitStack() as msc:
            mstg = Ring(nc, msc, "modstg", 2, [128, KC, 128], F32)
            macc = None
            maccA = Ring(nc, msc, "modaccA", 6, [128, 128], F32)
            zt = msc.enter_context(nc.sbuf_tensor(_u("zt"), [128, UP], BF16))
            Bz = Buf("zt")
            P.op("dve", lambda e: e.memset(zt[:], 0.0), writes=[Bz])
            for kc in range(KC):
                upv = UPD[kc * 128:(kc + 1) * 128, :]
                P.dma("sp", lambda e, upv=upv: e.dma_start(out=upv[:, 0:UP], in_=zt[:]), reads=[Bz])
                P.dma("sp", lambda e, upv=upv: e.dma_start(out=upv[:, UP + S:UP + S + UP], in_=zt[:]), reads=[Bz])
            for c in layer_casts(0):
                emit_cast(c)
            for nb in range(48):
                mod_task(0, nb, mstg, macc, "sp", "dve", maccA)
            mod_flush()
            for ls in range(1, 2 * depth):
                for nb in range(48):
                    bg_mod.append(lambda ls=ls, nb=nb: mod_task(ls, nb, mstg, macc, "sp", "dve", maccA))
            P.barrier()
            P.emit()
            conv_layer(0, xT, yT if (depth == 1 and stop_mixer) else XS)
            if not (depth == 1 and stop_mixer):
                mlp_pass(0, XS, yT if depth == 1 else XS)
            bg_mod_run(10 ** 6)
            mod_flush()
            P.barrier()
            P.emit()
        for i in range(1, depth):
            src = XS
            last = (i == depth - 1)
            mdst = yT if (last and stop_mixer) else XS
            if i % 2 == 0:
                conv_layer(i, src, mdst)
            else:
                attn_layer(i, src, mdst)
            if not (last and stop_mixer):
                mlp_pass(i, XS, yT if last else XS)
    return nc


def _pk(v):
    v = np.asarray(v, dtype=np.float32)
    lead = v.shape[:-1]
    n = v.shape[-1] // 128
    v = v.reshape(*lead, n, 128)
    v = np.moveaxis(v, -1, 0)
    return np.ascontiguousarray(v.reshape(128, -1))


def _rope_tables(S):
    grid_w = 64
    t = np.arange(S)
    row = (t // grid_w).astype(np.float32)
    col = (t % grid_w).astype(np.float32)
    half = 32
    inv = (np.float32(10000.0) ** (-np.arange(half, dtype=np.float32) / np.float32(half))).astype(np.float32)
    ang_r = row[None, :] * inv[:, None]
    ang_c = col[None, :] * inv[:, None]
    cos = np.concatenate([np.cos(ang_r), np.cos(ang_r), np.cos(ang_c), np.cos(ang_c)], axis=0)
    sin = np.concatenate([np.sin(ang_r), np.sin(ang_r), np.sin(ang_c), np.sin(ang_c)], axis=0)
    return np.stack([cos, sin]).astype(np.float32)


def _consts():
    c = np.zeros((128, 256), np.float32)
    c[:, :128] = np.eye(128, dtype=np.float32)
    for m in range(128):
        if m % 64 < 32:
            c[m + 32, 128 + m] = -1.0
        else:
            c[m - 32, 128 + m] = 1.0
    return c


def make_in_maps(inp, S, nb):
    f = lambda k: np.ascontiguousarray(np.asarray(inp[k], dtype=np.float32))
    shared = {
        "cst": _consts(),
        "rope": _rope_tables(S),
        "mod_w": f("mod_w").reshape(8, D, 3 * D),
        "conv_w_in": f("conv_w_in"), "conv_w_out": f("conv_w_out"),
        "attn_w_qkv": f("attn_w_qkv"), "attn_w_out": f("attn_w_out"),
        "mlp_w_in": f("mlp_w_in"), "mlp_w_out": f("mlp_w_out"),
    }
    x = f("x")
    c = f("c")
    cdw = np.asarray(inp["conv_dw"], np.float32)
    maps = []
    for b in range(nb):
        cols = [
            _pk(c[b]),
            _pk(f("mod_b").reshape(8, 3 * D)),
            _pk(f("ln_g").reshape(8, D)),
            _pk(f("ln_b").reshape(8, D)),
            _pk(f("conv_b_in")),
            _pk(cdw),
            _pk(f("conv_dw_b")), _pk(f("conv_ln_g")), _pk(f("conv_ln_b")),
            np.ascontiguousarray(np.asarray(inp["attn_q_norm"], np.float32).T),
            np.ascontiguousarray(np.asarray(inp["attn_k_norm"], np.float32).T),
        ]
        vecs = np.ascontiguousarray(np.concatenate(cols, axis=1))
        assert vecs.shape == (128, NV), vecs.shape
        m = dict(shared)
        m["xT"] = np.ascontiguousarray(x[b, :S].T)
        m["vecs"] = vecs
        maps.append(m)
    return maps


_NC_CACHE = {}


def kernel(**inputs):
    x = np.asarray(inputs["x"])
    B, S, _ = x.shape
    key = (S, DEPTH)
    if key not in _NC_CACHE:
        _NC_CACHE[key] = build(S, DEPTH)
    nc = _NC_CACHE[key]
    maps = make_in_maps(inputs, S, B)
    res = run_bass_kernel_spmd(nc, maps, core_ids=list(range(B)))
    out = np.stack([np.ascontiguousarray(r["yT"].T) for r in res.results], axis=0)
    return out.astype(np.float32)
```

```python
import numpy as np
from contextlib import ExitStack
import concourse.bass as bass
import concourse.mybir as mybir
from concourse.bass_utils import run_bass_kernel_spmd

F32 = mybir.dt.float32
BF16 = mybir.dt.bfloat16
AF = mybir.ActivationFunctionType
ALU = mybir.AluOpType

D = 2048
KC = 16
T = 512
DEPTH = 4
DFF = 8192
ALPHA = (2 * DEPTH) ** 0.25
LN_EPS = 1e-5
RMS_EPS = 1e-6
CW = 31
UP = 16

V_C = 0
V_MODB = V_C + 16
V_LNG = V_MODB + 8 * 48
V_LNB = V_LNG + 8 * 16
V_CBIN = V_LNB + 8 * 16
V_CDW = V_CBIN + 2 * 32
V_CDWB = V_CDW + 2 * 31 * 16
V_CLNG = V_CDWB + 2 * 16
V_CLNB = V_CLNG + 2 * 16
V_QN = V_CLNB + 2 * 16
V_KN = V_QN + 2
NV = V_KN + 2


_UC = [0]


def _u(name):
    _UC[0] += 1
    return f"{name}_{_UC[0]}"


class Buf:
    __slots__ = ("name", "w", "r")

    def __init__(self, name=""):
        self.name = name
        self.w = None
        self.r = {}


class Prog:
    COMPUTE = ("pe", "act", "dve", "pool")
    ENG = ("pe", "act", "dve", "pool", "sp")

    def __init__(self, nc, stack, n_dma_sems=14):
        self.nc = nc
        self.stack = stack
        self.ops = {e: [] for e in self.ENG}
        self.sems = []
        self.own = {}
        self.cnt = {}
        self.n_own = 0
        self.sig = {}
        for e in self.COMPUTE:
            self._new_own(e)
        self.seen = {e: {} for e in self.ENG}
        self.dma_sems = {}
        self.dma_rr = {}
        self.dma_val = {}
        for q in ("sp", "pool"):
            lst = []
            for i in range(n_dma_sems):
                lst.append(len(self.sems))
                self.dma_val[len(self.sems)] = 0
                self.sems.append(stack.enter_context(nc.semaphore(f"d_{q}{i}")))
            self.dma_sems[q] = lst
            self.dma_rr[q] = 0
        self.nops = 0

    def _new_own(self, e):
        self.own[e] = len(self.sems)
        self.sems.append(self.stack.enter_context(self.nc.semaphore(f"s_{e}{self.n_own}")))
        self.n_own += 1
        self.cnt[e] = 0

    def _deps(self, e, reads, writes, extra=()):
        need = {}
        for b in reads:
            if b.w is not None:
                s, v = b.w
                if need.get(s, 0) < v:
                    need[s] = v
        for b in writes:
            if b.w is not None:
                s, v = b.w
                if need.get(s, 0) < v:
                    need[s] = v
            for s, v in b.r.items():
                if need.get(s, 0) < v:
                    need[s] = v
        for s, v in extra:
            if need.get(s, 0) < v:
                need[s] = v
        own = self.own.get(e)
        seen = self.seen[e]
        for s, v in need.items():
            if s == own and e == "pe":
                continue
            if seen.get(s, 0) >= v:
                continue
            seen[s] = v
            self.ops[e].append(("wait", s, v))

    def _mark(self, tok, reads, writes):
        s, v = tok
        for b in reads:
            if b.r.get(s, 0) < v:
                b.r[s] = v
        for b in writes:
            b.w = tok
            b.r = {}

    def op(self, e, fn, reads=(), writes=(), signal=True):
        self._deps(e, reads, writes)
        tok = (self.own[e], self.cnt[e] + 1)
        if signal:
            self.cnt[e] += 1
        self.ops[e].append(("op", fn, self.own[e] if signal else None))
        if signal:
            self.sig[self.own[e]] = self.cnt[e]
        self._mark(tok, reads, writes)
        self.nops += 1
        if signal and self.cnt[e] >= 28000:
            self._new_own(e)

    def dma(self, q, fn, reads=(), writes=()):
        slot = self.dma_rr[q]
        self.dma_rr[q] = (slot + 1) % len(self.dma_sems[q])
        s = self.dma_sems[q][slot]
        prev = self.dma_val[s]
        self._deps(q, reads, writes, extra=((s, prev),) if prev else ())
        tok = (s, prev + 16)
        self.dma_val[s] = prev + 16
        self.ops[q].append(("dma", fn, s))
        self._mark(tok, reads, writes)
        self.nops += 1

    def barrier(self):
        toks = [(s, v) for s, v in self.sig.items() if v > 0]
        for s, v in self.dma_val.items():
            if v > 0:
                toks.append((s, v))
        for e in self.ENG:
            seen = self.seen[e]
            for s, v in toks:
                if s == self.own.get(e):
                    continue
                if seen.get(s, 0) >= v:
                    continue
                seen[s] = v
                self.ops[e].append(("wait", s, v))

    def _replay(self, e, eng):
        for item in self.ops[e]:
            if item[0] == "wait":
                eng.wait_ge(self.sems[item[1]], item[2])
            elif item[0] == "op":
                ins = item[1](eng)
                if item[2] is not None:
                    ins.then_inc(self.sems[item[2]], 1)
            else:
                item[1](eng).then_inc(self.sems[item[2]], 16)
        self.ops[e] = []

    def emit(self):
        with self.nc.Block() as block:
            @block.tensor
            def _(eng):
                self._replay("pe", eng)

            @block.scalar
            def _(eng):
                self._replay("act", eng)

            @block.vector
            def _(eng):
                self._replay("dve", eng)

            @block.gpsimd
            def _(eng):
                self._replay("pool", eng)

            @block.sync
            def _(eng):
                self._replay("sp", eng)


class Ring:
    def __init__(self, nc, st, name, n, shape, dt):
        self.t = [st.enter_context(nc.sbuf_tensor(_u(f"{name}{i}"), shape, dt)) for i in range(n)]
        self.b = [Buf(f"{name}{i}") for i in range(n)]
        self.i = 0
        self.n = n

    def next(self):
        i = self.i
        self.i = (i + 1) % self.n
        return self.t[i], self.b[i]


class G:
    pass


def build(S=4096, depth=DEPTH, stop_mixer=False, dbg=False):
    NT = S // T
    nc = bass.Bass("TRN2", target_bir_lowering=False)
    g = G()
    g.nc = nc
    g.S, g.NT = S, NT
    dt = nc.dram_tensor
    xT = dt("xT", [D, S], F32, kind="ExternalInput").ap()
    vecs = dt("vecs", [128, NV], F32, kind="ExternalInput").ap()
    cst = dt("cst", [128, 256], F32, kind="ExternalInput").ap()
    rope = dt("rope", [2, 128, S], F32, kind="ExternalInput").ap()
    mod_w = dt("mod_w", [8, D, 3 * D], F32, kind="ExternalInput").ap()
    conv_w_in = dt("conv_w_in", [2, D, 2 * D], F32, kind="ExternalInput").ap()
    conv_w_out = dt("conv_w_out", [2, D, D], F32, kind="ExternalInput").ap()
    attn_w_qkv = dt("attn_w_qkv", [2, D, 3072], F32, kind="ExternalInput").ap()
    attn_w_out = dt("attn_w_out", [2, D, D], F32, kind="ExternalInput").ap()
    mlp_w_in = dt("mlp_w_in", [4, D, DFF], F32, kind="ExternalInput").ap()
    mlp_w_out = dt("mlp_w_out", [4, DFF, D], F32, kind="ExternalInput").ap()
    yT = dt("yT", [D, S], F32, kind="ExternalOutput").ap()
    XS = dt("XS", [D, S], F32).ap()
    UPD = dt("UPD", [D, S + 2 * UP], BF16, kind="ExternalOutput" if dbg else "Internal").ap()
    VC = dt("VC", [D, S], F32, kind="ExternalOutput" if dbg else "Internal").ap()
    n_conv = (depth + 1) // 2
    n_attn = depth // 2
    WB = {}
    for j in range(n_conv):
        WB["cin", j] = dt(f"wb_cin{j}", [8, 128, KC * 512], BF16).ap()
        WB["cout", j] = dt(f"wb_cout{j}", [4, 128, KC * 512], BF16).ap()
    for j in range(n_attn):
        WB["qkv", j] = dt(f"wb_qkv{j}", [6, 128, KC * 512], BF16).ap()
        WB["aout", j] = dt(f"wb_aout{j}", [4, 128, KC * 512], BF16).ap()
    for i in range(depth):
        WB["min", i] = dt(f"wb_min{i}", [16, 128, KC * 512], BF16).ap()
        WB["mout", i] = dt(f"wb_mout{i}", [16, 128, 32 * 256], BF16).ap()

    def xtile(ap, j):
        return ap.rearrange("(kc p) s -> p kc s", p=128)[:, :, j * T:(j + 1) * T]

    with ExitStack() as st:
        P = Prog(nc, st)
        g.P = P
        vec = st.enter_context(nc.sbuf_tensor(_u("vec"), [128, NV], F32))
        cst32 = st.enter_context(nc.sbuf_tensor(_u("cst32"), [128, 256], F32))
        ident_bf = st.enter_context(nc.sbuf_tensor(_u("ident_bf"), [128, 128], BF16))
        ones_bf = st.enter_context(nc.sbuf_tensor(_u("ones_bf"), [128, 128], BF16))
        cact = st.enter_context(nc.sbuf_tensor(_u("cact"), [128, 16], F32))
        modv = st.enter_context(nc.sbuf_tensor(_u("modv"), [128, 8, 48], F32))
        gqk = st.enter_context(nc.sbuf_tensor(_u("gqk"), [128, 4], F32))
        Bconst = Buf("const")
        Bmodv = Buf("modv")
        ps = [st.enter_context(nc.psum_tensor(f"ps{i}", [128, 512], F32)) for i in range(8)]
        psb = [Buf(f"ps{i}") for i in range(8)]
        rotT = cst32[:, 128:256]

        P.dma("sp", lambda e: e.dma_start(out=vec[:], in_=vecs), writes=[Bconst])
        P.dma("sp", lambda e: e.dma_start(out=cst32[:], in_=cst), writes=[Bconst])
        P.op("dve", lambda e: e.tensor_copy(out=ident_bf[:], in_=cst32[:, 0:128]), reads=[Bconst], writes=[Bconst])
        P.op("dve", lambda e: e.memset(ones_bf[:], 1.0), writes=[Bconst])
        P.op("act", lambda e: e.activation(out=cact[:], in_=vec[:, V_C:V_C + 16], func=AF.Silu), reads=[Bconst], writes=[Bconst])
        P.op("dve", lambda e: e.tensor_scalar(out=gqk[:], in0=vec[:, V_QN:V_QN + 4], scalar1=float(128.0 ** 0.5), scalar2=None, op0=ALU.mult),
             reads=[Bconst], writes=[Bconst])

        ones32 = st.enter_context(nc.sbuf_tensor(_u("ones32"), [128, 1], F32))
        P.op("dve", lambda e: e.memset(ones32[:], 1.0), writes=[Bconst])
        MODB = 7

        def cast_k2048(w, wb, nblk):
            wv = w.rearrange("(kc p) n -> p kc n", p=128)
            return [(wb[nb].rearrange("p (kc n) -> p kc n", kc=KC), wv[:, :, nb * 512:(nb + 1) * 512]) for nb in range(nblk)]

        def cast_mout(w, wb):
            wv = w.rearrange("(kc p) n -> p kc n", p=128)
            out = []
            for nb2 in range(8):
                for kh in range(2):
                    out.append((wb[nb2 * 2 + kh].rearrange("p (kc n) -> p kc n", kc=32),
                                wv[:, kh * 32:(kh + 1) * 32, nb2 * 256:(nb2 + 1) * 256]))
            return out

        def layer_casts(i):
            j = i // 2
            if i % 2 == 0:
                return cast_k2048(conv_w_in[j], WB["cin", j], 8) + cast_k2048(conv_w_out[j], WB["cout", j], 4)
            return cast_k2048(attn_w_qkv[j], WB["qkv", j], 6) + cast_k2048(attn_w_out[j], WB["aout", j], 4)

        def mlp_casts(i):
            return cast_k2048(mlp_w_in[i], WB["min", i], 16) + cast_mout(mlp_w_out[i], WB["mout", i])

        def emit_cast(c):
            dst, srcv = c
            P.dma("pool", lambda e: e.dma_start(out=dst, in_=srcv))

        bg_cast = []
        cast_mark = {}
        for i in range(depth):
            if i > 0:
                bg_cast += layer_casts(i)
            cast_mark["layer", i] = len(bg_cast)
            bg_cast += mlp_casts(i)
            cast_mark["mlp", i] = len(bg_cast)
        cast_mark["layer", depth] = len(bg_cast)
        bg_cast_pos = [0]

        def bg_cast_until(idx):
            bg_cast_run(max(0, idx - bg_cast_pos[0]))

        def bg_cast_run(n):
            for _ in range(n):
                if bg_cast_pos[0] < len(bg_cast):
                    emit_cast(bg_cast[bg_cast_pos[0]])
                    bg_cast_pos[0] += 1

        def mod_task(ls, nb, stgr, accr, q, eng, maccA):
            tl, tb = stgr.next()
            acc, accb = maccA.next()
            wv = mod_w[ls].rearrange("(kc p) n -> p kc n", p=128)
            P.dma(q, lambda e: e.dma_start(out=tl[:], in_=wv[:, :, nb * 256:(nb + 1) * 256]), writes=[tb])
            P.op(eng, lambda e: e.tensor_scalar(out=acc[:], in0=tl[:, 0, :], scalar1=cact[:, 0:1], scalar2=None, op0=ALU.mult),
                 reads=[tb, Bconst], writes=[accb])
            for kc in range(1, KC):
                if eng == "dve":
                    P.op(eng, lambda e, kc=kc: e.scalar_tensor_tensor(out=acc[:], in0=tl[:, kc, :], scalar=cact[:, kc:kc + 1], in1=acc[:],
                                                                     op0=ALU.mult, op1=ALU.add),
                         reads=[tb, accb, Bconst], writes=[accb])
                else:
                    tm, tmb = accr.next()
                    P.op(eng, lambda e, kc=kc, tm=tm: e.tensor_scalar(out=tm[:], in0=tl[:, kc, :], scalar1=cact[:, kc:kc + 1], scalar2=None, op0=ALU.mult),
                         reads=[tb, Bconst], writes=[tmb])
                    P.op(eng, lambda e, tm=tm: e.tensor_tensor(out=acc[:], in0=acc[:], in1=tm[:], op=ALU.add),
                         reads=[tmb, accb], writes=[accb])
            def pe_part():
                for n4 in range(2):
                    col = ls * 48 + nb * 2 + n4
                    P.op("pe", lambda e, n4=n4, col=col: e.matmul(ps[MODB][:, col:col + 1], acc[:, n4 * 128:(n4 + 1) * 128], ones32[:, 0:1],
                                                                  start=True, stop=True),
                         reads=[accb, Bconst], writes=[psb[MODB]])
                if nb == 23:
                    fin_part()

            def fin_part():
                P.op("dve", lambda e: e.tensor_tensor(out=modv[:, ls, :], in0=ps[MODB][:, ls * 48:(ls + 1) * 48],
                                                      in1=vec[:, V_MODB + ls * 48:V_MODB + (ls + 1) * 48], op=ALU.add),
                     reads=[psb[MODB], Bconst], writes=[Bmodv])
                P.op("dve", lambda e: e.tensor_scalar(out=modv[:, ls, 16:32], in0=modv[:, ls, 16:32], scalar1=1.0, scalar2=None, op0=ALU.add),
                     reads=[Bmodv], writes=[Bmodv])
                P.op("dve", lambda e: e.tensor_scalar(out=modv[:, ls, 32:48], in0=modv[:, ls, 32:48], scalar1=1.0, scalar2=float(1.0 / ALPHA),
                                                      op0=ALU.add, op1=ALU.mult),
                     reads=[Bmodv], writes=[Bmodv])

            mod_pend.append(pe_part)
            while len(mod_pend) > 3:
                mod_pend.pop(0)()

        mod_pend = []

        def mod_flush():
            while mod_pend:
                mod_pend.pop(0)()

        bg_mod = []
        bg_mod_pos = [0]

        def bg_mod_run(n):
            for _ in range(n):
                if bg_mod_pos[0] < len(bg_mod):
                    bg_mod[bg_mod_pos[0]]()
                    bg_mod_pos[0] += 1

        def mv(ls, part, n):
            return modv[:, ls, part * 16 + n:part * 16 + n + 1]

        def vcol(base, n):
            return vec[:, base + n:base + n + 1]

        def make_h(src, j, ls, h, hb, stg):
            sv = xtile(src, j)
            for kc2 in range(0, KC, 2):
                tl, tb = stg.next()
                P.dma("sp", lambda e, tl=tl, kc2=kc2: e.dma_start(out=tl[:], in_=sv[:, kc2:kc2 + 2, :]), writes=[tb])
                for q in range(2):
                    kc = kc2 + q
                    P.op("dve", lambda e, tl=tl, q=q, kc=kc: e.tensor_scalar(
                        out=h[:, kc, :], in0=tl[:, q, :], scalar1=mv(ls, 1, kc), scalar2=mv(ls, 0, kc),
                        op0=ALU.mult, op1=ALU.add), reads=[tb, Bmodv], writes=[hb[kc]])

        class Epi:
            def __init__(self, X, Xb, ls, tmpr, statr, s1, s2):
                self.X, self.Xb, self.ls = X, Xb, ls
                self.tmpr, self.statr = tmpr, statr
                self.s1, self.s2 = s1, s2

            def chunk(self, n, pt, ptb):
                X, Xb, ls = self.X, self.Xb, self.ls
                P.op("dve", lambda e: e.scalar_tensor_tensor(out=X[:, n, :], in0=pt[:], scalar=mv(ls, 2, n), in1=X[:, n, :],
                                                             op0=ALU.mult, op1=ALU.add),
                     reads=[ptb, Xb[n], Bmodv], writes=[Xb[n]])
                zs, zsb = self.tmpr.next()
                zb, zbb = self.tmpr.next()
                P.op("act", lambda e: e.activation(out=zs[:], in_=X[:, n, :], func=AF.Square), reads=[Xb[n]], writes=[zsb])
                P.op("dve", lambda e: e.tensor_copy(out=zb[:], in_=X[:, n, :]), reads=[Xb[n]], writes=[zbb])
                P.op("pe", lambda e: e.matmul(ps[self.s1][:], ones_bf[:], zb[:], start=(n == 0), stop=(n == KC - 1)),
                     reads=[zbb, Bconst], writes=[psb[self.s1]], signal=True)
                P.op("pe", lambda e: e.matmul(ps[self.s2][:], ones_bf[:], zs[:], start=(n == 0), stop=(n == KC - 1)),
                     reads=[zsb, Bconst], writes=[psb[self.s2]], signal=True)

            def stats(self, eps):
                mean, meanb = self.statr.next()
                rstd, rstdb = self.statr.next()
                P.op("dve", lambda e: e.tensor_scalar(out=mean[:], in0=ps[self.s1][:], scalar1=float(1.0 / D), scalar2=None, op0=ALU.mult),
                     reads=[psb[self.s1]], writes=[meanb])
                P.op("dve", lambda e: e.tensor_tensor(out=rstd[:], in0=mean[:], in1=mean[:], op=ALU.mult), reads=[meanb], writes=[rstdb])
                P.op("dve", lambda e: e.scalar_tensor_tensor(out=rstd[:], in0=ps[self.s2][:], scalar=float(1.0 / D), in1=rstd[:],
                                                             op0=ALU.mult, op1=ALU.subtract),
                     reads=[psb[self.s2], rstdb], writes=[rstdb])
                P.op("act", lambda e: e.activation(out=rstd[:], in_=rstd[:], func=AF.Sqrt, bias=float(eps), scale=1.0),
                     reads=[rstdb], writes=[rstdb])
                P.op("dve", lambda e: e.reciprocal(out=rstd[:], in_=rstd[:]), reads=[rstdb], writes=[rstdb])
                return mean, meanb, rstd, rstdb

            def finish(self, gbase, bbase):
                X, Xb = self.X, self.Xb
                mean, meanb, rstd, rstdb = self.stats(LN_EPS / (ALPHA * ALPHA))
                for n in range(KC):
                    P.op("dve", lambda e, n=n: e.tensor_tensor(out=X[:, n, :], in0=X[:, n, :], in1=mean[:], op=ALU.subtract),
                         reads=[Xb[n], meanb], writes=[Xb[n]])
                    P.op("dve", lambda e, n=n: e.tensor_tensor(out=X[:, n, :], in0=X[:, n, :], in1=rstd[:], op=ALU.mult),
                         reads=[Xb[n], rstdb], writes=[Xb[n]])
                    P.op("act", lambda e, n=n: e.activation(out=X[:, n, :], in_=X[:, n, :], func=AF.Identity,
                                                            scale=vcol(gbase, n), bias=vcol(bbase, n)),
                         reads=[Xb[n], Bconst], writes=[Xb[n]])

        def stream(steps, ring, depth_pf=None):
            n = len(steps)
            pf = ring.n - 1 if depth_pf is None else depth_pf
            loaded = {}

            def load(i):
                ap = steps[i][0]
                if ap is None:
                    loaded[i] = (None, None)
                    return
                tl, tb = ring.next()
                P.dma("sp", lambda e: e.dma_start(out=tl[:].rearrange("p a b -> p (a b)"), in_=ap), writes=[tb])
                loaded[i] = (tl, tb)

            for i in range(min(pf, n)):
                load(i)
            for i in range(n):
                if i + pf < n:
                    load(i + pf)
                tl, tb = loaded.pop(i)
                steps[i][1](tl, tb)

        def store_x(X, Xb, dst, j):
            P.dma("sp", lambda e: e.dma_start(out=xtile(dst, j), in_=X[:]), reads=list(Xb))

        def load_x(X, Xb, src, j):
            P.dma("sp", lambda e: e.dma_start(out=X[:], in_=xtile(src, j)), writes=list(Xb))

        def mlp_pass(i, src, dst):
            ls = 2 * i + 1
            with ExitStack() as ph:
                X = ph.enter_context(nc.sbuf_tensor(_u("mX"), [128, KC, T], F32))
                Xb = [Buf(f"mX{n}") for n in range(KC)]
                h = ph.enter_context(nc.sbuf_tensor(_u("mh"), [128, KC, T], BF16))
                hb = [Buf(f"mh{n}") for n in range(KC)]
                uT = ph.enter_context(nc.sbuf_tensor(_u("muT"), [128, 64, T], BF16))
                ub = [Buf(f"mu{n}") for n in range(64)]
                stg = Ring(nc, ph, "mstg", 2, [128, 2, T], F32)
                wr = Ring(nc, ph, "mw", 4, [128, KC, 512], BF16)
                tmpr = Ring(nc, ph, "mtmp", 4, [128, T], BF16)
                rr = Ring(nc, ph, "mrl", 3, [128, T], BF16)
                statr = Ring(nc, ph, "mstat", 2, [128, T], F32)
                steps = []
                pin = [2, 3, 6, 7]
                pin_i = [0]
                make_h(src, 0, ls, h, hb, stg)
                for j in range(NT):
                    epi = Epi(X, Xb, ls, tmpr, statr, 0, 1)
                    for nb in range(16):
                        def c_in(tl, tb, nb=nb):
                            bg_cast_run(1)
                            for hc in range(4):
                                pi = pin[pin_i[0] % 4]
                                pin_i[0] += 1
                                for kc in range(KC):
                                    P.op("pe", lambda e, kc=kc, hc=hc, pi=pi: e.matmul(
                                        ps[pi][:], tl[:, kc, hc * 128:(hc + 1) * 128], h[:, kc, :], start=(kc == 0), stop=(kc == KC - 1)),
                                        reads=[tb, hb[kc]], writes=[psb[pi]], signal=(kc == KC - 1))
                                r, rb = rr.next()
                                P.op("act", lambda e, pi=pi, r=r: e.activation(out=r[:], in_=ps[pi][:], func=AF.Relu),
                                     reads=[psb[pi]], writes=[rb])
                                u = nb * 4 + hc
                                eng = "dve"
                                P.op(eng, lambda e, r=r, u=u: e.tensor_tensor(out=uT[:, u, :], in0=r[:], in1=r[:], op=ALU.mult),
                                     reads=[rb], writes=[ub[u]])
                        steps.append((WB["min", i][nb], c_in))
                    for nb2 in range(8):
                        for kh in range(2):
                            def c_out(tl, tb, nb2=nb2, kh=kh, j=j, epi=epi):
                                if nb2 == 0 and kh == 0:
                                    load_x(X, Xb, src, j)
                                    if j + 1 < NT:
                                        make_h(src, j + 1, ls, h, hb, stg)
                                wv = tl[:].rearrange("p a b -> p (a b)").rearrange("p (kc n) -> p kc n", kc=32)
                                for half in range(2):
                                    pi = 4 + half
                                    for kk in range(32):
                                        kc = kh * 32 + kk
                                        P.op("pe", lambda e, kk=kk, kc=kc, half=half, pi=pi: e.matmul(
                                            ps[pi][:], wv[:, kk, half * 128:(half + 1) * 128], uT[:, kc, :],
                                            start=(kc == 0), stop=(kc == 63)),
                                            reads=[tb, ub[kc]], writes=[psb[pi]], signal=(kk == 31))
                                if kh == 1:
                                    for half in range(2):
                                        epi.chunk(nb2 * 2 + half, ps[4 + half], psb[4 + half])
                                if nb2 == 7 and kh == 1:
                                    epi.finish(V_LNG + ls * 16, V_LNB + ls * 16)
                                    store_x(X, Xb, dst, j)
                            steps.append((WB["mout", i][nb2 * 2 + kh], c_out))
                stream(steps, wr)
                bg_cast_until(cast_mark["layer", i + 1])
                P.barrier()
                P.emit()

        def conv_layer(i, src, dst):
            j_ = i // 2
            ls = 2 * i
            with ExitStack() as ph:
                h = ph.enter_context(nc.sbuf_tensor(_u("ch"), [128, KC, T], BF16))
                hb = [Buf(f"ch{n}") for n in range(KC)]
                stg = Ring(nc, ph, "cstg", 2, [128, 2, T], F32)
                wr = Ring(nc, ph, "cw", 4, [128, KC, 512], BF16)
                sgr = Ring(nc, ph, "csg", 3, [128, T], F32)
                ur = Ring(nc, ph, "cu", 2, [128, KC, T], BF16)
                pin_i = [0]
                steps = []
                make_h(src, 0, ls, h, hb, stg)
                for j in range(NT):
                    u, ubuf = ur.next()
                    for nb in range(4):
                        hold = {}

                        def c_a(tl, tb, hold=hold):
                            hold["a"] = (tl, tb)
                            bg_cast_run(1)
                            bg_mod_run(1)

                        def c_g(tl, tb, nb=nb, hold=hold, u=u, ubuf=ubuf, j=j):
                            ta, tab = hold["a"]
                            for hc in range(4):
                                n = nb * 4 + hc
                                pa = (pin_i[0] % 3) * 2
                                pin_i[0] += 1
                                pg = pa + 1
                                for (wt, wtb, pi) in ((ta, tab, pa), (tl, tb, pg)):
                                    for kc in range(KC):
                                        P.op("pe", lambda e, kc=kc, hc=hc, pi=pi, wt=wt: e.matmul(
                                            ps[pi][:], wt[:, kc, hc * 128:(hc + 1) * 128], h[:, kc, :], start=(kc == 0), stop=(kc == KC - 1)),
                                            reads=[wtb, hb[kc]], writes=[psb[pi]], signal=(kc == KC - 1))
                                sg, sgb = sgr.next()
                                P.op("act", lambda e, pg=pg, sg=sg, n=n: e.activation(out=sg[:], in_=ps[pg][:], func=AF.Sigmoid,
                                                                                   bias=vcol(V_CBIN + j_ * 32 + 16, n), scale=1.0),
                                     reads=[psb[pg], Bconst], writes=[sgb])
                                P.op("dve", lambda e, pa=pa, sg=sg, n=n, u=u: e.scalar_tensor_tensor(
                                    out=u[:, n, :], in0=ps[pa][:], scalar=vcol(V_CBIN + j_ * 32, n), in1=sg[:], op0=ALU.add, op1=ALU.mult),
                                    reads=[psb[pa], sgb, Bconst], writes=[ubuf])
                            if nb == 3:
                                P.dma("sp", lambda e, u=u, j=j: e.dma_start(
                                    out=UPD.rearrange("(kc p) s -> p kc s", p=128)[:, :, UP + j * T:UP + (j + 1) * T], in_=u[:]), reads=[ubuf])
                                if j + 1 < NT:
                                    make_h(src, j + 1, ls, h, hb, stg)
                        steps.append((WB["cin", j_][nb], c_a))
                        steps.append((WB["cin", j_][4 + nb], c_g))
                stream(steps, wr, depth_pf=2)
                bg_mod_run(max(0, 24 - bg_mod_pos[0]))
                mod_flush()
                P.barrier()
                P.emit()
            with ExitStack() as ph:
                urow = Ring(nc, ph, "c2u", 2, [128, S + 2 * UP], BF16)
                vrow = Ring(nc, ph, "c2v", 2, [128, S], F32)
                dgr = Ring(nc, ph, "c2d", 2, [128, CW, 128], BF16)
                pin_i = [0]
                def pre(c):
                    ur_, urb = urow.next()
                    dg, dgb = dgr.next()
                    P.dma("sp", lambda e: e.dma_start(out=ur_[:], in_=UPD[c * 128:(c + 1) * 128, :]), writes=[urb])
                    for tp in range(CW):
                        P.op("act", lambda e, tp=tp: e.activation(
                            out=dg[:, tp, :], in_=ident_bf[:], func=AF.Identity, scale=vcol(V_CDW + j_ * CW * 16 + tp * 16, c)),
                            reads=[Bconst], writes=[dgb])
                    return ur_, urb, dg, dgb

                nxt = pre(0)
                for c in range(KC):
                    ur_, urb, dg, dgb = nxt
                    if c + 1 < KC:
                        nxt = pre(c + 1)
                    vr_, vrb = vrow.next()
                    bg_cast_run(3)
                    bg_mod_run(5)
                    for j in range(NT):
                        pi = pin_i[0] % 7
                        pin_i[0] += 1
                        for tp in range(CW):
                            o = UP + j * T + tp - (CW // 2)
                            P.op("pe", lambda e, dg=dg, tp=tp, o=o, pi=pi, ur_=ur_: e.matmul(
                                ps[pi][:], dg[:, tp, :], ur_[:, o:o + T], start=(tp == 0), stop=(tp == CW - 1)),
                                reads=[dgb, urb], writes=[psb[pi]], signal=(tp == CW - 1))
                        eng = "act" if j % 2 == 0 else "dve"
                        if eng == "act":
                            P.op("act", lambda e, pi=pi, vr_=vr_, j=j, c=c: e.activation(
                                out=vr_[:, j * T:(j + 1) * T], in_=ps[pi][:], func=AF.Identity, bias=vcol(V_CDWB + j_ * 16, c), scale=1.0),
                                reads=[psb[pi], Bconst], writes=[vrb])
                        else:
                            P.op("dve", lambda e, pi=pi, vr_=vr_, j=j, c=c: e.tensor_scalar(
                                out=vr_[:, j * T:(j + 1) * T], in0=ps[pi][:], scalar1=vcol(V_CDWB + j_ * 16, c), scalar2=None, op0=ALU.add),
                                reads=[psb[pi], Bconst], writes=[vrb])
                    P.dma("sp", lambda e, vr_=vr_, c=c: e.dma_start(out=VC[c * 128:(c + 1) * 128, :], in_=vr_[:]), reads=[vrb])
                mod_flush()
                P.barrier()
                P.emit()
            with ExitStack() as ph:
                Xr = [(ph.enter_context(nc.sbuf_tensor(_u("cX"), [128, KC, T], F32)), [Buf(f"cX{q}_{n}") for n in range(KC)]) for q in range(2)]
                sTr = Ring(nc, ph, "csT", 2, [128, KC, T], BF16)
                vstg = Ring(nc, ph, "cvst", 3, [128, 2, T], F32)
                wr = Ring(nc, ph, "c3w", 2, [128, KC, 512], BF16)
                tmpr = Ring(nc, ph, "ctmp", 4, [128, T], BF16)
                statr = Ring(nc, ph, "cstat", 4, [128, T], F32)
                steps = []
                pin_i = [0]
                vt = xtile(VC, 0)
                Bdummy = Buf("vdummy")

                def prep1(j):
                    sv = xtile(VC, j)
                    for k2 in range(0, KC, 2):
                        tl, tb = vstg.next()
                        P.dma("sp", lambda e, tl=tl, k2=k2: e.dma_start(out=tl[:], in_=sv[:, k2:k2 + 2, :]), writes=[tb])
                        for q in range(2):
                            n = k2 + q
                            zs, zsb = tmpr.next()
                            zb, zbb = tmpr.next()
                            P.op("act", lambda e, tl=tl, q=q, zs=zs: e.activation(out=zs[:], in_=tl[:, q, :], func=AF.Square), reads=[tb], writes=[zsb])
                            P.op("dve", lambda e, tl=tl, q=q, zb=zb: e.tensor_copy(out=zb[:], in_=tl[:, q, :]), reads=[tb], writes=[zbb])
                            P.op("pe", lambda e, n=n, zb=zb: e.matmul(ps[2][:], ones_bf[:], zb[:], start=(n == 0), stop=(n == KC - 1)),
                                 reads=[zbb, Bconst], writes=[psb[2]])
                            P.op("pe", lambda e, n=n, zs=zs: e.matmul(ps[3][:], ones_bf[:], zs[:], start=(n == 0), stop=(n == KC - 1)),
                                 reads=[zsb, Bconst], writes=[psb[3]])

                def prep2(j, sT, sb):
                    vepi = Epi(None, None, ls, tmpr, statr, 2, 3)
                    mean, meanb, rstd, rstdb = vepi.stats(LN_EPS)
                    sv = xtile(VC, j)
                    for k2 in range(0, KC, 2):
                        tl, tb = vstg.next()
                        P.dma("sp", lambda e, tl=tl, k2=k2: e.dma_start(out=tl[:], in_=sv[:, k2:k2 + 2, :]), writes=[tb])
                        for q in range(2):
                            n = k2 + q
                            P.op("dve", lambda e, tl=tl, q=q: e.tensor_tensor(out=tl[:, q, :], in0=tl[:, q, :], in1=mean[:], op=ALU.subtract),
                                 reads=[tb, meanb], writes=[tb])
                            P.op("dve", lambda e, tl=tl, q=q: e.tensor_tensor(out=tl[:, q, :], in0=tl[:, q, :], in1=rstd[:], op=ALU.mult),
                                 reads=[tb, rstdb], writes=[tb])
                            P.op("act", lambda e, tl=tl, q=q, n=n: e.activation(out=sT[:, n, :], in_=tl[:, q, :], func=AF.Silu,
                                                                               scale=vcol(V_CLNG + j_ * 16, n), bias=vcol(V_CLNB + j_ * 16, n)),
                                 reads=[tb, Bconst], writes=[sb[n]])

                sTb = {}
                for j in range(NT):
                    sTb[j] = sTr.next() + ([Buf(f"cs{j}_{n}") for n in range(KC)],)
                slot_bufs = {}
                for j in range(NT):
                    slot_bufs.setdefault(j % 2, sTb[j][2])
                    sTb[j] = (sTb[j][0], sTb[j][1], slot_bufs[j % 2])
                prep1(0)
                prep2(0, sTb[0][0], sTb[0][2])
                load_x(Xr[0][0], Xr[0][1], src, 0)
                for j in range(NT):
                    X, Xb = Xr[j % 2]
                    epi = Epi(X, Xb, ls, tmpr, statr, 0, 1)
                    sT, _, sb = sTb[j]
                    for nb in range(4):
                        def c_o(tl, tb, nb=nb, j=j, epi=epi, sT=sT, sb=sb, X=X, Xb=Xb):
                            bg_cast_run(1)
                            bg_mod_run(1)
                            if nb == 0 and j + 1 < NT:
                                load_x(Xr[(j + 1) % 2][0], Xr[(j + 1) % 2][1], src, j + 1)
                            if nb == 1 and j + 1 < NT:
                                prep1(j + 1)
                            if nb == 2 and j + 1 < NT:
                                prep2(j + 1, sTb[j + 1][0], sTb[j + 1][2])
                            for hc in range(4):
                                n = nb * 4 + hc
                                pi = 4 + (pin_i[0] % 3)
                                pin_i[0] += 1
                                for kc in range(KC):
                                    P.op("pe", lambda e, kc=kc, hc=hc, pi=pi: e.matmul(
                                        ps[pi][:], tl[:, kc, hc * 128:(hc + 1) * 128], sT[:, kc, :], start=(kc == 0), stop=(kc == KC - 1)),
                                        reads=[tb, sb[kc]], writes=[psb[pi]], signal=(kc == KC - 1))
                                epi.chunk(n, ps[pi], psb[pi])
                            if nb == 3:
                                epi.finish(V_LNG + ls * 16, V_LNB + ls * 16)
                                store_x(X, Xb, dst, j)
                        steps.append((WB["cout", j_][nb], c_o))
                stream(steps, wr)
                bg_mod_run(10 ** 6)
                mod_flush()
                bg_cast_until(cast_mark["mlp", i])
                P.barrier()
                P.emit()

        def attn_layer(i, src, dst):
            j_ = i // 2
            ls = 2 * i
            NKC = S // 128
            with ExitStack() as al:
                KT = al.enter_context(nc.sbuf_tensor(_u("KT"), [128, 4, S], BF16))
                Vs = al.enter_context(nc.sbuf_tensor(_u("Vs"), [128, NKC, 512], BF16))
                KTb = [[Buf(f"KT{kv}_{j}") for j in range(NT)] for kv in range(4)]
                Vsb = [Buf(f"Vs{c}") for c in range(NKC)]
                rope_rings = [None, None]
                gen_i = [0]

                gen_banks = [[4, 5, 6, 7]]

                def gbank():
                    gb = gen_banks[0]
                    pi = gb[gen_i[0] % len(gb)]
                    gen_i[0] += 1
                    return pi

                def load_rope(j):
                    ct, cb = rope_rings[0].next()
                    sn, snb = rope_rings[1].next()
                    P.dma("sp", lambda e: e.dma_start(out=ct[:], in_=rope[0][:, j * T:(j + 1) * T]), writes=[cb])
                    P.dma("sp", lambda e: e.dma_start(out=sn[:], in_=rope[1][:, j * T:(j + 1) * T]), writes=[snb])
                    return ct, cb, sn, snb

                def norm_rope_stages(proj_fn, gcol, rp, out_ap, out_bufs, sqr, f32r):
                    ct, cb, sn, snb = rp
                    stt = {}

                    def A():
                        pi = gbank()
                        proj_fn(pi)
                        sq, sqb = sqr.next()
                        P.op("act", lambda e: e.activation(out=sq[:], in_=ps[pi][:], func=AF.Square), reads=[psb[pi]], writes=[sqb])
                        stt["pi"], stt["sq"], stt["sqb"] = pi, sq, sqb

                    def B():
                        pi, sq, sqb = stt["pi"], stt["sq"], stt["sqb"]
                        p2 = gbank()
                        P.op("pe", lambda e: e.matmul(ps[p2][:], ones_bf[:], sq[:], start=True, stop=True),
                             reads=[sqb, Bconst], writes=[psb[p2]])
                        rs, rsb = f32r.next()
                        P.op("act", lambda e: e.activation(out=rs[:], in_=ps[p2][:], func=AF.Ln, bias=float(128.0 * RMS_EPS), scale=1.0),
                             reads=[psb[p2]], writes=[rsb])
                        P.op("act", lambda e: e.activation(out=rs[:], in_=rs[:], func=AF.Exp, scale=-0.5), reads=[rsb], writes=[rsb])
                        qn, qnb = f32r.next()
                        P.op("dve", lambda e: e.scalar_tensor_tensor(out=qn[:], in0=ps[pi][:], scalar=gcol, in1=rs[:], op0=ALU.mult, op1=ALU.mult),
                             reads=[psb[pi], rsb, Bconst], writes=[qnb])
                        stt["qn"], stt["qnb"], stt["rs"], stt["rsb"] = qn, qnb, rs, rsb

                    def C():
                        qn, qnb, rs, rsb = stt["qn"], stt["qnb"], stt["rs"], stt["rsb"]
                        p3 = gbank()
                        P.op("pe", lambda e: e.matmul(ps[p3][:], rotT, qn[:], start=True, stop=True),
                             reads=[qnb, Bconst], writes=[psb[p3]])
                        P.op("dve", lambda e: e.tensor_tensor(out=rs[:], in0=ps[p3][:], in1=sn[:], op=ALU.mult),
                             reads=[psb[p3], snb], writes=[rsb])
                        P.op("dve", lambda e: e.tensor_tensor(out=qn[:], in0=qn[:], in1=ct[:], op=ALU.mult),
                             reads=[qnb, cb], writes=[qnb])
                        P.op("dve", lambda e: e.tensor_tensor(out=out_ap, in0=qn[:], in1=rs[:], op=ALU.add),
                             reads=[qnb, rsb], writes=out_bufs)

                    return [A, B, C]

                with ExitStack() as ph:
                    h = ph.enter_context(nc.sbuf_tensor(_u("ah"), [128, KC, T], BF16))
                    hb = [Buf(f"ah{n}") for n in range(KC)]
                    stg = Ring(nc, ph, "astg", 2, [128, 2, T], F32)
                    rope_rings[0] = Ring(nc, ph, "cosr1", 2, [128, T], F32)
                    rope_rings[1] = Ring(nc, ph, "sinr1", 2, [128, T], F32)
                    sqr1 = Ring(nc, ph, "sqr1", 4, [128, T], BF16)
                    f32r1 = Ring(nc, ph, "f32r1", 8, [128, T], F32)
                    wk = ph.enter_context(nc.sbuf_tensor(_u("awk"), [128, KC, 512], BF16))
                    wv_ = ph.enter_context(nc.sbuf_tensor(_u("awv"), [128, KC, 512], BF16))
                    Bwk, Bwv = Buf("wk"), Buf("wv")
                    P.dma("sp", lambda e: e.dma_start(out=wk[:].rearrange("p a b -> p (a b)"), in_=WB["qkv", j_][4]), writes=[Bwk])
                    P.dma("sp", lambda e: e.dma_start(out=wv_[:].rearrange("p a b -> p (a b)"), in_=WB["qkv", j_][5]), writes=[Bwv])
                    def vproj(j, tc4):
                        pi = 6 + (tc4 % 2)
                        for kc in range(KC):
                            P.op("pe", lambda e, kc=kc, tc4=tc4, pi=pi: e.matmul(ps[pi][:], h[:, kc, tc4 * 128:(tc4 + 1) * 128], wv_[:, kc, :],
                                                                                  start=(kc == 0), stop=(kc == KC - 1)),
                                 reads=[Bwv, hb[kc]], writes=[psb[pi]], signal=(kc == KC - 1))
                        c = j * 4 + tc4
                        if tc4 % 2 == 0:
                            P.op("act", lambda e, c=c, pi=pi: e.activation(out=Vs[:, c, :], in_=ps[pi][:], func=AF.Copy), reads=[psb[pi]], writes=[Vsb[c]])
                        else:
                            P.op("dve", lambda e, c=c, pi=pi: e.tensor_copy(out=Vs[:, c, :], in_=ps[pi][:]), reads=[psb[pi]], writes=[Vsb[c]])

                    for j in range(NT):
                        make_h(src, j, ls, h, hb, stg)
                        rp = load_rope(j)
                        stgs = []
                        for kv in range(4):
                            def proj(pi, kv=kv):
                                for kc in range(KC):
                                    P.op("pe", lambda e, kc=kc: e.matmul(ps[pi][:], wk[:, kc, kv * 128:(kv + 1) * 128], h[:, kc, :],
                                                                          start=(kc == 0), stop=(kc == KC - 1)),
                                         reads=[Bwk, hb[kc]], writes=[psb[pi]], signal=(kc == KC - 1))
                            stgs.append(norm_rope_stages(proj, gqk[:, 2 + j_:3 + j_], rp, KT[:, kv, j * T:(j + 1) * T], [KTb[kv][j]], sqr1, f32r1))
                        gen_banks[0] = [0, 1, 2, 3]
                        gen_i[0] = 0
                        for kv in range(4):
                            stgs[kv][0]()
                        vproj(j, 0)
                        vproj(j, 1)
                        gen_banks[0] = [4, 5]
                        gen_i[0] = 0
                        for kv in range(4):
                            stgs[kv][1]()
                        vproj(j, 2)
                        vproj(j, 3)
                        for kv in range(4):
                            stgs[kv][2]()
                    P.barrier()
                    P.emit()

                with ExitStack() as ph:
                    X = ph.enter_context(nc.sbuf_tensor(_u("aX"), [128, KC, T], F32))
                    Xb = [Buf(f"aX{n}") for n in range(KC)]
                    h = ph.enter_context(nc.sbuf_tensor(_u("a2h"), [128, KC, T], BF16))
                    hb = [Buf(f"a2h{n}") for n in range(KC)]
                    stg = Ring(nc, ph, "a2stg", 2, [128, 1, T], F32)
                    qr = Ring(nc, ph, "a2q", 2, [128, T], BF16)
                    OT = ph.enter_context(nc.sbuf_tensor(_u("a2O"), [128, KC, T], BF16))
                    Ob = [Buf(f"a2O{n}") for n in range(KC)]
                    pr = Ring(nc, ph, "a2p", 6, [128, T], BF16)
                    wqr = Ring(nc, ph, "a2wq", 2, [128, KC, 512], BF16)
                    tmpr = Ring(nc, ph, "a2tmp", 4, [128, T], BF16)
                    statr = Ring(nc, ph, "a2stat", 2, [128, T], F32)
                    rcr = Ring(nc, ph, "a2rc", 2, [128, T], F32)
                    rope_rings[0] = Ring(nc, ph, "cosr2", 1, [128, T], F32)
                    rope_rings[1] = Ring(nc, ph, "sinr2", 1, [128, T], F32)
                    sqr2 = Ring(nc, ph, "sqr2", 2, [128, T], BF16)
                    f32r2 = Ring(nc, ph, "f32r2", 2, [128, T], F32)
                    scale = float(128.0 ** -0.5)

                    def make_h1(j):
                        sv = xtile(src, j)
                        for kc in range(KC):
                            tl, tb = stg.next()
                            P.dma("sp", lambda e, tl=tl, kc=kc: e.dma_start(out=tl[:], in_=sv[:, kc:kc + 1, :]), writes=[tb])
                            P.op("dve", lambda e, tl=tl, kc=kc: e.tensor_scalar(
                                out=h[:, kc, :], in0=tl[:, 0, :], scalar1=mv(ls, 1, kc), scalar2=mv(ls, 0, kc),
                                op0=ALU.mult, op1=ALU.add), reads=[tb, Bmodv], writes=[hb[kc]])

                    units = [(j, hd) for j in range(NT) for hd in range(16)]
                    wq_cur = {}
                    rope_cur = {}

                    def prep(u):
                        j, hd = units[u]
                        if hd == 0:
                            rope_cur[j] = load_rope(j)
                        if hd % 4 == 0:
                            tl, tb = wqr.next()
                            blk = WB["qkv", j_][hd // 4]
                            P.dma("sp", lambda e: e.dma_start(out=tl[:].rearrange("p a b -> p (a b)"), in_=blk), writes=[tb])
                            wq_cur[(j, hd // 4)] = (tl, tb)
                        tl, tb = wq_cur[(j, hd // 4)]
                        q, qb = qr.next()

                        def proj(pi):
                            hh = hd % 4
                            for kc in range(KC):
                                P.op("pe", lambda e, kc=kc: e.matmul(ps[pi][:], tl[:, kc, hh * 128:(hh + 1) * 128], h[:, kc, :],
                                                                      start=(kc == 0), stop=(kc == KC - 1)),
                                     reads=[tb, hb[kc]], writes=[psb[pi]], signal=(kc == KC - 1))
                        return norm_rope_stages(proj, gqk[:, j_:j_ + 1], rope_cur[j], q[:], [qb], sqr2, f32r2), (q, qb)

                    gen_banks[0] = [6, 7]
                    gen_i[0] = 0

                    def run_tile(j, qs):
                        us = [u for u in range(len(units)) if units[u][0] == j]
                        its = [(u, kc) for u in us for kc in range(NKC)]
                        pend = {}
                        hk = [0, max(1, NKC // 4), max(2, NKC // 2)]

                        def qk(i):
                            u, kc = its[i]
                            q, qb = qs[u]
                            kv = units[u][1] // 4
                            bk = i % 3
                            P.op("pe", lambda e: e.matmul(ps[bk][:], KT[:, kv, kc * 128:(kc + 1) * 128], q[:], start=True, stop=True),
                                 reads=[KTb[kv][kc // 4], qb], writes=[psb[bk]])
                            p, pb = pr.next()
                            P.op("act", lambda e: e.activation(out=p[:], in_=ps[bk][:], func=AF.Exp, scale=scale),
                                 reads=[psb[bk]], writes=[pb])
                            pend[i] = (p, pb)

                        for i0 in range(min(3, len(its))):
                            qk(i0)
                        for i in range(len(its)):
                            u, kc = its[i]
                            hd = units[u][1]
                            kv = hd // 4
                            OACC, SUMS = 3 + (hd % 2), 5
                            if hd < 15:
                                if kc == hk[0]:
                                    st_, qq = prep(u + 1)
                                    qs[u + 1] = qq
                                    qs[("st", u + 1)] = st_
                                    st_[0]()
                                elif kc == hk[1]:
                                    qs[("st", u + 1)][1]()
                                elif kc == hk[2]:
                                    qs[("st", u + 1)][2]()
                            elif kc == hk[0] and j + 1 < NT:
                                make_h1(j + 1)
                            p, pb = pend.pop(i)
                            P.op("pe", lambda e, kc=kc, kv=kv, p=p, OACC=OACC: e.matmul(ps[OACC][:], Vs[:, kc, kv * 128:(kv + 1) * 128], p[:],
                                                                              start=(kc == 0), stop=(kc == NKC - 1)),
                                 reads=[Vsb[kc], pb], writes=[psb[OACC]], signal=False)
                            P.op("pe", lambda e, kc=kc, p=p, SUMS=SUMS: e.matmul(ps[SUMS][:], ones_bf[:], p[:], start=(kc == 0), stop=(kc == NKC - 1)),
                                 reads=[pb, Bconst], writes=[psb[SUMS]], signal=True)
                            if i + 3 < len(its):
                                qk(i + 3)
                            if kc == NKC - 1:
                                rc, rcb = rcr.next()
                                P.op("dve", lambda e, rc=rc, SUMS=SUMS: e.tensor_copy(out=rc[:], in_=ps[SUMS][:]), reads=[psb[SUMS]], writes=[rcb])
                                P.op("dve", lambda e, rc=rc: e.reciprocal(out=rc[:], in_=rc[:]), reads=[rcb], writes=[rcb])
                                P.op("dve", lambda e, rc=rc, hd=hd, OACC=OACC: e.tensor_tensor(out=OT[:, hd, :], in0=ps[OACC][:], in1=rc[:], op=ALU.mult),
                                     reads=[psb[OACC], rcb], writes=[Ob[hd]])

                    def out_proj(j):
                        epi = Epi(X, Xb, ls, tmpr, statr, 0, 1)
                        load_x(X, Xb, src, j)
                        for nb in range(4):
                            tl, tb = wqr.next()
                            blk = WB["aout", j_][nb]
                            P.dma("sp", lambda e, tl=tl, blk=blk: e.dma_start(out=tl[:].rearrange("p a b -> p (a b)"), in_=blk), writes=[tb])
                            for hc in range(4):
                                n = nb * 4 + hc
                                pi = gbank()
                                for kc in range(KC):
                                    P.op("pe", lambda e, kc=kc, hc=hc, pi=pi, tl=tl: e.matmul(
                                        ps[pi][:], tl[:, kc, hc * 128:(hc + 1) * 128], OT[:, kc, :], start=(kc == 0), stop=(kc == KC - 1)),
                                        reads=[tb, Ob[kc]], writes=[psb[pi]], signal=(kc == KC - 1))
                                epi.chunk(n, ps[pi], psb[pi])
                        epi.finish(V_LNG + ls * 16, V_LNB + ls * 16)
                        store_x(X, Xb, dst, j)

                    make_h1(0)
                    for j in range(NT):
                        u0 = j * 16
                        st_, qq = prep(u0)
                        for s_ in st_:
                            s_()
                        qs = {u0: qq}
                        run_tile(j, qs)
                        out_proj(j)
                    bg_cast_until(cast_mark["mlp", i])
                    P.barrier()
                    P.emit()

        with ExitStack() as msc:
            mstg = Ring(nc, msc, "modstg", 2, [128, KC, 256], F32)
            macc = Ring(nc, msc, "modacc", 2, [128, 256], F32)
            maccA = Ring(nc, msc, "modaccA", 6, [128, 256], F32)
            zt = msc.enter_context(nc.sbuf_tensor(_u("zt"), [128, UP], BF16))
            Bz = Buf("zt")
            P.op("dve", lambda e: e.memset(zt[:], 0.0), writes=[Bz])
            for kc in range(KC):
                upv = UPD[kc * 128:(kc + 1) * 128, :]
                P.dma("sp", lambda e, upv=upv: e.dma_start(out=upv[:, 0:UP], in_=zt[:]), reads=[Bz])
                P.dma("sp", lambda e, upv=upv: e.dma_start(out=upv[:, UP + S:UP + S + UP], in_=zt[:]), reads=[Bz])
            for c in layer_casts(0):
                emit_cast(c)
            for nb in range(24):
                mod_task(0, nb, mstg, macc, "sp", "dve", maccA)
            mod_flush()
            for ls in range(1, 2 * depth):
                for nb in range(24):
                    bg_mod.append(lambda ls=ls, nb=nb: mod_task(ls, nb, mstg, macc, "pool", "dve", maccA))
            P.barrier()
            P.emit()
            conv_layer(0, xT, yT if (depth == 1 and stop_mixer) else XS)
        if not (depth == 1 and stop_mixer):
            mlp_pass(0, XS, yT if depth == 1 else XS)
        for i in range(1, depth):
            src = XS
            last = (i == depth - 1)
            mdst = yT if (last and stop_mixer) else XS
            if i % 2 == 0:
                conv_layer(i, src, mdst)
            else:
                attn_layer(i, src, mdst)
            if not (last and stop_mixer):
                mlp_pass(i, XS, yT if last else XS)
    return nc


def _pk(v):
    v = np.asarray(v, dtype=np.float32)
    lead = v.shape[:-1]
    n = v.shape[-1] // 128
    v = v.reshape(*lead, n, 128)
    v = np.moveaxis(v, -1, 0)
    return np.ascontiguousarray(v.reshape(128, -1))


def _rope_tables(S):
    grid_w = 64
    t = np.arange(S)
    row = (t // grid_w).astype(np.float32)
    col = (t % grid_w).astype(np.float32)
    half = 32
    inv = (np.float32(10000.0) ** (-np.arange(half, dtype=np.float32) / np.float32(half))).astype(np.float32)
    ang_r = row[None, :] * inv[:, None]
    ang_c = col[None, :] * inv[:, None]
    cos = np.concatenate([np.cos(ang_r), np.cos(ang_r), np.cos(ang_c), np.cos(ang_c)], axis=0)
    sin = np.concatenate([np.sin(ang_r), np.sin(ang_r), np.sin(ang_c), np.sin(ang_c)], axis=0)
    return np.stack([cos, sin]).astype(np.float32)


def _consts():
    c = np.zeros((128, 256), np.float32)
    c[:, :128] = np.eye(128, dtype=np.float32)
    for m in range(128):
        if m % 64 < 32:
            c[m + 32, 128 + m] = -1.0
        else:
            c[m - 32, 128 + m] = 1.0
    return c


def make_in_maps(inp, S, nb):
    f = lambda k: np.ascontiguousarray(np.asarray(inp[k], dtype=np.float32))
    shared = {
        "cst": _consts(),
        "rope": _rope_tables(S),
        "mod_w": f("mod_w").reshape(8, D, 3 * D),
        "conv_w_in": f("conv_w_in"), "conv_w_out": f("conv_w_out"),
        "attn_w_qkv": f("attn_w_qkv"), "attn_w_out": f("attn_w_out"),
        "mlp_w_in": f("mlp_w_in"), "mlp_w_out": f("mlp_w_out"),
    }
    x = f("x")
    c = f("c")
    cdw = np.asarray(inp["conv_dw"], np.float32)
    maps = []
    for b in range(nb):
        cols = [
            _pk(c[b]),
            _pk(f("mod_b").reshape(8, 3 * D)),
            _pk(f("ln_g").reshape(8, D)),
            _pk(f("ln_b").reshape(8, D)),
            _pk(f("conv_b_in")),
            _pk(cdw),
            _pk(f("conv_dw_b")), _pk(f("conv_ln_g")), _pk(f("conv_ln_b")),
            np.ascontiguousarray(np.asarray(inp["attn_q_norm"], np.float32).T),
            np.ascontiguousarray(np.asarray(inp["attn_k_norm"], np.float32).T),
        ]
        vecs = np.ascontiguousarray(np.concatenate(cols, axis=1))
        assert vecs.shape == (128, NV), vecs.shape
        m = dict(shared)
        m["xT"] = np.ascontiguousarray(x[b, :S].T)
        m["vecs"] = vecs
        maps.append(m)
    return maps


_NC_CACHE = {}


def kernel(**inputs):
    x = np.asarray(inputs["x"])
    B, S, _ = x.shape
    key = (S, DEPTH)
    if key not in _NC_CACHE:
        _NC_CACHE[key] = build(S, DEPTH)
    nc = _NC_CACHE[key]
    maps = make_in_maps(inputs, S, B)
    res = run_bass_kernel_spmd(nc, maps, core_ids=list(range(B)))
    out = np.stack([np.ascontiguousarray(r["yT"].T) for r in res.results], axis=0)
    return out.astype(np.float32)
```

```python
import numpy as np
from contextlib import ExitStack
import concourse.bass as bass
import concourse.mybir as mybir
from concourse.bass_utils import run_bass_kernel_spmd

F32 = mybir.dt.float32
BF16 = mybir.dt.bfloat16
AF = mybir.ActivationFunctionType
ALU = mybir.AluOpType

D = 2048
KC = 16
T = 512
DEPTH = 4
DFF = 8192
ALPHA = (2 * DEPTH) ** 0.25
LN_EPS = 1e-5
RMS_EPS = 1e-6
CW = 31
UP = 16

V_C = 0
V_MODB = V_C + 16
V_LNG = V_MODB + 8 * 48
V_LNB = V_LNG + 8 * 16
V_CBIN = V_LNB + 8 * 16
V_CDW = V_CBIN + 2 * 32
V_CDWB = V_CDW + 2 * 31 * 16
V_CLNG = V_CDWB + 2 * 16
V_CLNB = V_CLNG + 2 * 16
V_QN = V_CLNB + 2 * 16
V_KN = V_QN + 2
NV = V_KN + 2


_UC = [0]


def _u(name):
    _UC[0] += 1
    return f"{name}_{_UC[0]}"


class Buf:
    __slots__ = ("name", "w", "r")

    def __init__(self, name=""):
        self.name = name
        self.w = None
        self.r = {}


class Prog:
    COMPUTE = ("pe", "act", "dve", "pool")
    ENG = ("pe", "act", "dve", "pool", "sp")

    def __init__(self, nc, stack, n_dma_sems=14):
        self.nc = nc
        self.stack = stack
        self.ops = {e: [] for e in self.ENG}
        self.sems = []
        self.own = {}
        self.cnt = {}
        self.n_own = 0
        self.sig = {}
        for e in self.COMPUTE:
            self._new_own(e)
        self.seen = {e: {} for e in self.ENG}
        self.dma_sems = {}
        self.dma_rr = {}
        self.dma_val = {}
        for q in ("sp", "pool"):
            lst = []
            for i in range(n_dma_sems):
                lst.append(len(self.sems))
                self.dma_val[len(self.sems)] = 0
                self.sems.append(stack.enter_context(nc.semaphore(f"d_{q}{i}")))
            self.dma_sems[q] = lst
            self.dma_rr[q] = 0
        self.nops = 0

    def _new_own(self, e):
        self.own[e] = len(self.sems)
        self.sems.append(self.stack.enter_context(self.nc.semaphore(f"s_{e}{self.n_own}")))
        self.n_own += 1
        self.cnt[e] = 0

    def _deps(self, e, reads, writes, extra=()):
        need = {}
        for b in reads:
            if b.w is not None:
                s, v = b.w
                if need.get(s, 0) < v:
                    need[s] = v
        for b in writes:
            if b.w is not None:
                s, v = b.w
                if need.get(s, 0) < v:
                    need[s] = v
            for s, v in b.r.items():
                if need.get(s, 0) < v:
                    need[s] = v
        for s, v in extra:
            if need.get(s, 0) < v:
                need[s] = v
        own = self.own.get(e)
        seen = self.seen[e]
        for s, v in need.items():
            if s == own and e == "pe":
                continue
            if seen.get(s, 0) >= v:
                continue
            seen[s] = v
            self.ops[e].append(("wait", s, v))

    def _mark(self, tok, reads, writes):
        s, v = tok
        for b in reads:
            if b.r.get(s, 0) < v:
                b.r[s] = v
        for b in writes:
            b.w = tok
            b.r = {}

    def op(self, e, fn, reads=(), writes=(), signal=True):
        self._deps(e, reads, writes)
        tok = (self.own[e], self.cnt[e] + 1)
        if signal:
            self.cnt[e] += 1
        self.ops[e].append(("op", fn, self.own[e] if signal else None))
        if signal:
            self.sig[self.own[e]] = self.cnt[e]
        self._mark(tok, reads, writes)
        self.nops += 1
        if signal and self.cnt[e] >= 28000:
            self._new_own(e)

    def dma(self, q, fn, reads=(), writes=()):
        slot = self.dma_rr[q]
        self.dma_rr[q] = (slot + 1) % len(self.dma_sems[q])
        s = self.dma_sems[q][slot]
        prev = self.dma_val[s]
        self._deps(q, reads, writes, extra=((s, prev),) if prev else ())
        tok = (s, prev + 16)
        self.dma_val[s] = prev + 16
        self.ops[q].append(("dma", fn, s))
        self._mark(tok, reads, writes)
        self.nops += 1

    def barrier(self):
        toks = [(s, v) for s, v in self.sig.items() if v > 0]
        for s, v in self.dma_val.items():
            if v > 0:
                toks.append((s, v))
        for e in self.ENG:
            seen = self.seen[e]
            for s, v in toks:
                if s == self.own.get(e):
                    continue
                if seen.get(s, 0) >= v:
                    continue
                seen[s] = v
                self.ops[e].append(("wait", s, v))

    def _replay(self, e, eng):
        for item in self.ops[e]:
            if item[0] == "wait":
                eng.wait_ge(self.sems[item[1]], item[2])
            elif item[0] == "op":
                ins = item[1](eng)
                if item[2] is not None:
                    ins.then_inc(self.sems[item[2]], 1)
            else:
                item[1](eng).then_inc(self.sems[item[2]], 16)
        self.ops[e] = []

    def emit(self):
        with self.nc.Block() as block:
            @block.tensor
            def _(eng):
                self._replay("pe", eng)

            @block.scalar
            def _(eng):
                self._replay("act", eng)

            @block.vector
            def _(eng):
                self._replay("dve", eng)

            @block.gpsimd
            def _(eng):
                self._replay("pool", eng)

            @block.sync
            def _(eng):
                self._replay("sp", eng)


class Ring:
    def __init__(self, nc, st, name, n, shape, dt):
        self.t = [st.enter_context(nc.sbuf_tensor(_u(f"{name}{i}"), shape, dt)) for i in range(n)]
        self.b = [Buf(f"{name}{i}") for i in range(n)]
        self.i = 0
        self.n = n

    def next(self):
        i = self.i
        self.i = (i + 1) % self.n
        return self.t[i], self.b[i]


class G:
    pass


def build(S=4096, depth=DEPTH, stop_mixer=False, dbg=False):
    NT = S // T
    nc = bass.Bass("TRN2", target_bir_lowering=False)
    g = G()
    g.nc = nc
    g.S, g.NT = S, NT
    dt = nc.dram_tensor
    xT = dt("xT", [D, S], F32, kind="ExternalInput").ap()
    vecs = dt("vecs", [128, NV], F32, kind="ExternalInput").ap()
    cst = dt("cst", [128, 256], F32, kind="ExternalInput").ap()
    rope = dt("rope", [2, 128, S], F32, kind="ExternalInput").ap()
    mod_w = dt("mod_w", [8, D, 3 * D], F32, kind="ExternalInput").ap()
    conv_w_in = dt("conv_w_in", [2, D, 2 * D], F32, kind="ExternalInput").ap()
    conv_w_out = dt("conv_w_out", [2, D, D], F32, kind="ExternalInput").ap()
    attn_w_qkv = dt("attn_w_qkv", [2, D, 3072], F32, kind="ExternalInput").ap()
    attn_w_out = dt("attn_w_out", [2, D, D], F32, kind="ExternalInput").ap()
    mlp_w_in = dt("mlp_w_in", [4, D, DFF], F32, kind="ExternalInput").ap()
    mlp_w_out = dt("mlp_w_out", [4, DFF, D], F32, kind="ExternalInput").ap()
    yT = dt("yT", [D, S], F32, kind="ExternalOutput").ap()
    XS = dt("XS", [D, S], F32).ap()
    UPD = dt("UPD", [D, S + 2 * UP], BF16, kind="ExternalOutput" if dbg else "Internal").ap()
    VC = dt("VC", [D, S], F32, kind="ExternalOutput" if dbg else "Internal").ap()
    n_conv = (depth + 1) // 2
    n_attn = depth // 2
    WB = {}
    for j in range(n_conv):
        WB["cin", j] = dt(f"wb_cin{j}", [8, 128, KC * 512], BF16).ap()
        WB["cout", j] = dt(f"wb_cout{j}", [4, 128, KC * 512], BF16).ap()
    for j in range(n_attn):
        WB["qkv", j] = dt(f"wb_qkv{j}", [6, 128, KC * 512], BF16).ap()
        WB["aout", j] = dt(f"wb_aout{j}", [4, 128, KC * 512], BF16).ap()
    for i in range(depth):
        WB["min", i] = dt(f"wb_min{i}", [16, 128, KC * 512], BF16).ap()
        WB["mout", i] = dt(f"wb_mout{i}", [16, 128, 32 * 256], BF16).ap()

    def xtile(ap, j):
        return ap.rearrange("(kc p) s -> p kc s", p=128)[:, :, j * T:(j + 1) * T]

    with ExitStack() as st:
        P = Prog(nc, st)
        g.P = P
        vec = st.enter_context(nc.sbuf_tensor(_u("vec"), [128, NV], F32))
        cst32 = st.enter_context(nc.sbuf_tensor(_u("cst32"), [128, 256], F32))
        ident_bf = st.enter_context(nc.sbuf_tensor(_u("ident_bf"), [128, 128], BF16))
        ones_bf = st.enter_context(nc.sbuf_tensor(_u("ones_bf"), [128, 128], BF16))
        cact = st.enter_context(nc.sbuf_tensor(_u("cact"), [128, 16], F32))
        modv = st.enter_context(nc.sbuf_tensor(_u("modv"), [128, 8, 48], F32))
        gqk = st.enter_context(nc.sbuf_tensor(_u("gqk"), [128, 4], F32))
        Bconst = Buf("const")
        Bmodv = Buf("modv")
        ps = [st.enter_context(nc.psum_tensor(f"ps{i}", [128, 512], F32)) for i in range(8)]
        psb = [Buf(f"ps{i}") for i in range(8)]
        rotT = cst32[:, 128:256]

        P.dma("sp", lambda e: e.dma_start(out=vec[:], in_=vecs), writes=[Bconst])
        P.dma("sp", lambda e: e.dma_start(out=cst32[:], in_=cst), writes=[Bconst])
        P.op("dve", lambda e: e.tensor_copy(out=ident_bf[:], in_=cst32[:, 0:128]), reads=[Bconst], writes=[Bconst])
        P.op("dve", lambda e: e.memset(ones_bf[:], 1.0), writes=[Bconst])
        P.op("act", lambda e: e.activation(out=cact[:], in_=vec[:, V_C:V_C + 16], func=AF.Silu), reads=[Bconst], writes=[Bconst])
        P.op("dve", lambda e: e.tensor_scalar(out=gqk[:], in0=vec[:, V_QN:V_QN + 4], scalar1=float(128.0 ** 0.5), scalar2=None, op0=ALU.mult),
             reads=[Bconst], writes=[Bconst])

        ones32 = st.enter_context(nc.sbuf_tensor(_u("ones32"), [128, 1], F32))
        P.op("dve", lambda e: e.memset(ones32[:], 1.0), writes=[Bconst])
        MODB = 7

        def cast_k2048(w, wb, nblk):
            wv = w.rearrange("(kc p) n -> p kc n", p=128)
            return [(wb[nb].rearrange("p (kc n) -> p kc n", kc=KC), wv[:, :, nb * 512:(nb + 1) * 512]) for nb in range(nblk)]

        def cast_mout(w, wb):
            wv = w.rearrange("(kc p) n -> p kc n", p=128)
            out = []
            for nb2 in range(8):
                for kh in range(2):
                    out.append((wb[nb2 * 2 + kh].rearrange("p (kc n) -> p kc n", kc=32),
                                wv[:, kh * 32:(kh + 1) * 32, nb2 * 256:(nb2 + 1) * 256]))
            return out

        def layer_casts(i):
            j = i // 2
            if i % 2 == 0:
                return cast_k2048(conv_w_in[j], WB["cin", j], 8) + cast_k2048(conv_w_out[j], WB["cout", j], 4)
            return cast_k2048(attn_w_qkv[j], WB["qkv", j], 6) + cast_k2048(attn_w_out[j], WB["aout", j], 4)

        def mlp_casts(i):
            return cast_k2048(mlp_w_in[i], WB["min", i], 16) + cast_mout(mlp_w_out[i], WB["mout", i])

        def emit_cast(c):
            dst, srcv = c
            P.dma("pool", lambda e: e.dma_start(out=dst, in_=srcv))

        bg_cast = []
        cast_mark = {}
        for i in range(depth):
            if i > 0:
                bg_cast += layer_casts(i)
            cast_mark["layer", i] = len(bg_cast)
            bg_cast += mlp_casts(i)
            cast_mark["mlp", i] = len(bg_cast)
        cast_mark["layer", depth] = len(bg_cast)
        bg_cast_pos = [0]

        def bg_cast_until(idx):
            bg_cast_run(max(0, idx - bg_cast_pos[0]))

        def bg_cast_run(n):
            for _ in range(n):
                if bg_cast_pos[0] < len(bg_cast):
                    emit_cast(bg_cast[bg_cast_pos[0]])
                    bg_cast_pos[0] += 1

        def mod_task(ls, nb, stgr, accr, q, eng, maccA):
            tl, tb = stgr.next()
            acc, accb = maccA.next()
            wv = mod_w[ls].rearrange("(kc p) n -> p kc n", p=128)
            P.dma(q, lambda e: e.dma_start(out=tl[:], in_=wv[:, :, nb * 256:(nb + 1) * 256]), writes=[tb])
            P.op(eng, lambda e: e.tensor_scalar(out=acc[:], in0=tl[:, 0, :], scalar1=cact[:, 0:1], scalar2=None, op0=ALU.mult),
                 reads=[tb, Bconst], writes=[accb])
            for kc in range(1, KC):
                if eng == "dve":
                    P.op(eng, lambda e, kc=kc: e.scalar_tensor_tensor(out=acc[:], in0=tl[:, kc, :], scalar=cact[:, kc:kc + 1], in1=acc[:],
                                                                     op0=ALU.mult, op1=ALU.add),
                         reads=[tb, accb, Bconst], writes=[accb])
                else:
                    tm, tmb = accr.next()
                    P.op(eng, lambda e, kc=kc, tm=tm: e.tensor_scalar(out=tm[:], in0=tl[:, kc, :], scalar1=cact[:, kc:kc + 1], scalar2=None, op0=ALU.mult),
                         reads=[tb, Bconst], writes=[tmb])
                    P.op(eng, lambda e, tm=tm: e.tensor_tensor(out=acc[:], in0=acc[:], in1=tm[:], op=ALU.add),
                         reads=[tmb, accb], writes=[accb])
            def pe_part():
                for n4 in range(2):
                    col = ls * 48 + nb * 2 + n4
                    P.op("pe", lambda e, n4=n4, col=col: e.matmul(ps[MODB][:, col:col + 1], acc[:, n4 * 128:(n4 + 1) * 128], ones32[:, 0:1],
                                                                  start=True, stop=True),
                         reads=[accb, Bconst], writes=[psb[MODB]])
                if nb == 23:
                    fin_part()

            def fin_part():
                P.op("dve", lambda e: e.tensor_tensor(out=modv[:, ls, :], in0=ps[MODB][:, ls * 48:(ls + 1) * 48],
                                                      in1=vec[:, V_MODB + ls * 48:V_MODB + (ls + 1) * 48], op=ALU.add),
                     reads=[psb[MODB], Bconst], writes=[Bmodv])
                P.op("dve", lambda e: e.tensor_scalar(out=modv[:, ls, 16:32], in0=modv[:, ls, 16:32], scalar1=1.0, scalar2=None, op0=ALU.add),
                     reads=[Bmodv], writes=[Bmodv])
                P.op("dve", lambda e: e.tensor_scalar(out=modv[:, ls, 32:48], in0=modv[:, ls, 32:48], scalar1=1.0, scalar2=float(1.0 / ALPHA),
                                                      op0=ALU.add, op1=ALU.mult),
                     reads=[Bmodv], writes=[Bmodv])

            mod_pend.append(pe_part)
            while len(mod_pend) > 3:
                mod_pend.pop(0)()

        mod_pend = []

        def mod_flush():
            while mod_pend:
                mod_pend.pop(0)()

        bg_mod = []
        bg_mod_pos = [0]

        def bg_mod_run(n):
            for _ in range(n):
                if bg_mod_pos[0] < len(bg_mod):
                    bg_mod[bg_mod_pos[0]]()
                    bg_mod_pos[0] += 1

        def mv(ls, part, n):
            return modv[:, ls, part * 16 + n:part * 16 + n + 1]

        def vcol(base, n):
            return vec[:, base + n:base + n + 1]

        def make_h(src, j, ls, h, hb, stg):
            sv = xtile(src, j)
            for kc2 in range(0, KC, 2):
                tl, tb = stg.next()
                P.dma("sp", lambda e, tl=tl, kc2=kc2: e.dma_start(out=tl[:], in_=sv[:, kc2:kc2 + 2, :]), writes=[tb])
                for q in range(2):
                    kc = kc2 + q
                    P.op("dve", lambda e, tl=tl, q=q, kc=kc: e.tensor_scalar(
                        out=h[:, kc, :], in0=tl[:, q, :], scalar1=mv(ls, 1, kc), scalar2=mv(ls, 0, kc),
                        op0=ALU.mult, op1=ALU.add), reads=[tb, Bmodv], writes=[hb[kc]])

        class Epi:
            def __init__(self, X, Xb, ls, tmpr, statr, s1, s2):
                self.X, self.Xb, self.ls = X, Xb, ls
                self.tmpr, self.statr = tmpr, statr
                self.s1, self.s2 = s1, s2

            def chunk(self, n, pt, ptb):
                X, Xb, ls = self.X, self.Xb, self.ls
                P.op("dve", lambda e: e.scalar_tensor_tensor(out=X[:, n, :], in0=pt[:], scalar=mv(ls, 2, n), in1=X[:, n, :],
                                                             op0=ALU.mult, op1=ALU.add),
                     reads=[ptb, Xb[n], Bmodv], writes=[Xb[n]])
                zs, zsb = self.tmpr.next()
                zb, zbb = self.tmpr.next()
                P.op("act", lambda e: e.activation(out=zs[:], in_=X[:, n, :], func=AF.Square), reads=[Xb[n]], writes=[zsb])
                P.op("dve", lambda e: e.tensor_copy(out=zb[:], in_=X[:, n, :]), reads=[Xb[n]], writes=[zbb])
                P.op("pe", lambda e: e.matmul(ps[self.s1][:], ones_bf[:], zb[:], start=(n == 0), stop=(n == KC - 1)),
                     reads=[zbb, Bconst], writes=[psb[self.s1]], signal=True)
                P.op("pe", lambda e: e.matmul(ps[self.s2][:], ones_bf[:], zs[:], start=(n == 0), stop=(n == KC - 1)),
                     reads=[zsb, Bconst], writes=[psb[self.s2]], signal=True)

            def stats(self, eps):
                mean, meanb = self.statr.next()
                rstd, rstdb = self.statr.next()
                P.op("dve", lambda e: e.tensor_scalar(out=mean[:], in0=ps[self.s1][:], scalar1=float(1.0 / D), scalar2=None, op0=ALU.mult),
                     reads=[psb[self.s1]], writes=[meanb])
                P.op("dve", lambda e: e.tensor_tensor(out=rstd[:], in0=mean[:], in1=mean[:], op=ALU.mult), reads=[meanb], writes=[rstdb])
                P.op("dve", lambda e: e.scalar_tensor_tensor(out=rstd[:], in0=ps[self.s2][:], scalar=float(1.0 / D), in1=rstd[:],
                                                             op0=ALU.mult, op1=ALU.subtract),
                     reads=[psb[self.s2], rstdb], writes=[rstdb])
                P.op("act", lambda e: e.activation(out=rstd[:], in_=rstd[:], func=AF.Sqrt, bias=float(eps), scale=1.0),
                     reads=[rstdb], writes=[rstdb])
                P.op("dve", lambda e: e.reciprocal(out=rstd[:], in_=rstd[:]), reads=[rstdb], writes=[rstdb])
                return mean, meanb, rstd, rstdb

            def finish(self, gbase, bbase):
                X, Xb = self.X, self.Xb
                mean, meanb, rstd, rstdb = self.stats(LN_EPS / (ALPHA * ALPHA))
                for n in range(KC):
                    P.op("dve", lambda e, n=n: e.tensor_tensor(out=X[:, n, :], in0=X[:, n, :], in1=mean[:], op=ALU.subtract),
                         reads=[Xb[n], meanb], writes=[Xb[n]])
                    P.op("dve", lambda e, n=n: e.tensor_tensor(out=X[:, n, :], in0=X[:, n, :], in1=rstd[:], op=ALU.mult),
                         reads=[Xb[n], rstdb], writes=[Xb[n]])
                    P.op("act", lambda e, n=n: e.activation(out=X[:, n, :], in_=X[:, n, :], func=AF.Identity,
                                                            scale=vcol(gbase, n), bias=vcol(bbase, n)),
                         reads=[Xb[n], Bconst], writes=[Xb[n]])

        def stream(steps, ring, depth_pf=None):
            n = len(steps)
            pf = ring.n - 1 if depth_pf is None else depth_pf
            loaded = {}

            def load(i):
                ap = steps[i][0]
                if ap is None:
                    loaded[i] = (None, None)
                    return
                tl, tb = ring.next()
                P.dma("sp", lambda e: e.dma_start(out=tl[:].rearrange("p a b -> p (a b)"), in_=ap), writes=[tb])
                loaded[i] = (tl, tb)

            for i in range(min(pf, n)):
                load(i)
            for i in range(n):
                if i + pf < n:
                    load(i + pf)
                tl, tb = loaded.pop(i)
                steps[i][1](tl, tb)

        def store_x(X, Xb, dst, j):
            P.dma("sp", lambda e: e.dma_start(out=xtile(dst, j), in_=X[:]), reads=list(Xb))

        def load_x(X, Xb, src, j):
            P.dma("sp", lambda e: e.dma_start(out=X[:], in_=xtile(src, j)), writes=list(Xb))

        def mlp_pass(i, src, dst):
            ls = 2 * i + 1
            with ExitStack() as ph:
                X = ph.enter_context(nc.sbuf_tensor(_u("mX"), [128, KC, T], F32))
                Xb = [Buf(f"mX{n}") for n in range(KC)]
                h = ph.enter_context(nc.sbuf_tensor(_u("mh"), [128, KC, T], BF16))
                hb = [Buf(f"mh{n}") for n in range(KC)]
                uT = ph.enter_context(nc.sbuf_tensor(_u("muT"), [128, 64, T], BF16))
                ub = [Buf(f"mu{n}") for n in range(64)]
                stg = Ring(nc, ph, "mstg", 2, [128, 2, T], F32)
                wr = Ring(nc, ph, "mw", 4, [128, KC, 512], BF16)
                tmpr = Ring(nc, ph, "mtmp", 4, [128, T], BF16)
                rr = Ring(nc, ph, "mrl", 3, [128, T], BF16)
                statr = Ring(nc, ph, "mstat", 2, [128, T], F32)
                steps = []
                pin = [2, 3, 6, 7]
                pin_i = [0]
                make_h(src, 0, ls, h, hb, stg)
                for j in range(NT):
                    epi = Epi(X, Xb, ls, tmpr, statr, 0, 1)
                    for nb in range(16):
                        def c_in(tl, tb, nb=nb):
                            bg_cast_run(1)
                            for hc in range(4):
                                pi = pin[pin_i[0] % 4]
                                pin_i[0] += 1
                                for kc in range(KC):
                                    P.op("pe", lambda e, kc=kc, hc=hc, pi=pi: e.matmul(
                                        ps[pi][:], tl[:, kc, hc * 128:(hc + 1) * 128], h[:, kc, :], start=(kc == 0), stop=(kc == KC - 1)),
                                        reads=[tb, hb[kc]], writes=[psb[pi]], signal=(kc == KC - 1))
                                r, rb = rr.next()
                                P.op("act", lambda e, pi=pi, r=r: e.activation(out=r[:], in_=ps[pi][:], func=AF.Relu),
                                     reads=[psb[pi]], writes=[rb])
                                u = nb * 4 + hc
                                eng = "dve"
                                P.op(eng, lambda e, r=r, u=u: e.tensor_tensor(out=uT[:, u, :], in0=r[:], in1=r[:], op=ALU.mult),
                                     reads=[rb], writes=[ub[u]])
                        steps.append((WB["min", i][nb], c_in))
                    for nb2 in range(8):
                        for kh in range(2):
                            def c_out(tl, tb, nb2=nb2, kh=kh, j=j, epi=epi):
                                if nb2 == 0 and kh == 0:
                                    load_x(X, Xb, src, j)
                                    if j + 1 < NT:
                                        make_h(src, j + 1, ls, h, hb, stg)
                                wv = tl[:].rearrange("p a b -> p (a b)").rearrange("p (kc n) -> p kc n", kc=32)
                                for half in range(2):
                                    pi = 4 + half
                                    for kk in range(32):
                                        kc = kh * 32 + kk
                                        P.op("pe", lambda e, kk=kk, kc=kc, half=half, pi=pi: e.matmul(
                                            ps[pi][:], wv[:, kk, half * 128:(half + 1) * 128], uT[:, kc, :],
                                            start=(kc == 0), stop=(kc == 63)),
                                            reads=[tb, ub[kc]], writes=[psb[pi]], signal=(kk == 31))
                                if kh == 1:
                                    for half in range(2):
                                        epi.chunk(nb2 * 2 + half, ps[4 + half], psb[4 + half])
                                if nb2 == 7 and kh == 1:
                                    epi.finish(V_LNG + ls * 16, V_LNB + ls * 16)
                                    store_x(X, Xb, dst, j)
                            steps.append((WB["mout", i][nb2 * 2 + kh], c_out))
                stream(steps, wr)
                bg_cast_until(cast_mark["layer", i + 1])
                P.barrier()
                P.emit()

        def conv_layer(i, src, dst):
            j_ = i // 2
            ls = 2 * i
            with ExitStack() as ph:
                h = ph.enter_context(nc.sbuf_tensor(_u("ch"), [128, KC, T], BF16))
                hb = [Buf(f"ch{n}") for n in range(KC)]
                stg = Ring(nc, ph, "cstg", 2, [128, 2, T], F32)
                wr = Ring(nc, ph, "cw", 4, [128, KC, 512], BF16)
                sgr = Ring(nc, ph, "csg", 3, [128, T], F32)
                ur = Ring(nc, ph, "cu", 2, [128, KC, T], BF16)
                pin_i = [0]
                steps = []
                make_h(src, 0, ls, h, hb, stg)
                for j in range(NT):
                    u, ubuf = ur.next()
                    for nb in range(4):
                        hold = {}

                        def c_a(tl, tb, hold=hold):
                            hold["a"] = (tl, tb)
                            bg_cast_run(1)
                            bg_mod_run(1)

                        def c_g(tl, tb, nb=nb, hold=hold, u=u, ubuf=ubuf, j=j):
                            bg_mod_run(1)
                            ta, tab = hold["a"]
                            for hc in range(4):
                                n = nb * 4 + hc
                                pa = (pin_i[0] % 3) * 2
                                pin_i[0] += 1
                                pg = pa + 1
                                for (wt, wtb, pi) in ((ta, tab, pa), (tl, tb, pg)):
                                    for kc in range(KC):
                                        P.op("pe", lambda e, kc=kc, hc=hc, pi=pi, wt=wt: e.matmul(
                                            ps[pi][:], wt[:, kc, hc * 128:(hc + 1) * 128], h[:, kc, :], start=(kc == 0), stop=(kc == KC - 1)),
                                            reads=[wtb, hb[kc]], writes=[psb[pi]], signal=(kc == KC - 1))
                                sg, sgb = sgr.next()
                                P.op("act", lambda e, pg=pg, sg=sg, n=n: e.activation(out=sg[:], in_=ps[pg][:], func=AF.Sigmoid,
                                                                                   bias=vcol(V_CBIN + j_ * 32 + 16, n), scale=1.0),
                                     reads=[psb[pg], Bconst], writes=[sgb])
                                P.op("dve", lambda e, pa=pa, sg=sg, n=n, u=u: e.scalar_tensor_tensor(
                                    out=u[:, n, :], in0=ps[pa][:], scalar=vcol(V_CBIN + j_ * 32, n), in1=sg[:], op0=ALU.add, op1=ALU.mult),
                                    reads=[psb[pa], sgb, Bconst], writes=[ubuf])
                            if nb == 3:
                                P.dma("sp", lambda e, u=u, j=j: e.dma_start(
                                    out=UPD.rearrange("(kc p) s -> p kc s", p=128)[:, :, UP + j * T:UP + (j + 1) * T], in_=u[:]), reads=[ubuf])
                                if j + 1 < NT:
                                    make_h(src, j + 1, ls, h, hb, stg)
                        steps.append((WB["cin", j_][nb], c_a))
                        steps.append((WB["cin", j_][4 + nb], c_g))
                stream(steps, wr, depth_pf=2)
                bg_mod_run(max(0, 24 - bg_mod_pos[0]))
                mod_flush()
                P.barrier()
                P.emit()
            with ExitStack() as ph:
                urow = Ring(nc, ph, "c2u", 2, [128, S + 2 * UP], BF16)
                vrow = Ring(nc, ph, "c2v", 2, [128, S], F32)
                dgr = Ring(nc, ph, "c2d", 2, [128, CW, 128], BF16)
                pin_i = [0]
                def pre(c):
                    ur_, urb = urow.next()
                    dg, dgb = dgr.next()
                    P.dma("sp", lambda e: e.dma_start(out=ur_[:], in_=UPD[c * 128:(c + 1) * 128, :]), writes=[urb])
                    for tp in range(CW):
                        P.op("act", lambda e, tp=tp: e.activation(
                            out=dg[:, tp, :], in_=ident_bf[:], func=AF.Identity, scale=vcol(V_CDW + j_ * CW * 16 + tp * 16, c)),
                            reads=[Bconst], writes=[dgb])
                    return ur_, urb, dg, dgb

                nxt = pre(0)
                for c in range(KC):
                    ur_, urb, dg, dgb = nxt
                    if c + 1 < KC:
                        nxt = pre(c + 1)
                    vr_, vrb = vrow.next()
                    bg_cast_run(3)
                    bg_mod_run(3)
                    for j in range(NT):
                        pi = pin_i[0] % 7
                        pin_i[0] += 1
                        for tp in range(CW):
                            o = UP + j * T + tp - (CW // 2)
                            P.op("pe", lambda e, dg=dg, tp=tp, o=o, pi=pi, ur_=ur_: e.matmul(
                                ps[pi][:], dg[:, tp, :], ur_[:, o:o + T], start=(tp == 0), stop=(tp == CW - 1)),
                                reads=[dgb, urb], writes=[psb[pi]], signal=(tp == CW - 1))
                        eng = "act" if j % 2 == 0 else "dve"
                        if eng == "act":
                            P.op("act", lambda e, pi=pi, vr_=vr_, j=j, c=c: e.activation(
                                out=vr_[:, j * T:(j + 1) * T], in_=ps[pi][:], func=AF.Identity, bias=vcol(V_CDWB + j_ * 16, c), scale=1.0),
                                reads=[psb[pi], Bconst], writes=[vrb])
                        else:
                            P.op("dve", lambda e, pi=pi, vr_=vr_, j=j, c=c: e.tensor_scalar(
                                out=vr_[:, j * T:(j + 1) * T], in0=ps[pi][:], scalar1=vcol(V_CDWB + j_ * 16, c), scalar2=None, op0=ALU.add),
                                reads=[psb[pi], Bconst], writes=[vrb])
                    P.dma("sp", lambda e, vr_=vr_, c=c: e.dma_start(out=VC[c * 128:(c + 1) * 128, :], in_=vr_[:]), reads=[vrb])
                mod_flush()
                P.barrier()
                P.emit()
            with ExitStack() as ph:
                Xr = [(ph.enter_context(nc.sbuf_tensor(_u("cX"), [128, KC, T], F32)), [Buf(f"cX{q}_{n}") for n in range(KC)]) for q in range(2)]
                sTr = Ring(nc, ph, "csT", 2, [128, KC, T], BF16)
                vstg = Ring(nc, ph, "cvst", 3, [128, 2, T], F32)
                wr = Ring(nc, ph, "c3w", 2, [128, KC, 512], BF16)
                tmpr = Ring(nc, ph, "ctmp", 4, [128, T], BF16)
                statr = Ring(nc, ph, "cstat", 4, [128, T], F32)
                steps = []
                pin_i = [0]
                vt = xtile(VC, 0)
                Bdummy = Buf("vdummy")

                def prep1(j):
                    sv = xtile(VC, j)
                    for k2 in range(0, KC, 2):
                        tl, tb = vstg.next()
                        P.dma("sp", lambda e, tl=tl, k2=k2: e.dma_start(out=tl[:], in_=sv[:, k2:k2 + 2, :]), writes=[tb])
                        for q in range(2):
                            n = k2 + q
                            zs, zsb = tmpr.next()
                            zb, zbb = tmpr.next()
                            P.op("act", lambda e, tl=tl, q=q, zs=zs: e.activation(out=zs[:], in_=tl[:, q, :], func=AF.Square), reads=[tb], writes=[zsb])
                            P.op("dve", lambda e, tl=tl, q=q, zb=zb: e.tensor_copy(out=zb[:], in_=tl[:, q, :]), reads=[tb], writes=[zbb])
                            P.op("pe", lambda e, n=n, zb=zb: e.matmul(ps[2][:], ones_bf[:], zb[:], start=(n == 0), stop=(n == KC - 1)),
                                 reads=[zbb, Bconst], writes=[psb[2]])
                            P.op("pe", lambda e, n=n, zs=zs: e.matmul(ps[3][:], ones_bf[:], zs[:], start=(n == 0), stop=(n == KC - 1)),
                                 reads=[zsb, Bconst], writes=[psb[3]])

                def prep2(j, sT, sb):
                    vepi = Epi(None, None, ls, tmpr, statr, 2, 3)
                    mean, meanb, rstd, rstdb = vepi.stats(LN_EPS)
                    sv = xtile(VC, j)
                    for k2 in range(0, KC, 2):
                        tl, tb = vstg.next()
                        P.dma("sp", lambda e, tl=tl, k2=k2: e.dma_start(out=tl[:], in_=sv[:, k2:k2 + 2, :]), writes=[tb])
                        for q in range(2):
                            n = k2 + q
                            P.op("dve", lambda e, tl=tl, q=q: e.tensor_tensor(out=tl[:, q, :], in0=tl[:, q, :], in1=mean[:], op=ALU.subtract),
                                 reads=[tb, meanb], writes=[tb])
                            P.op("dve", lambda e, tl=tl, q=q: e.tensor_tensor(out=tl[:, q, :], in0=tl[:, q, :], in1=rstd[:], op=ALU.mult),
                                 reads=[tb, rstdb], writes=[tb])
                            P.op("act", lambda e, tl=tl, q=q, n=n: e.activation(out=sT[:, n, :], in_=tl[:, q, :], func=AF.Silu,
                                                                               scale=vcol(V_CLNG + j_ * 16, n), bias=vcol(V_CLNB + j_ * 16, n)),
                                 reads=[tb, Bconst], writes=[sb[n]])

                sTb = {}
                for j in range(NT):
                    sTb[j] = sTr.next() + ([Buf(f"cs{j}_{n}") for n in range(KC)],)
                slot_bufs = {}
                for j in range(NT):
                    slot_bufs.setdefault(j % 2, sTb[j][2])
                    sTb[j] = (sTb[j][0], sTb[j][1], slot_bufs[j % 2])
                prep1(0)
                prep2(0, sTb[0][0], sTb[0][2])
                load_x(Xr[0][0], Xr[0][1], src, 0)
                for j in range(NT):
                    X, Xb = Xr[j % 2]
                    epi = Epi(X, Xb, ls, tmpr, statr, 0, 1)
                    sT, _, sb = sTb[j]
                    for nb in range(4):
                        def c_o(tl, tb, nb=nb, j=j, epi=epi, sT=sT, sb=sb, X=X, Xb=Xb):
                            bg_cast_run(1)
                            if nb == 0 and j + 1 < NT:
                                load_x(Xr[(j + 1) % 2][0], Xr[(j + 1) % 2][1], src, j + 1)
                            if nb == 1 and j + 1 < NT:
                                prep1(j + 1)
                            if nb == 2 and j + 1 < NT:
                                prep2(j + 1, sTb[j + 1][0], sTb[j + 1][2])
                            for hc in range(4):
                                n = nb * 4 + hc
                                pi = 4 + (pin_i[0] % 3)
                                pin_i[0] += 1
                                for kc in range(KC):
                                    P.op("pe", lambda e, kc=kc, hc=hc, pi=pi: e.matmul(
                                        ps[pi][:], tl[:, kc, hc * 128:(hc + 1) * 128], sT[:, kc, :], start=(kc == 0), stop=(kc == KC - 1)),
                                        reads=[tb, sb[kc]], writes=[psb[pi]], signal=(kc == KC - 1))
                                epi.chunk(n, ps[pi], psb[pi])
                            if nb == 3:
                                epi.finish(V_LNG + ls * 16, V_LNB + ls * 16)
                                store_x(X, Xb, dst, j)
                        steps.append((WB["cout", j_][nb], c_o))
                stream(steps, wr)
                bg_mod_run(10 ** 6)
                mod_flush()
                bg_cast_until(cast_mark["mlp", i])
                P.barrier()
                P.emit()

        def attn_layer(i, src, dst):
            j_ = i // 2
            ls = 2 * i
            NKC = S // 128
            with ExitStack() as al:
                KT = al.enter_context(nc.sbuf_tensor(_u("KT"), [128, 4, S], BF16))
                Vs = al.enter_context(nc.sbuf_tensor(_u("Vs"), [128, NKC, 512], BF16))
                KTb = [[Buf(f"KT{kv}_{j}") for j in range(NT)] for kv in range(4)]
                Vsb = [Buf(f"Vs{c}") for c in range(NKC)]
                rope_rings = [None, None]
                gen_i = [0]

                gen_banks = [[4, 5, 6, 7]]

                def gbank():
                    gb = gen_banks[0]
                    pi = gb[gen_i[0] % len(gb)]
                    gen_i[0] += 1
                    return pi

                def load_rope(j):
                    ct, cb = rope_rings[0].next()
                    sn, snb = rope_rings[1].next()
                    P.dma("sp", lambda e: e.dma_start(out=ct[:], in_=rope[0][:, j * T:(j + 1) * T]), writes=[cb])
                    P.dma("sp", lambda e: e.dma_start(out=sn[:], in_=rope[1][:, j * T:(j + 1) * T]), writes=[snb])
                    return ct, cb, sn, snb

                def norm_rope_stages(proj_fn, gcol, rp, out_ap, out_bufs, sqr, f32r):
                    ct, cb, sn, snb = rp
                    stt = {}

                    def A():
                        pi = gbank()
                        proj_fn(pi)
                        sq, sqb = sqr.next()
                        P.op("act", lambda e: e.activation(out=sq[:], in_=ps[pi][:], func=AF.Square), reads=[psb[pi]], writes=[sqb])
                        stt["pi"], stt["sq"], stt["sqb"] = pi, sq, sqb

                    def B():
                        pi, sq, sqb = stt["pi"], stt["sq"], stt["sqb"]
                        p2 = gbank()
                        P.op("pe", lambda e: e.matmul(ps[p2][:], ones_bf[:], sq[:], start=True, stop=True),
                             reads=[sqb, Bconst], writes=[psb[p2]])
                        rs, rsb = f32r.next()
                        P.op("act", lambda e: e.activation(out=rs[:], in_=ps[p2][:], func=AF.Ln, bias=float(128.0 * RMS_EPS), scale=1.0),
                             reads=[psb[p2]], writes=[rsb])
                        P.op("act", lambda e: e.activation(out=rs[:], in_=rs[:], func=AF.Exp, scale=-0.5), reads=[rsb], writes=[rsb])
                        qn, qnb = f32r.next()
                        P.op("dve", lambda e: e.scalar_tensor_tensor(out=qn[:], in0=ps[pi][:], scalar=gcol, in1=rs[:], op0=ALU.mult, op1=ALU.mult),
                             reads=[psb[pi], rsb, Bconst], writes=[qnb])
                        stt["qn"], stt["qnb"], stt["rs"], stt["rsb"] = qn, qnb, rs, rsb

                    def C():
                        qn, qnb, rs, rsb = stt["qn"], stt["qnb"], stt["rs"], stt["rsb"]
                        p3 = gbank()
                        P.op("pe", lambda e: e.matmul(ps[p3][:], rotT, qn[:], start=True, stop=True),
                             reads=[qnb, Bconst], writes=[psb[p3]])
                        P.op("dve", lambda e: e.tensor_tensor(out=rs[:], in0=ps[p3][:], in1=sn[:], op=ALU.mult),
                             reads=[psb[p3], snb], writes=[rsb])
                        P.op("dve", lambda e: e.tensor_tensor(out=qn[:], in0=qn[:], in1=ct[:], op=ALU.mult),
                             reads=[qnb, cb], writes=[qnb])
                        P.op("dve", lambda e: e.tensor_tensor(out=out_ap, in0=qn[:], in1=rs[:], op=ALU.add),
                             reads=[qnb, rsb], writes=out_bufs)

                    return [A, B, C]

                with ExitStack() as ph:
                    h = ph.enter_context(nc.sbuf_tensor(_u("ah"), [128, KC, T], BF16))
                    hb = [Buf(f"ah{n}") for n in range(KC)]
                    stg = Ring(nc, ph, "astg", 2, [128, 2, T], F32)
                    rope_rings[0] = Ring(nc, ph, "cosr1", 2, [128, T], F32)
                    rope_rings[1] = Ring(nc, ph, "sinr1", 2, [128, T], F32)
                    sqr1 = Ring(nc, ph, "sqr1", 4, [128, T], BF16)
                    f32r1 = Ring(nc, ph, "f32r1", 8, [128, T], F32)
                    wk = ph.enter_context(nc.sbuf_tensor(_u("awk"), [128, KC, 512], BF16))
                    wv_ = ph.enter_context(nc.sbuf_tensor(_u("awv"), [128, KC, 512], BF16))
                    Bwk, Bwv = Buf("wk"), Buf("wv")
                    P.dma("sp", lambda e: e.dma_start(out=wk[:].rearrange("p a b -> p (a b)"), in_=WB["qkv", j_][4]), writes=[Bwk])
                    P.dma("sp", lambda e: e.dma_start(out=wv_[:].rearrange("p a b -> p (a b)"), in_=WB["qkv", j_][5]), writes=[Bwv])
                    def vproj(j, tc4):
                        pi = 6 + (tc4 % 2)
                        for kc in range(KC):
                            P.op("pe", lambda e, kc=kc, tc4=tc4, pi=pi: e.matmul(ps[pi][:], h[:, kc, tc4 * 128:(tc4 + 1) * 128], wv_[:, kc, :],
                                                                                  start=(kc == 0), stop=(kc == KC - 1)),
                                 reads=[Bwv, hb[kc]], writes=[psb[pi]], signal=(kc == KC - 1))
                        c = j * 4 + tc4
                        if tc4 % 2 == 0:
                            P.op("act", lambda e, c=c, pi=pi: e.activation(out=Vs[:, c, :], in_=ps[pi][:], func=AF.Copy), reads=[psb[pi]], writes=[Vsb[c]])
                        else:
                            P.op("dve", lambda e, c=c, pi=pi: e.tensor_copy(out=Vs[:, c, :], in_=ps[pi][:]), reads=[psb[pi]], writes=[Vsb[c]])

                    for j in range(NT):
                        make_h(src, j, ls, h, hb, stg)
                        rp = load_rope(j)
                        stgs = []
                        for kv in range(4):
                            def proj(pi, kv=kv):
                                for kc in range(KC):
                                    P.op("pe", lambda e, kc=kc: e.matmul(ps[pi][:], wk[:, kc, kv * 128:(kv + 1) * 128], h[:, kc, :],
                                                                          start=(kc == 0), stop=(kc == KC - 1)),
                                         reads=[Bwk, hb[kc]], writes=[psb[pi]], signal=(kc == KC - 1))
                            stgs.append(norm_rope_stages(proj, gqk[:, 2 + j_:3 + j_], rp, KT[:, kv, j * T:(j + 1) * T], [KTb[kv][j]], sqr1, f32r1))
                        gen_banks[0] = [0, 1, 2, 3]
                        gen_i[0] = 0
                        for kv in range(4):
                            stgs[kv][0]()
                        vproj(j, 0)
                        vproj(j, 1)
                        gen_banks[0] = [4, 5]
                        gen_i[0] = 0
                        for kv in range(4):
                            stgs[kv][1]()
                        vproj(j, 2)
                        vproj(j, 3)
                        for kv in range(4):
                            stgs[kv][2]()
                    P.barrier()
                    P.emit()

                with ExitStack() as ph:
                    X = ph.enter_context(nc.sbuf_tensor(_u("aX"), [128, KC, T], F32))
                    Xb = [Buf(f"aX{n}") for n in range(KC)]
                    h = ph.enter_context(nc.sbuf_tensor(_u("a2h"), [128, KC, T], BF16))
                    hb = [Buf(f"a2h{n}") for n in range(KC)]
                    stg = Ring(nc, ph, "a2stg", 2, [128, 1, T], F32)
                    qr = Ring(nc, ph, "a2q", 2, [128, T], BF16)
                    OT = ph.enter_context(nc.sbuf_tensor(_u("a2O"), [128, KC, T], BF16))
                    Ob = [Buf(f"a2O{n}") for n in range(KC)]
                    pr = Ring(nc, ph, "a2p", 6, [128, T], BF16)
                    wqr = Ring(nc, ph, "a2wq", 2, [128, KC, 512], BF16)
                    tmpr = Ring(nc, ph, "a2tmp", 4, [128, T], BF16)
                    statr = Ring(nc, ph, "a2stat", 2, [128, T], F32)
                    rcr = Ring(nc, ph, "a2rc", 2, [128, T], F32)
                    rope_rings[0] = Ring(nc, ph, "cosr2", 1, [128, T], F32)
                    rope_rings[1] = Ring(nc, ph, "sinr2", 1, [128, T], F32)
                    sqr2 = Ring(nc, ph, "sqr2", 2, [128, T], BF16)
                    f32r2 = Ring(nc, ph, "f32r2", 2, [128, T], F32)
                    scale = float(128.0 ** -0.5)

                    def make_h1(j):
                        sv = xtile(src, j)
                        for kc in range(KC):
                            tl, tb = stg.next()
                            P.dma("sp", lambda e, tl=tl, kc=kc: e.dma_start(out=tl[:], in_=sv[:, kc:kc + 1, :]), writes=[tb])
                            P.op("dve", lambda e, tl=tl, kc=kc: e.tensor_scalar(
                                out=h[:, kc, :], in0=tl[:, 0, :], scalar1=mv(ls, 1, kc), scalar2=mv(ls, 0, kc),
                                op0=ALU.mult, op1=ALU.add), reads=[tb, Bmodv], writes=[hb[kc]])

                    units = [(j, hd) for j in range(NT) for hd in range(16)]
                    wq_cur = {}
                    rope_cur = {}

                    def prep(u):
                        j, hd = units[u]
                        if hd == 0:
                            rope_cur[j] = load_rope(j)
                        if hd % 4 == 0:
                            tl, tb = wqr.next()
                            blk = WB["qkv", j_][hd // 4]
                            P.dma("sp", lambda e: e.dma_start(out=tl[:].rearrange("p a b -> p (a b)"), in_=blk), writes=[tb])
                            wq_cur[(j, hd // 4)] = (tl, tb)
                        tl, tb = wq_cur[(j, hd // 4)]
                        q, qb = qr.next()

                        def proj(pi):
                            hh = hd % 4
                            for kc in range(KC):
                                P.op("pe", lambda e, kc=kc: e.matmul(ps[pi][:], tl[:, kc, hh * 128:(hh + 1) * 128], h[:, kc, :],
                                                                      start=(kc == 0), stop=(kc == KC - 1)),
                                     reads=[tb, hb[kc]], writes=[psb[pi]], signal=(kc == KC - 1))
                        return norm_rope_stages(proj, gqk[:, j_:j_ + 1], rope_cur[j], q[:], [qb], sqr2, f32r2), (q, qb)

                    gen_banks[0] = [6, 7]
                    gen_i[0] = 0

                    def run_tile(j, qs):
                        us = [u for u in range(len(units)) if units[u][0] == j]
                        its = [(u, kc) for u in us for kc in range(NKC)]
                        pend = {}
                        hk = [0, max(1, NKC // 4), max(2, NKC // 2)]

                        def qk(i):
                            u, kc = its[i]
                            q, qb = qs[u]
                            kv = units[u][1] // 4
                            bk = i % 3
                            P.op("pe", lambda e: e.matmul(ps[bk][:], KT[:, kv, kc * 128:(kc + 1) * 128], q[:], start=True, stop=True),
                                 reads=[KTb[kv][kc // 4], qb], writes=[psb[bk]])
                            p, pb = pr.next()
                            P.op("act", lambda e: e.activation(out=p[:], in_=ps[bk][:], func=AF.Exp, scale=scale),
                                 reads=[psb[bk]], writes=[pb])
                            pend[i] = (p, pb)

                        for i0 in range(min(3, len(its))):
                            qk(i0)
                        for i in range(len(its)):
                            u, kc = its[i]
                            hd = units[u][1]
                            kv = hd // 4
                            OACC, SUMS = 3 + (hd % 2), 5
                            if hd < 15:
                                if kc == hk[0]:
                                    st_, qq = prep(u + 1)
                                    qs[u + 1] = qq
                                    qs[("st", u + 1)] = st_
                                    st_[0]()
                                elif kc == hk[1]:
                                    qs[("st", u + 1)][1]()
                                elif kc == hk[2]:
                                    qs[("st", u + 1)][2]()
                            elif kc == hk[0] and j + 1 < NT:
                                make_h1(j + 1)
                            p, pb = pend.pop(i)
                            P.op("pe", lambda e, kc=kc, kv=kv, p=p, OACC=OACC: e.matmul(ps[OACC][:], Vs[:, kc, kv * 128:(kv + 1) * 128], p[:],
                                                                              start=(kc == 0), stop=(kc == NKC - 1)),
                                 reads=[Vsb[kc], pb], writes=[psb[OACC]], signal=False)
                            P.op("pe", lambda e, kc=kc, p=p, SUMS=SUMS: e.matmul(ps[SUMS][:], ones_bf[:], p[:], start=(kc == 0), stop=(kc == NKC - 1)),
                                 reads=[pb, Bconst], writes=[psb[SUMS]], signal=True)
                            if i + 3 < len(its):
                                qk(i + 3)
                            if kc == NKC - 1:
                                rc, rcb = rcr.next()
                                P.op("dve", lambda e, rc=rc, SUMS=SUMS: e.tensor_copy(out=rc[:], in_=ps[SUMS][:]), reads=[psb[SUMS]], writes=[rcb])
                                P.op("dve", lambda e, rc=rc: e.reciprocal(out=rc[:], in_=rc[:]), reads=[rcb], writes=[rcb])
                                P.op("dve", lambda e, rc=rc, hd=hd, OACC=OACC: e.tensor_tensor(out=OT[:, hd, :], in0=ps[OACC][:], in1=rc[:], op=ALU.mult),
                                     reads=[psb[OACC], rcb], writes=[Ob[hd]])

                    def out_proj(j):
                        epi = Epi(X, Xb, ls, tmpr, statr, 0, 1)
                        load_x(X, Xb, src, j)
                        for nb in range(4):
                            tl, tb = wqr.next()
                            blk = WB["aout", j_][nb]
                            P.dma("sp", lambda e, tl=tl, blk=blk: e.dma_start(out=tl[:].rearrange("p a b -> p (a b)"), in_=blk), writes=[tb])
                            for hc in range(4):
                                n = nb * 4 + hc
                                pi = gbank()
                                for kc in range(KC):
                                    P.op("pe", lambda e, kc=kc, hc=hc, pi=pi, tl=tl: e.matmul(
                                        ps[pi][:], tl[:, kc, hc * 128:(hc + 1) * 128], OT[:, kc, :], start=(kc == 0), stop=(kc == KC - 1)),
                                        reads=[tb, Ob[kc]], writes=[psb[pi]], signal=(kc == KC - 1))
                                epi.chunk(n, ps[pi], psb[pi])
                        epi.finish(V_LNG + ls * 16, V_LNB + ls * 16)
                        store_x(X, Xb, dst, j)

                    make_h1(0)
                    for j in range(NT):
                        u0 = j * 16
                        st_, qq = prep(u0)
                        for s_ in st_:
                            s_()
                        qs = {u0: qq}
                        run_tile(j, qs)
                        out_proj(j)
                    bg_cast_until(cast_mark["mlp", i])
                    P.barrier()
                    P.emit()

        with ExitStack() as msc:
            mstg = Ring(nc, msc, "modstg", 2, [128, KC, 256], F32)
            macc = Ring(nc, msc, "modacc", 2, [128, 256], F32)
            maccA = Ring(nc, msc, "modaccA", 6, [128, 256], F32)
            zt = msc.enter_context(nc.sbuf_tensor(_u("zt"), [128, UP], BF16))
            Bz = Buf("zt")
            P.op("dve", lambda e: e.memset(zt[:], 0.0), writes=[Bz])
            for kc in range(KC):
                upv = UPD[kc * 128:(kc + 1) * 128, :]
                P.dma("sp", lambda e, upv=upv: e.dma_start(out=upv[:, 0:UP], in_=zt[:]), reads=[Bz])
                P.dma("sp", lambda e, upv=upv: e.dma_start(out=upv[:, UP + S:UP + S + UP], in_=zt[:]), reads=[Bz])
            for c in layer_casts(0):
                emit_cast(c)
            for nb in range(24):
                mod_task(0, nb, mstg, macc, "sp", "dve", maccA)
            mod_flush()
            for ls in range(1, 2 * depth):
                for nb in range(24):
                    bg_mod.append(lambda ls=ls, nb=nb: mod_task(ls, nb, mstg, macc, "pool", "dve", maccA))
            P.barrier()
            P.emit()
            conv_layer(0, xT, yT if (depth == 1 and stop_mixer) else XS)
        if not (depth == 1 and stop_mixer):
            mlp_pass(0, XS, yT if depth == 1 else XS)
        for i in range(1, depth):
            src = XS
            last = (i == depth - 1)
            mdst = yT if (last and stop_mixer) else XS
            if i % 2 == 0:
                conv_layer(i, src, mdst)
            else:
                attn_layer(i, src, mdst)
            if not (last and stop_mixer):
                mlp_pass(i, XS, yT if last else XS)
    return nc


def _pk(v):
    v = np.asarray(v, dtype=np.float32)
    lead = v.shape[:-1]
    n = v.shape[-1] // 128
    v = v.reshape(*lead, n, 128)
    v = np.moveaxis(v, -1, 0)
    return np.ascontiguousarray(v.reshape(128, -1))


def _rope_tables(S):
    grid_w = 64
    t = np.arange(S)
    row = (t // grid_w).astype(np.float32)
    col = (t % grid_w).astype(np.float32)
    half = 32
    inv = (np.float32(10000.0) ** (-np.arange(half, dtype=np.float32) / np.float32(half))).astype(np.float32)
    ang_r = row[None, :] * inv[:, None]
    ang_c = col[None, :] * inv[:, None]
    cos = np.concatenate([np.cos(ang_r), np.cos(ang_r), np.cos(ang_c), np.cos(ang_c)], axis=0)
    sin = np.concatenate([np.sin(ang_r), np.sin(ang_r), np.sin(ang_c), np.sin(ang_c)], axis=0)
    return np.stack([cos, sin]).astype(np.float32)


def _consts():
    c = np.zeros((128, 256), np.float32)
    c[:, :128] = np.eye(128, dtype=np.float32)
    for m in range(128):
        if m % 64 < 32:
            c[m + 32, 128 + m] = -1.0
        else:
            c[m - 32, 128 + m] = 1.0
    return c


def make_in_maps(inp, S, nb):
    f = lambda k: np.ascontiguousarray(np.asarray(inp[k], dtype=np.float32))
    shared = {
        "cst": _consts(),
        "rope": _rope_tables(S),
        "mod_w": f("mod_w").reshape(8, D, 3 * D),
        "conv_w_in": f("conv_w_in"), "conv_w_out": f("conv_w_out"),
        "attn_w_qkv": f("attn_w_qkv"), "attn_w_out": f("attn_w_out"),
        "mlp_w_in": f("mlp_w_in"), "mlp_w_out": f("mlp_w_out"),
    }
    x = f("x")
    c = f("c")
    cdw = np.asarray(inp["conv_dw"], np.float32)
    maps = []
    for b in range(nb):
        cols = [
            _pk(c[b]),
            _pk(f("mod_b").reshape(8, 3 * D)),
            _pk(f("ln_g").reshape(8, D)),
            _pk(f("ln_b").reshape(8, D)),
            _pk(f("conv_b_in")),
            _pk(cdw),
            _pk(f("conv_dw_b")), _pk(f("conv_ln_g")), _pk(f("conv_ln_b")),
            np.ascontiguousarray(np.asarray(inp["attn_q_norm"], np.float32).T),
            np.ascontiguousarray(np.asarray(inp["attn_k_norm"], np.float32).T),
        ]
        vecs = np.ascontiguousarray(np.concatenate(cols, axis=1))
        assert vecs.shape == (128, NV), vecs.shape
        m = dict(shared)
        m["xT"] = np.ascontiguousarray(x[b, :S].T)
        m["vecs"] = vecs
        maps.append(m)
    return maps


_NC_CACHE = {}


def kernel(**inputs):
    x = np.asarray(inputs["x"])
    B, S, _ = x.shape
    key = (S, DEPTH)
    if key not in _NC_CACHE:
        _NC_CACHE[key] = build(S, DEPTH)
    nc = _NC_CACHE[key]
    maps = make_in_maps(inputs, S, B)
    res = run_bass_kernel_spmd(nc, maps, core_ids=list(range(B)))
    out = np.stack([np.ascontiguousarray(r["yT"].T) for r in res.results], axis=0)
    return out.astype(np.float32)
```

```python
import numpy as np
from contextlib import ExitStack
import concourse.bass as bass
import concourse.mybir as mybir
from concourse.bass_utils import run_bass_kernel_spmd

F32 = mybir.dt.float32
BF16 = mybir.dt.bfloat16
AF = mybir.ActivationFunctionType
ALU = mybir.AluOpType

D = 2048
KC = 16
T = 512
DEPTH = 4
DFF = 8192
ALPHA = (2 * DEPTH) ** 0.25
LN_EPS = 1e-5
RMS_EPS = 1e-6
CW = 31
UP = 16

V_C = 0
V_MODB = V_C + 16
V_LNG = V_MODB + 8 * 48
V_LNB = V_LNG + 8 * 16
V_CBIN = V_LNB + 8 * 16
V_CDW = V_CBIN + 2 * 32
V_CDWB = V_CDW + 2 * 31 * 16
V_CLNG = V_CDWB + 2 * 16
V_CLNB = V_CLNG + 2 * 16
V_QN = V_CLNB + 2 * 16
V_KN = V_QN + 2
NV = V_KN + 2


_UC = [0]


def _u(name):
    _UC[0] += 1
    return f"{name}_{_UC[0]}"


class Buf:
    __slots__ = ("name", "w", "r")

    def __init__(self, name=""):
        self.name = name
        self.w = None
        self.r = {}


class Prog:
    COMPUTE = ("pe", "act", "dve", "pool")
    ENG = ("pe", "act", "dve", "pool", "sp")

    def __init__(self, nc, stack, n_dma_sems=14):
        self.nc = nc
        self.stack = stack
        self.ops = {e: [] for e in self.ENG}
        self.sems = []
        self.own = {}
        self.cnt = {}
        self.n_own = 0
        self.sig = {}
        for e in self.COMPUTE:
            self._new_own(e)
        self.seen = {e: {} for e in self.ENG}
        self.dma_sems = {}
        self.dma_rr = {}
        self.dma_val = {}
        for q in ("sp", "pool"):
            lst = []
            for i in range(n_dma_sems):
                lst.append(len(self.sems))
                self.dma_val[len(self.sems)] = 0
                self.sems.append(stack.enter_context(nc.semaphore(f"d_{q}{i}")))
            self.dma_sems[q] = lst
            self.dma_rr[q] = 0
        self.nops = 0

    def _new_own(self, e):
        self.own[e] = len(self.sems)
        self.sems.append(self.stack.enter_context(self.nc.semaphore(f"s_{e}{self.n_own}")))
        self.n_own += 1
        self.cnt[e] = 0

    def _deps(self, e, reads, writes, extra=()):
        need = {}
        for b in reads:
            if b.w is not None:
                s, v = b.w
                if need.get(s, 0) < v:
                    need[s] = v
        for b in writes:
            if b.w is not None:
                s, v = b.w
                if need.get(s, 0) < v:
                    need[s] = v
            for s, v in b.r.items():
                if need.get(s, 0) < v:
                    need[s] = v
        for s, v in extra:
            if need.get(s, 0) < v:
                need[s] = v
        own = self.own.get(e)
        seen = self.seen[e]
        for s, v in need.items():
            if s == own and e == "pe":
                continue
            if seen.get(s, 0) >= v:
                continue
            seen[s] = v
            self.ops[e].append(("wait", s, v))

    def _mark(self, tok, reads, writes):
        s, v = tok
        for b in reads:
            if b.r.get(s, 0) < v:
                b.r[s] = v
        for b in writes:
            b.w = tok
            b.r = {}

    def op(self, e, fn, reads=(), writes=(), signal=True):
        self._deps(e, reads, writes)
        tok = (self.own[e], self.cnt[e] + 1)
        if signal:
            self.cnt[e] += 1
        self.ops[e].append(("op", fn, self.own[e] if signal else None))
        if signal:
            self.sig[self.own[e]] = self.cnt[e]
        self._mark(tok, reads, writes)
        self.nops += 1
        if signal and self.cnt[e] >= 28000:
            self._new_own(e)

    def dma(self, q, fn, reads=(), writes=()):
        slot = self.dma_rr[q]
        self.dma_rr[q] = (slot + 1) % len(self.dma_sems[q])
        s = self.dma_sems[q][slot]
        prev = self.dma_val[s]
        self._deps(q, reads, writes, extra=((s, prev),) if prev else ())
        tok = (s, prev + 16)
        self.dma_val[s] = prev + 16
        self.ops[q].append(("dma", fn, s))
        self._mark(tok, reads, writes)
        self.nops += 1

    def barrier(self):
        toks = [(s, v) for s, v in self.sig.items() if v > 0]
        for s, v in self.dma_val.items():
            if v > 0:
                toks.append((s, v))
        for e in self.ENG:
            seen = self.seen[e]
            for s, v in toks:
                if s == self.own.get(e):
                    continue
                if seen.get(s, 0) >= v:
                    continue
                seen[s] = v
                self.ops[e].append(("wait", s, v))

    def _replay(self, e, eng):
        for item in self.ops[e]:
            if item[0] == "wait":
                eng.wait_ge(self.sems[item[1]], item[2])
            elif item[0] == "op":
                ins = item[1](eng)
                if item[2] is not None:
                    ins.then_inc(self.sems[item[2]], 1)
            else:
                item[1](eng).then_inc(self.sems[item[2]], 16)
        self.ops[e] = []

    def emit(self):
        with self.nc.Block() as block:
            @block.tensor
            def _(eng):
                self._replay("pe", eng)

            @block.scalar
            def _(eng):
                self._replay("act", eng)

            @block.vector
            def _(eng):
                self._replay("dve", eng)

            @block.gpsimd
            def _(eng):
                self._replay("pool", eng)

            @block.sync
            def _(eng):
                self._replay("sp", eng)


class Ring:
    def __init__(self, nc, st, name, n, shape, dt):
        self.t = [st.enter_context(nc.sbuf_tensor(_u(f"{name}{i}"), shape, dt)) for i in range(n)]
        self.b = [Buf(f"{name}{i}") for i in range(n)]
        self.i = 0
        self.n = n

    def next(self):
        i = self.i
        self.i = (i + 1) % self.n
        return self.t[i], self.b[i]


class G:
    pass


def build(S=4096, depth=DEPTH, stop_mixer=False, dbg=False):
    NT = S // T
    nc = bass.Bass("TRN2", target_bir_lowering=False)
    g = G()
    g.nc = nc
    g.S, g.NT = S, NT
    dt = nc.dram_tensor
    xT = dt("xT", [D, S], F32, kind="ExternalInput").ap()
    vecs = dt("vecs", [128, NV], F32, kind="ExternalInput").ap()
    cst = dt("cst", [128, 256], F32, kind="ExternalInput").ap()
    rope = dt("rope", [2, 128, S], F32, kind="ExternalInput").ap()
    mod_w = dt("mod_w", [8, D, 3 * D], F32, kind="ExternalInput").ap()
    conv_w_in = dt("conv_w_in", [2, D, 2 * D], F32, kind="ExternalInput").ap()
    conv_w_out = dt("conv_w_out", [2, D, D], F32, kind="ExternalInput").ap()
    attn_w_qkv = dt("attn_w_qkv", [2, D, 3072], F32, kind="ExternalInput").ap()
    attn_w_out = dt("attn_w_out", [2, D, D], F32, kind="ExternalInput").ap()
    mlp_w_in = dt("mlp_w_in", [4, D, DFF], F32, kind="ExternalInput").ap()
    mlp_w_out = dt("mlp_w_out", [4, DFF, D], F32, kind="ExternalInput").ap()
    yT = dt("yT", [D, S], F32, kind="ExternalOutput").ap()
    XS = dt("XS", [D, S], F32).ap()
    UPD = dt("UPD", [D, S + 2 * UP], BF16, kind="ExternalOutput" if dbg else "Internal").ap()
    VC = dt("VC", [D, S], F32, kind="ExternalOutput" if dbg else "Internal").ap()
    n_conv = (depth + 1) // 2
    n_attn = depth // 2
    WB = {}
    for j in range(n_conv):
        WB["cin", j] = dt(f"wb_cin{j}", [8, 128, KC * 512], BF16).ap()
        WB["cout", j] = dt(f"wb_cout{j}", [4, 128, KC * 512], BF16).ap()
    for j in range(n_attn):
        WB["qkv", j] = dt(f"wb_qkv{j}", [6, 128, KC * 512], BF16).ap()
        WB["aout", j] = dt(f"wb_aout{j}", [4, 128, KC * 512], BF16).ap()
    for i in range(depth):
        WB["min", i] = dt(f"wb_min{i}", [16, 128, KC * 512], BF16).ap()
        WB["mout", i] = dt(f"wb_mout{i}", [16, 128, 32 * 256], BF16).ap()

    def xtile(ap, j):
        return ap.rearrange("(kc p) s -> p kc s", p=128)[:, :, j * T:(j + 1) * T]

    with ExitStack() as st:
        P = Prog(nc, st)
        g.P = P
        vec = st.enter_context(nc.sbuf_tensor(_u("vec"), [128, NV], F32))
        cst32 = st.enter_context(nc.sbuf_tensor(_u("cst32"), [128, 256], F32))
        ident_bf = st.enter_context(nc.sbuf_tensor(_u("ident_bf"), [128, 128], BF16))
        ones_bf = st.enter_context(nc.sbuf_tensor(_u("ones_bf"), [128, 128], BF16))
        cact = st.enter_context(nc.sbuf_tensor(_u("cact"), [128, 16], F32))
        modv = st.enter_context(nc.sbuf_tensor(_u("modv"), [128, 8, 48], F32))
        gqk = st.enter_context(nc.sbuf_tensor(_u("gqk"), [128, 4], F32))
        Bconst = Buf("const")
        Bmodv = Buf("modv")
        ps = [st.enter_context(nc.psum_tensor(f"ps{i}", [128, 512], F32)) for i in range(8)]
        psb = [Buf(f"ps{i}") for i in range(8)]
        rotT = cst32[:, 128:256]

        P.dma("sp", lambda e: e.dma_start(out=vec[:], in_=vecs), writes=[Bconst])
        P.dma("sp", lambda e: e.dma_start(out=cst32[:], in_=cst), writes=[Bconst])
        P.op("dve", lambda e: e.tensor_copy(out=ident_bf[:], in_=cst32[:, 0:128]), reads=[Bconst], writes=[Bconst])
        P.op("dve", lambda e: e.memset(ones_bf[:], 1.0), writes=[Bconst])
        P.op("act", lambda e: e.activation(out=cact[:], in_=vec[:, V_C:V_C + 16], func=AF.Silu), reads=[Bconst], writes=[Bconst])
        P.op("dve", lambda e: e.tensor_scalar(out=gqk[:], in0=vec[:, V_QN:V_QN + 4], scalar1=float(128.0 ** 0.5), scalar2=None, op0=ALU.mult),
             reads=[Bconst], writes=[Bconst])

        ones32 = st.enter_context(nc.sbuf_tensor(_u("ones32"), [128, 1], F32))
        P.op("dve", lambda e: e.memset(ones32[:], 1.0), writes=[Bconst])
        MODB = 7

        def cast_k2048(w, wb, nblk):
            wv = w.rearrange("(kc p) n -> p kc n", p=128)
            return [(wb[nb].rearrange("p (kc n) -> p kc n", kc=KC), wv[:, :, nb * 512:(nb + 1) * 512]) for nb in range(nblk)]

        def cast_mout(w, wb):
            wv = w.rearrange("(kc p) n -> p kc n", p=128)
            out = []
            for nb2 in range(8):
                for kh in range(2):
                    out.append((wb[nb2 * 2 + kh].rearrange("p (kc n) -> p kc n", kc=32),
                                wv[:, kh * 32:(kh + 1) * 32, nb2 * 256:(nb2 + 1) * 256]))
            return out

        def layer_casts(i):
            j = i // 2
            if i % 2 == 0:
                return cast_k2048(conv_w_in[j], WB["cin", j], 8) + cast_k2048(conv_w_out[j], WB["cout", j], 4)
            return cast_k2048(attn_w_qkv[j], WB["qkv", j], 6) + cast_k2048(attn_w_out[j], WB["aout", j], 4)

        def mlp_casts(i):
            return cast_k2048(mlp_w_in[i], WB["min", i], 16) + cast_mout(mlp_w_out[i], WB["mout", i])

        def emit_cast(c):
            dst, srcv = c
            P.dma("pool", lambda e: e.dma_start(out=dst, in_=srcv))

        bg_cast = []
        cast_mark = {}
        for i in range(depth):
            if i > 0:
                bg_cast += layer_casts(i)
            cast_mark["layer", i] = len(bg_cast)
            bg_cast += mlp_casts(i)
            cast_mark["mlp", i] = len(bg_cast)
        cast_mark["layer", depth] = len(bg_cast)
        bg_cast_pos = [0]

        def bg_cast_until(idx):
            bg_cast_run(max(0, idx - bg_cast_pos[0]))

        def bg_cast_run(n):
            for _ in range(n):
                if bg_cast_pos[0] < len(bg_cast):
                    emit_cast(bg_cast[bg_cast_pos[0]])
                    bg_cast_pos[0] += 1

        def mod_task(ls, nb, stgr, accr, q, eng, maccA):
            tl, tb = stgr.next()
            acc, accb = maccA.next()
            wv = mod_w[ls].rearrange("(kc p) n -> p kc n", p=128)
            P.dma(q, lambda e: e.dma_start(out=tl[:], in_=wv[:, :, nb * 256:(nb + 1) * 256]), writes=[tb])
            P.op(eng, lambda e: e.tensor_scalar(out=acc[:], in0=tl[:, 0, :], scalar1=cact[:, 0:1], scalar2=None, op0=ALU.mult),
                 reads=[tb, Bconst], writes=[accb])
            for kc in range(1, KC):
                if eng == "dve":
                    P.op(eng, lambda e, kc=kc: e.scalar_tensor_tensor(out=acc[:], in0=tl[:, kc, :], scalar=cact[:, kc:kc + 1], in1=acc[:],
                                                                     op0=ALU.mult, op1=ALU.add),
                         reads=[tb, accb, Bconst], writes=[accb])
                else:
                    tm, tmb = accr.next()
                    P.op(eng, lambda e, kc=kc, tm=tm: e.tensor_scalar(out=tm[:], in0=tl[:, kc, :], scalar1=cact[:, kc:kc + 1], scalar2=None, op0=ALU.mult),
                         reads=[tb, Bconst], writes=[tmb])
                    P.op(eng, lambda e, tm=tm: e.tensor_tensor(out=acc[:], in0=acc[:], in1=tm[:], op=ALU.add),
                         reads=[tmb, accb], writes=[accb])
            def pe_part():
                for n4 in range(2):
                    col = ls * 48 + nb * 2 + n4
                    P.op("pe", lambda e, n4=n4, col=col: e.matmul(ps[MODB][:, col:col + 1], acc[:, n4 * 128:(n4 + 1) * 128], ones32[:, 0:1],
                                                                  start=True, stop=True),
                         reads=[accb, Bconst], writes=[psb[MODB]])
                if nb == 23:
                    fin_part()

            def fin_part():
                P.op("dve", lambda e: e.tensor_tensor(out=modv[:, ls, :], in0=ps[MODB][:, ls * 48:(ls + 1) * 48],
                                                      in1=vec[:, V_MODB + ls * 48:V_MODB + (ls + 1) * 48], op=ALU.add),
                     reads=[psb[MODB], Bconst], writes=[Bmodv])
                P.op("dve", lambda e: e.tensor_scalar(out=modv[:, ls, 16:32], in0=modv[:, ls, 16:32], scalar1=1.0, scalar2=None, op0=ALU.add),
                     reads=[Bmodv], writes=[Bmodv])
                P.op("dve", lambda e: e.tensor_scalar(out=modv[:, ls, 32:48], in0=modv[:, ls, 32:48], scalar1=1.0, scalar2=float(1.0 / ALPHA),
                                                      op0=ALU.add, op1=ALU.mult),
                     reads=[Bmodv], writes=[Bmodv])

            mod_pend.append(pe_part)
            while len(mod_pend) > 3:
                mod_pend.pop(0)()

        mod_pend = []

        def mod_flush():
            while mod_pend:
                mod_pend.pop(0)()

        bg_mod = []
        bg_mod_pos = [0]

        def bg_mod_run(n):
            for _ in range(n):
                if bg_mod_pos[0] < len(bg_mod):
                    bg_mod[bg_mod_pos[0]]()
                    bg_mod_pos[0] += 1

        def mv(ls, part, n):
            return modv[:, ls, part * 16 + n:part * 16 + n + 1]

        def vcol(base, n):
            return vec[:, base + n:base + n + 1]

        def make_h(src, j, ls, h, hb, stg):
            sv = xtile(src, j)
            for kc2 in range(0, KC, 2):
                tl, tb = stg.next()
                P.dma("sp", lambda e, tl=tl, kc2=kc2: e.dma_start(out=tl[:], in_=sv[:, kc2:kc2 + 2, :]), writes=[tb])
                for q in range(2):
                    kc = kc2 + q
                    P.op("dve", lambda e, tl=tl, q=q, kc=kc: e.tensor_scalar(
                        out=h[:, kc, :], in0=tl[:, q, :], scalar1=mv(ls, 1, kc), scalar2=mv(ls, 0, kc),
                        op0=ALU.mult, op1=ALU.add), reads=[tb, Bmodv], writes=[hb[kc]])

        class Epi:
            def __init__(self, X, Xb, ls, tmpr, statr, s1, s2):
                self.X, self.Xb, self.ls = X, Xb, ls
                self.tmpr, self.statr = tmpr, statr
                self.s1, self.s2 = s1, s2

            def chunk(self, n, pt, ptb):
                X, Xb, ls = self.X, self.Xb, self.ls
                P.op("dve", lambda e: e.scalar_tensor_tensor(out=X[:, n, :], in0=pt[:], scalar=mv(ls, 2, n), in1=X[:, n, :],
                                                             op0=ALU.mult, op1=ALU.add),
                     reads=[ptb, Xb[n], Bmodv], writes=[Xb[n]])
                zs, zsb = self.tmpr.next()
                zb, zbb = self.tmpr.next()
                P.op("act", lambda e: e.activation(out=zs[:], in_=X[:, n, :], func=AF.Square), reads=[Xb[n]], writes=[zsb])
                P.op("dve", lambda e: e.tensor_copy(out=zb[:], in_=X[:, n, :]), reads=[Xb[n]], writes=[zbb])
                P.op("pe", lambda e: e.matmul(ps[self.s1][:], ones_bf[:], zb[:], start=(n == 0), stop=(n == KC - 1)),
                     reads=[zbb, Bconst], writes=[psb[self.s1]], signal=True)
                P.op("pe", lambda e: e.matmul(ps[self.s2][:], ones_bf[:], zs[:], start=(n == 0), stop=(n == KC - 1)),
                     reads=[zsb, Bconst], writes=[psb[self.s2]], signal=True)

            def stats(self, eps):
                mean, meanb = self.statr.next()
                rstd, rstdb = self.statr.next()
                P.op("dve", lambda e: e.tensor_scalar(out=mean[:], in0=ps[self.s1][:], scalar1=float(1.0 / D), scalar2=None, op0=ALU.mult),
                     reads=[psb[self.s1]], writes=[meanb])
                P.op("dve", lambda e: e.tensor_tensor(out=rstd[:], in0=mean[:], in1=mean[:], op=ALU.mult), reads=[meanb], writes=[rstdb])
                P.op("dve", lambda e: e.scalar_tensor_tensor(out=rstd[:], in0=ps[self.s2][:], scalar=float(1.0 / D), in1=rstd[:],
                                                             op0=ALU.mult, op1=ALU.subtract),
                     reads=[psb[self.s2], rstdb], writes=[rstdb])
                P.op("act", lambda e: e.activation(out=rstd[:], in_=rstd[:], func=AF.Sqrt, bias=float(eps), scale=1.0),
                     reads=[rstdb], writes=[rstdb])
                P.op("dve", lambda e: e.reciprocal(out=rstd[:], in_=rstd[:]), reads=[rstdb], writes=[rstdb])
                return mean, meanb, rstd, rstdb

            def finish(self, gbase, bbase):
                X, Xb = self.X, self.Xb
                mean, meanb, rstd, rstdb = self.stats(LN_EPS / (ALPHA * ALPHA))
                for n in range(KC):
                    P.op("dve", lambda e, n=n: e.tensor_tensor(out=X[:, n, :], in0=X[:, n, :], in1=mean[:], op=ALU.subtract),
                         reads=[Xb[n], meanb], writes=[Xb[n]])
                    P.op("dve", lambda e, n=n: e.tensor_tensor(out=X[:, n, :], in0=X[:, n, :], in1=rstd[:], op=ALU.mult),
                         reads=[Xb[n], rstdb], writes=[Xb[n]])
                    P.op("act", lambda e, n=n: e.activation(out=X[:, n, :], in_=X[:, n, :], func=AF.Identity,
                                                            scale=vcol(gbase, n), bias=vcol(bbase, n)),
                         reads=[Xb[n], Bconst], writes=[Xb[n]])

        def stream(steps, ring, depth_pf=None):
            n = len(steps)
            pf = ring.n - 1 if depth_pf is None else depth_pf
            loaded = {}

            def load(i):
                ap = steps[i][0]
                if ap is None:
                    loaded[i] = (None, None)
                    return
                tl, tb = ring.next()
                P.dma("sp", lambda e: e.dma_start(out=tl[:].rearrange("p a b -> p (a b)"), in_=ap), writes=[tb])
                loaded[i] = (tl, tb)

            for i in range(min(pf, n)):
                load(i)
            for i in range(n):
                if i + pf < n:
                    load(i + pf)
                tl, tb = loaded.pop(i)
                steps[i][1](tl, tb)

        def store_x(X, Xb, dst, j):
            P.dma("sp", lambda e: e.dma_start(out=xtile(dst, j), in_=X[:]), reads=list(Xb))

        def load_x(X, Xb, src, j):
            P.dma("sp", lambda e: e.dma_start(out=X[:], in_=xtile(src, j)), writes=list(Xb))

        def mlp_pass(i, src, dst):
            ls = 2 * i + 1
            with ExitStack() as ph:
                X = ph.enter_context(nc.sbuf_tensor(_u("mX"), [128, KC, T], F32))
                Xb = [Buf(f"mX{n}") for n in range(KC)]
                h = ph.enter_context(nc.sbuf_tensor(_u("mh"), [128, KC, T], BF16))
                hb = [Buf(f"mh{n}") for n in range(KC)]
                uT = ph.enter_context(nc.sbuf_tensor(_u("muT"), [128, 64, T], BF16))
                ub = [Buf(f"mu{n}") for n in range(64)]
                stg = Ring(nc, ph, "mstg", 2, [128, 2, T], F32)
                wr = Ring(nc, ph, "mw", 4, [128, KC, 512], BF16)
                tmpr = Ring(nc, ph, "mtmp", 4, [128, T], BF16)
                rr = Ring(nc, ph, "mrl", 3, [128, T], BF16)
                statr = Ring(nc, ph, "mstat", 2, [128, T], F32)
                steps = []
                pin = [2, 3, 6, 7]
                pin_i = [0]
                make_h(src, 0, ls, h, hb, stg)
                for j in range(NT):
                    epi = Epi(X, Xb, ls, tmpr, statr, 0, 1)
                    for nb in range(16):
                        def c_in(tl, tb, nb=nb):
                            bg_cast_run(1)
                            for hc in range(4):
                                pi = pin[pin_i[0] % 4]
                                pin_i[0] += 1
                                for kc in range(KC):
                                    P.op("pe", lambda e, kc=kc, hc=hc, pi=pi: e.matmul(
                                        ps[pi][:], tl[:, kc, hc * 128:(hc + 1) * 128], h[:, kc, :], start=(kc == 0), stop=(kc == KC - 1)),
                                        reads=[tb, hb[kc]], writes=[psb[pi]], signal=(kc == KC - 1))
                                r, rb = rr.next()
                                P.op("act", lambda e, pi=pi, r=r: e.activation(out=r[:], in_=ps[pi][:], func=AF.Relu),
                                     reads=[psb[pi]], writes=[rb])
                                u = nb * 4 + hc
                                eng = "dve"
                                P.op(eng, lambda e, r=r, u=u: e.tensor_tensor(out=uT[:, u, :], in0=r[:], in1=r[:], op=ALU.mult),
                                     reads=[rb], writes=[ub[u]])
                        steps.append((WB["min", i][nb], c_in))
                    for nb2 in range(8):
                        for kh in range(2):
                            def c_out(tl, tb, nb2=nb2, kh=kh, j=j, epi=epi):
                                if nb2 == 0 and kh == 0:
                                    load_x(X, Xb, src, j)
                                    if j + 1 < NT:
                                        make_h(src, j + 1, ls, h, hb, stg)
                                wv = tl[:].rearrange("p a b -> p (a b)").rearrange("p (kc n) -> p kc n", kc=32)
                                for half in range(2):
                                    pi = 4 + half
                                    for kk in range(32):
                                        kc = kh * 32 + kk
                                        P.op("pe", lambda e, kk=kk, kc=kc, half=half, pi=pi: e.matmul(
                                            ps[pi][:], wv[:, kk, half * 128:(half + 1) * 128], uT[:, kc, :],
                                            start=(kc == 0), stop=(kc == 63)),
                                            reads=[tb, ub[kc]], writes=[psb[pi]], signal=(kk == 31))
                                if kh == 1:
                                    for half in range(2):
                                        epi.chunk(nb2 * 2 + half, ps[4 + half], psb[4 + half])
                                if nb2 == 7 and kh == 1:
                                    epi.finish(V_LNG + ls * 16, V_LNB + ls * 16)
                                    store_x(X, Xb, dst, j)
                            steps.append((WB["mout", i][nb2 * 2 + kh], c_out))
                stream(steps, wr)
                bg_cast_until(cast_mark["layer", i + 1])
                P.barrier()
                P.emit()

        def conv_layer(i, src, dst):
            j_ = i // 2
            ls = 2 * i
            with ExitStack() as ph:
                h = ph.enter_context(nc.sbuf_tensor(_u("ch"), [128, KC, T], BF16))
                hb = [Buf(f"ch{n}") for n in range(KC)]
                stg = Ring(nc, ph, "cstg", 2, [128, 2, T], F32)
                wr = Ring(nc, ph, "cw", 4, [128, KC, 512], BF16)
                sgr = Ring(nc, ph, "csg", 3, [128, T], F32)
                ur = Ring(nc, ph, "cu", 2, [128, KC, T], BF16)
                pin_i = [0]
                steps = []
                make_h(src, 0, ls, h, hb, stg)
                for j in range(NT):
                    u, ubuf = ur.next()
                    for nb in range(4):
                        hold = {}

                        def c_a(tl, tb, hold=hold):
                            hold["a"] = (tl, tb)
                            bg_cast_run(1)
                            bg_mod_run(1)

                        def c_g(tl, tb, nb=nb, hold=hold, u=u, ubuf=ubuf, j=j):
                            ta, tab = hold["a"]
                            for hc in range(4):
                                n = nb * 4 + hc
                                pa = (pin_i[0] % 3) * 2
                                pin_i[0] += 1
                                pg = pa + 1
                                for (wt, wtb, pi) in ((ta, tab, pa), (tl, tb, pg)):
                                    for kc in range(KC):
                                        P.op("pe", lambda e, kc=kc, hc=hc, pi=pi, wt=wt: e.matmul(
                                            ps[pi][:], wt[:, kc, hc * 128:(hc + 1) * 128], h[:, kc, :], start=(kc == 0), stop=(kc == KC - 1)),
                                            reads=[wtb, hb[kc]], writes=[psb[pi]], signal=(kc == KC - 1))
                                sg, sgb = sgr.next()
                                P.op("act", lambda e, pg=pg, sg=sg, n=n: e.activation(out=sg[:], in_=ps[pg][:], func=AF.Sigmoid,
                                                                                   bias=vcol(V_CBIN + j_ * 32 + 16, n), scale=1.0),
                                     reads=[psb[pg], Bconst], writes=[sgb])
                                P.op("dve", lambda e, pa=pa, sg=sg, n=n, u=u: e.scalar_tensor_tensor(
                                    out=u[:, n, :], in0=ps[pa][:], scalar=vcol(V_CBIN + j_ * 32, n), in1=sg[:], op0=ALU.add, op1=ALU.mult),
                                    reads=[psb[pa], sgb, Bconst], writes=[ubuf])
                            if nb == 3:
                                P.dma("sp", lambda e, u=u, j=j: e.dma_start(
                                    out=UPD.rearrange("(kc p) s -> p kc s", p=128)[:, :, UP + j * T:UP + (j + 1) * T], in_=u[:]), reads=[ubuf])
                                if j + 1 < NT:
                                    make_h(src, j + 1, ls, h, hb, stg)
                        steps.append((WB["cin", j_][nb], c_a))
                        steps.append((WB["cin", j_][4 + nb], c_g))
                stream(steps, wr, depth_pf=2)
                bg_mod_run(max(0, 24 - bg_mod_pos[0]))
                mod_flush()
                P.barrier()
                P.emit()
            with ExitStack() as ph:
                urow = Ring(nc, ph, "c2u", 2, [128, S + 2 * UP], BF16)
                vrow = Ring(nc, ph, "c2v", 2, [128, S], F32)
                dgr = Ring(nc, ph, "c2d", 2, [128, CW, 128], BF16)
                pin_i = [0]
                def pre(c):
                    ur_, urb = urow.next()
                    dg, dgb = dgr.next()
                    P.dma("sp", lambda e: e.dma_start(out=ur_[:], in_=UPD[c * 128:(c + 1) * 128, :]), writes=[urb])
                    for tp in range(CW):
                        P.op("act", lambda e, tp=tp: e.activation(
                            out=dg[:, tp, :], in_=ident_bf[:], func=AF.Identity, scale=vcol(V_CDW + j_ * CW * 16 + tp * 16, c)),
                            reads=[Bconst], writes=[dgb])
                    return ur_, urb, dg, dgb

                nxt = pre(0)
                for c in range(KC):
                    ur_, urb, dg, dgb = nxt
                    if c + 1 < KC:
                        nxt = pre(c + 1)
                    vr_, vrb = vrow.next()
                    bg_cast_run(3)
                    bg_mod_run(5)
                    for j in range(NT):
                        pi = pin_i[0] % 7
                        pin_i[0] += 1
                        for tp in range(CW):
                            o = UP + j * T + tp - (CW // 2)
                            P.op("pe", lambda e, dg=dg, tp=tp, o=o, pi=pi, ur_=ur_: e.matmul(
                                ps[pi][:], dg[:, tp, :], ur_[:, o:o + T], start=(tp == 0), stop=(tp == CW - 1)),
                                reads=[dgb, urb], writes=[psb[pi]], signal=(tp == CW - 1))
                        eng = "act" if j % 2 == 0 else "dve"
                        if eng == "act":
                            P.op("act", lambda e, pi=pi, vr_=vr_, j=j, c=c: e.activation(
                                out=vr_[:, j * T:(j + 1) * T], in_=ps[pi][:], func=AF.Identity, bias=vcol(V_CDWB + j_ * 16, c), scale=1.0),
                                reads=[psb[pi], Bconst], writes=[vrb])
                        else:
                            P.op("dve", lambda e, pi=pi, vr_=vr_, j=j, c=c: e.tensor_scalar(
                                out=vr_[:, j * T:(j + 1) * T], in0=ps[pi][:], scalar1=vcol(V_CDWB + j_ * 16, c), scalar2=None, op0=ALU.add),
                                reads=[psb[pi], Bconst], writes=[vrb])
                    P.dma("sp", lambda e, vr_=vr_, c=c: e.dma_start(out=VC[c * 128:(c + 1) * 128, :], in_=vr_[:]), reads=[vrb])
                mod_flush()
                P.barrier()
                P.emit()
            with ExitStack() as ph:
                Xr = [(ph.enter_context(nc.sbuf_tensor(_u("cX"), [128, KC, T], F32)), [Buf(f"cX{q}_{n}") for n in range(KC)]) for q in range(2)]
                sTr = Ring(nc, ph, "csT", 2, [128, KC, T], BF16)
                vstg = Ring(nc, ph, "cvst", 3, [128, 2, T], F32)
                wr = Ring(nc, ph, "c3w", 2, [128, KC, 512], BF16)
                tmpr = Ring(nc, ph, "ctmp", 4, [128, T], BF16)
                statr = Ring(nc, ph, "cstat", 4, [128, T], F32)
                steps = []
                pin_i = [0]
                vt = xtile(VC, 0)
                Bdummy = Buf("vdummy")

                def prep1(j):
                    sv = xtile(VC, j)
                    for k2 in range(0, KC, 2):
                        tl, tb = vstg.next()
                        P.dma("sp", lambda e, tl=tl, k2=k2: e.dma_start(out=tl[:], in_=sv[:, k2:k2 + 2, :]), writes=[tb])
                        for q in range(2):
                            n = k2 + q
                            zs, zsb = tmpr.next()
                            zb, zbb = tmpr.next()
                            P.op("act", lambda e, tl=tl, q=q, zs=zs: e.activation(out=zs[:], in_=tl[:, q, :], func=AF.Square), reads=[tb], writes=[zsb])
                            P.op("dve", lambda e, tl=tl, q=q, zb=zb: e.tensor_copy(out=zb[:], in_=tl[:, q, :]), reads=[tb], writes=[zbb])
                            P.op("pe", lambda e, n=n, zb=zb: e.matmul(ps[2][:], ones_bf[:], zb[:], start=(n == 0), stop=(n == KC - 1)),
                                 reads=[zbb, Bconst], writes=[psb[2]])
                            P.op("pe", lambda e, n=n, zs=zs: e.matmul(ps[3][:], ones_bf[:], zs[:], start=(n == 0), stop=(n == KC - 1)),
                                 reads=[zsb, Bconst], writes=[psb[3]])

                def prep2(j, sT, sb):
                    vepi = Epi(None, None, ls, tmpr, statr, 2, 3)
                    mean, meanb, rstd, rstdb = vepi.stats(LN_EPS)
                    sv = xtile(VC, j)
                    for k2 in range(0, KC, 2):
                        tl, tb = vstg.next()
                        P.dma("sp", lambda e, tl=tl, k2=k2: e.dma_start(out=tl[:], in_=sv[:, k2:k2 + 2, :]), writes=[tb])
                        for q in range(2):
                            n = k2 + q
                            P.op("dve", lambda e, tl=tl, q=q: e.tensor_tensor(out=tl[:, q, :], in0=tl[:, q, :], in1=mean[:], op=ALU.subtract),
                                 reads=[tb, meanb], writes=[tb])
                            P.op("dve", lambda e, tl=tl, q=q: e.tensor_tensor(out=tl[:, q, :], in0=tl[:, q, :], in1=rstd[:], op=ALU.mult),
                                 reads=[tb, rstdb], writes=[tb])
                            P.op("act", lambda e, tl=tl, q=q, n=n: e.activation(out=sT[:, n, :], in_=tl[:, q, :], func=AF.Silu,
                                                                               scale=vcol(V_CLNG + j_ * 16, n), bias=vcol(V_CLNB + j_ * 16, n)),
                                 reads=[tb, Bconst], writes=[sb[n]])

                sTb = {}
                for j in range(NT):
                    sTb[j] = sTr.next() + ([Buf(f"cs{j}_{n}") for n in range(KC)],)
                slot_bufs = {}
                for j in range(NT):
                    slot_bufs.setdefault(j % 2, sTb[j][2])
                    sTb[j] = (sTb[j][0], sTb[j][1], slot_bufs[j % 2])
                prep1(0)
                prep2(0, sTb[0][0], sTb[0][2])
                load_x(Xr[0][0], Xr[0][1], src, 0)
                for j in range(NT):
                    X, Xb = Xr[j % 2]
                    epi = Epi(X, Xb, ls, tmpr, statr, 0, 1)
                    sT, _, sb = sTb[j]
                    for nb in range(4):
                        def c_o(tl, tb, nb=nb, j=j, epi=epi, sT=sT, sb=sb, X=X, Xb=Xb):
                            bg_cast_run(1)
                            bg_mod_run(1)
                            if nb == 0 and j + 1 < NT:
                                load_x(Xr[(j + 1) % 2][0], Xr[(j + 1) % 2][1], src, j + 1)
                            if nb == 1 and j + 1 < NT:
                                prep1(j + 1)
                            if nb == 2 and j + 1 < NT:
                                prep2(j + 1, sTb[j + 1][0], sTb[j + 1][2])
                            for hc in range(4):
                                n = nb * 4 + hc
                                pi = 4 + (pin_i[0] % 3)
                                pin_i[0] += 1
                                for kc in range(KC):
                                    P.op("pe", lambda e, kc=kc, hc=hc, pi=pi: e.matmul(
                                        ps[pi][:], tl[:, kc, hc * 128:(hc + 1) * 128], sT[:, kc, :], start=(kc == 0), stop=(kc == KC - 1)),
                                        reads=[tb, sb[kc]], writes=[psb[pi]], signal=(kc == KC - 1))
                                epi.chunk(n, ps[pi], psb[pi])
                            if nb == 3:
                                epi.finish(V_LNG + ls * 16, V_LNB + ls * 16)
                                store_x(X, Xb, dst, j)
                        steps.append((WB["cout", j_][nb], c_o))
                stream(steps, wr)
                bg_mod_run(10 ** 6)
                mod_flush()
                bg_cast_until(cast_mark["mlp", i])
                P.barrier()
                P.emit()

        def attn_layer(i, src, dst):
            j_ = i // 2
            ls = 2 * i
            NKC = S // 128
            with ExitStack() as al:
                KT = al.enter_context(nc.sbuf_tensor(_u("KT"), [128, 4, S], BF16))
                Vs = al.enter_context(nc.sbuf_tensor(_u("Vs"), [128, NKC, 512], BF16))
                KTb = [[Buf(f"KT{kv}_{j}") for j in range(NT)] for kv in range(4)]
                Vsb = [Buf(f"Vs{c}") for c in range(NKC)]
                rope_rings = [None, None]
                gen_i = [0]

                gen_banks = [[4, 5, 6, 7]]

                def gbank():
                    gb = gen_banks[0]
                    pi = gb[gen_i[0] % len(gb)]
                    gen_i[0] += 1
                    return pi

                def load_rope(j):
                    ct, cb = rope_rings[0].next()
                    sn, snb = rope_rings[1].next()
                    P.dma("sp", lambda e: e.dma_start(out=ct[:], in_=rope[0][:, j * T:(j + 1) * T]), writes=[cb])
                    P.dma("sp", lambda e: e.dma_start(out=sn[:], in_=rope[1][:, j * T:(j + 1) * T]), writes=[snb])
                    return ct, cb, sn, snb

                def norm_rope_stages(proj_fn, gcol, rp, out_ap, out_bufs, sqr, f32r):
                    ct, cb, sn, snb = rp
                    stt = {}

                    def A():
                        pi = gbank()
                        proj_fn(pi)
                        sq, sqb = sqr.next()
                        P.op("act", lambda e: e.activation(out=sq[:], in_=ps[pi][:], func=AF.Square), reads=[psb[pi]], writes=[sqb])
                        stt["pi"], stt["sq"], stt["sqb"] = pi, sq, sqb

                    def B():
                        pi, sq, sqb = stt["pi"], stt["sq"], stt["sqb"]
                        p2 = gbank()
                        P.op("pe", lambda e: e.matmul(ps[p2][:], ones_bf[:], sq[:], start=True, stop=True),
                             reads=[sqb, Bconst], writes=[psb[p2]])
                        rs, rsb = f32r.next()
                        P.op("act", lambda e: e.activation(out=rs[:], in_=ps[p2][:], func=AF.Ln, bias=float(128.0 * RMS_EPS), scale=1.0),
                             reads=[psb[p2]], writes=[rsb])
                        P.op("act", lambda e: e.activation(out=rs[:], in_=rs[:], func=AF.Exp, scale=-0.5), reads=[rsb], writes=[rsb])
                        qn, qnb = f32r.next()
                        P.op("dve", lambda e: e.scalar_tensor_tensor(out=qn[:], in0=ps[pi][:], scalar=gcol, in1=rs[:], op0=ALU.mult, op1=ALU.mult),
                             reads=[psb[pi], rsb, Bconst], writes=[qnb])
                        stt["qn"], stt["qnb"], stt["rs"], stt["rsb"] = qn, qnb, rs, rsb

                    def C():
                        qn, qnb, rs, rsb = stt["qn"], stt["qnb"], stt["rs"], stt["rsb"]
                        p3 = gbank()
                        P.op("pe", lambda e: e.matmul(ps[p3][:], rotT, qn[:], start=True, stop=True),
                             reads=[qnb, Bconst], writes=[psb[p3]])
                        P.op("dve", lambda e: e.tensor_tensor(out=rs[:], in0=ps[p3][:], in1=sn[:], op=ALU.mult),
                             reads=[psb[p3], snb], writes=[rsb])
                        P.op("dve", lambda e: e.tensor_tensor(out=qn[:], in0=qn[:], in1=ct[:], op=ALU.mult),
                             reads=[qnb, cb], writes=[qnb])
                        P.op("dve", lambda e: e.tensor_tensor(out=out_ap, in0=qn[:], in1=rs[:], op=ALU.add),
                             reads=[qnb, rsb], writes=out_bufs)

                    return [A, B, C]

                with ExitStack() as ph:
                    h = ph.enter_context(nc.sbuf_tensor(_u("ah"), [128, KC, T], BF16))
                    hb = [Buf(f"ah{n}") for n in range(KC)]
                    stg = Ring(nc, ph, "astg", 2, [128, 2, T], F32)
                    rope_rings[0] = Ring(nc, ph, "cosr1", 2, [128, T], F32)
                    rope_rings[1] = Ring(nc, ph, "sinr1", 2, [128, T], F32)
                    sqr1 = Ring(nc, ph, "sqr1", 4, [128, T], BF16)
                    f32r1 = Ring(nc, ph, "f32r1", 8, [128, T], F32)
                    wk = ph.enter_context(nc.sbuf_tensor(_u("awk"), [128, KC, 512], BF16))
                    wv_ = ph.enter_context(nc.sbuf_tensor(_u("awv"), [128, KC, 512], BF16))
                    Bwk, Bwv = Buf("wk"), Buf("wv")
                    P.dma("sp", lambda e: e.dma_start(out=wk[:].rearrange("p a b -> p (a b)"), in_=WB["qkv", j_][4]), writes=[Bwk])
                    P.dma("sp", lambda e: e.dma_start(out=wv_[:].rearrange("p a b -> p (a b)"), in_=WB["qkv", j_][5]), writes=[Bwv])
                    def vproj(j, tc4):
                        pi = 6 + (tc4 % 2)
                        for kc in range(KC):
                            P.op("pe", lambda e, kc=kc, tc4=tc4, pi=pi: e.matmul(ps[pi][:], h[:, kc, tc4 * 128:(tc4 + 1) * 128], wv_[:, kc, :],
                                                                                  start=(kc == 0), stop=(kc == KC - 1)),
                                 reads=[Bwv, hb[kc]], writes=[psb[pi]], signal=(kc == KC - 1))
                        c = j * 4 + tc4
                        if tc4 % 2 == 0:
                            P.op("act", lambda e, c=c, pi=pi: e.activation(out=Vs[:, c, :], in_=ps[pi][:], func=AF.Copy), reads=[psb[pi]], writes=[Vsb[c]])
                        else:
                            P.op("dve", lambda e, c=c, pi=pi: e.tensor_copy(out=Vs[:, c, :], in_=ps[pi][:]), reads=[psb[pi]], writes=[Vsb[c]])

                    for j in range(NT):
                        make_h(src, j, ls, h, hb, stg)
                        rp = load_rope(j)
                        stgs = []
                        for kv in range(4):
                            def proj(pi, kv=kv):
                                for kc in range(KC):
                                    P.op("pe", lambda e, kc=kc: e.matmul(ps[pi][:], wk[:, kc, kv * 128:(kv + 1) * 128], h[:, kc, :],
                                                                          start=(kc == 0), stop=(kc == KC - 1)),
                                         reads=[Bwk, hb[kc]], writes=[psb[pi]], signal=(kc == KC - 1))
                            stgs.append(norm_rope_stages(proj, gqk[:, 2 + j_:3 + j_], rp, KT[:, kv, j * T:(j + 1) * T], [KTb[kv][j]], sqr1, f32r1))
                        gen_banks[0] = [0, 1, 2, 3]
                        gen_i[0] = 0
                        for kv in range(4):
                            stgs[kv][0]()
                        vproj(j, 0)
                        vproj(j, 1)
                        gen_banks[0] = [4, 5]
                        gen_i[0] = 0
                        for kv in range(4):
                            stgs[kv][1]()
                        vproj(j, 2)
                        vproj(j, 3)
                        for kv in range(4):
                            stgs[kv][2]()
                    P.barrier()
                    P.emit()

                with ExitStack() as ph:
                    X = ph.enter_context(nc.sbuf_tensor(_u("aX"), [128, KC, T], F32))
                    Xb = [Buf(f"aX{n}") for n in range(KC)]
                    h = ph.enter_context(nc.sbuf_tensor(_u("a2h"), [128, KC, T], BF16))
                    hb = [Buf(f"a2h{n}") for n in range(KC)]
                    stg = Ring(nc, ph, "a2stg", 2, [128, 1, T], F32)
                    qr = Ring(nc, ph, "a2q", 2, [128, T], BF16)
                    OT = ph.enter_context(nc.sbuf_tensor(_u("a2O"), [128, KC, T], BF16))
                    Ob = [Buf(f"a2O{n}") for n in range(KC)]
                    pr = Ring(nc, ph, "a2p", 6, [128, T], BF16)
                    wqr = Ring(nc, ph, "a2wq", 2, [128, KC, 512], BF16)
                    tmpr = Ring(nc, ph, "a2tmp", 4, [128, T], BF16)
                    statr = Ring(nc, ph, "a2stat", 2, [128, T], F32)
                    rcr = Ring(nc, ph, "a2rc", 2, [128, T], F32)
                    rope_rings[0] = Ring(nc, ph, "cosr2", 1, [128, T], F32)
                    rope_rings[1] = Ring(nc, ph, "sinr2", 1, [128, T], F32)
                    sqr2 = Ring(nc, ph, "sqr2", 2, [128, T], BF16)
                    f32r2 = Ring(nc, ph, "f32r2", 2, [128, T], F32)
                    scale = float(128.0 ** -0.5)

                    def make_h1(j):
                        sv = xtile(src, j)
                        for kc in range(KC):
                            tl, tb = stg.next()
                            P.dma("sp", lambda e, tl=tl, kc=kc: e.dma_start(out=tl[:], in_=sv[:, kc:kc + 1, :]), writes=[tb])
                            P.op("dve", lambda e, tl=tl, kc=kc: e.tensor_scalar(
                                out=h[:, kc, :], in0=tl[:, 0, :], scalar1=mv(ls, 1, kc), scalar2=mv(ls, 0, kc),
                                op0=ALU.mult, op1=ALU.add), reads=[tb, Bmodv], writes=[hb[kc]])

                    units = [(j, hd) for j in range(NT) for hd in range(16)]
                    wq_cur = {}
                    rope_cur = {}

                    def prep(u):
                        j, hd = units[u]
                        if hd == 0:
                            rope_cur[j] = load_rope(j)
                        if hd % 4 == 0:
                            tl, tb = wqr.next()
                            blk = WB["qkv", j_][hd // 4]
                            P.dma("sp", lambda e: e.dma_start(out=tl[:].rearrange("p a b -> p (a b)"), in_=blk), writes=[tb])
                            wq_cur[(j, hd // 4)] = (tl, tb)
                        tl, tb = wq_cur[(j, hd // 4)]
                        q, qb = qr.next()

                        def proj(pi):
                            hh = hd % 4
                            for kc in range(KC):
                                P.op("pe", lambda e, kc=kc: e.matmul(ps[pi][:], tl[:, kc, hh * 128:(hh + 1) * 128], h[:, kc, :],
                                                                      start=(kc == 0), stop=(kc == KC - 1)),
                                     reads=[tb, hb[kc]], writes=[psb[pi]], signal=(kc == KC - 1))
                        return norm_rope_stages(proj, gqk[:, j_:j_ + 1], rope_cur[j], q[:], [qb], sqr2, f32r2), (q, qb)

                    gen_banks[0] = [6, 7]
                    gen_i[0] = 0

                    def run_tile(j, qs):
                        us = [u for u in range(len(units)) if units[u][0] == j]
                        its = [(u, kc) for u in us for kc in range(NKC)]
                        pend = {}
                        hk = [0, max(1, NKC // 4), max(2, NKC // 2)]

                        def qk(i):
                            u, kc = its[i]
                            q, qb = qs[u]
                            kv = units[u][1] // 4
                            bk = i % 3
                            P.op("pe", lambda e: e.matmul(ps[bk][:], KT[:, kv, kc * 128:(kc + 1) * 128], q[:], start=True, stop=True),
                                 reads=[KTb[kv][kc // 4], qb], writes=[psb[bk]])
                            p, pb = pr.next()
                            P.op("act", lambda e: e.activation(out=p[:], in_=ps[bk][:], func=AF.Exp, scale=scale),
                                 reads=[psb[bk]], writes=[pb])
                            pend[i] = (p, pb)

                        for i0 in range(min(3, len(its))):
                            qk(i0)
                        for i in range(len(its)):
                            u, kc = its[i]
                            hd = units[u][1]
                            kv = hd // 4
                            OACC, SUMS = 3 + (hd % 2), 5
                            if hd < 15:
                                if kc == hk[0]:
                                    st_, qq = prep(u + 1)
                                    qs[u + 1] = qq
                                    qs[("st", u + 1)] = st_
                                    st_[0]()
                                elif kc == hk[1]:
                                    qs[("st", u + 1)][1]()
                                elif kc == hk[2]:
                                    qs[("st", u + 1)][2]()
                            elif kc == hk[0] and j + 1 < NT:
                                make_h1(j + 1)
                            if hd == 8 and kc == hk[0]:
                                load_x(X, Xb, src, j)
                            if hd == 15 and kc == hk[1]:
                                tl0, tb0 = wqr.next()
                                blk0 = WB["aout", j_][0]
                                P.dma("sp", lambda e, tl0=tl0, blk0=blk0: e.dma_start(out=tl0[:].rearrange("p a b -> p (a b)"), in_=blk0), writes=[tb0])
                                wo_pref[j] = (tl0, tb0)
                            p, pb = pend.pop(i)
                            P.op("pe", lambda e, kc=kc, kv=kv, p=p, OACC=OACC: e.matmul(ps[OACC][:], Vs[:, kc, kv * 128:(kv + 1) * 128], p[:],
                                                                              start=(kc == 0), stop=(kc == NKC - 1)),
                                 reads=[Vsb[kc], pb], writes=[psb[OACC]], signal=False)
                            P.op("pe", lambda e, kc=kc, p=p, SUMS=SUMS: e.matmul(ps[SUMS][:], ones_bf[:], p[:], start=(kc == 0), stop=(kc == NKC - 1)),
                                 reads=[pb, Bconst], writes=[psb[SUMS]], signal=True)
                            if i + 3 < len(its):
                                qk(i + 3)
                            if kc == NKC - 1:
                                rc, rcb = rcr.next()
                                P.op("dve", lambda e, rc=rc, SUMS=SUMS: e.tensor_copy(out=rc[:], in_=ps[SUMS][:]), reads=[psb[SUMS]], writes=[rcb])
                                P.op("dve", lambda e, rc=rc: e.reciprocal(out=rc[:], in_=rc[:]), reads=[rcb], writes=[rcb])
                                P.op("dve", lambda e, rc=rc, hd=hd, OACC=OACC: e.tensor_tensor(out=OT[:, hd, :], in0=ps[OACC][:], in1=rc[:], op=ALU.mult),
                                     reads=[psb[OACC], rcb], writes=[Ob[hd]])

                    wo_pref = {}

                    def out_proj(j):
                        epi = Epi(X, Xb, ls, tmpr, statr, 0, 1)
                        for nb in range(4):
                            if nb == 0:
                                tl, tb = wo_pref.pop(j)
                            else:
                                tl, tb = wqr.next()
                                blk = WB["aout", j_][nb]
                                P.dma("sp", lambda e, tl=tl, blk=blk: e.dma_start(out=tl[:].rearrange("p a b -> p (a b)"), in_=blk), writes=[tb])
                            for hc in range(4):
                                n = nb * 4 + hc
                                pi = gbank()
                                for kc in range(KC):
                                    P.op("pe", lambda e, kc=kc, hc=hc, pi=pi, tl=tl: e.matmul(
                                        ps[pi][:], tl[:, kc, hc * 128:(hc + 1) * 128], OT[:, kc, :], start=(kc == 0), stop=(kc == KC - 1)),
                                        reads=[tb, Ob[kc]], writes=[psb[pi]], signal=(kc == KC - 1))
                                epi.chunk(n, ps[pi], psb[pi])
                        epi.finish(V_LNG + ls * 16, V_LNB + ls * 16)
                        store_x(X, Xb, dst, j)

                    make_h1(0)
                    for j in range(NT):
                        u0 = j * 16
                        st_, qq = prep(u0)
                        for s_ in st_:
                            s_()
                        qs = {u0: qq}
                        run_tile(j, qs)
                        out_proj(j)
                    bg_cast_until(cast_mark["mlp", i])
                    P.barrier()
                    P.emit()

        with ExitStack() as msc:
            mstg = Ring(nc, msc, "modstg", 2, [128, KC, 256], F32)
            macc = Ring(nc, msc, "modacc", 2, [128, 256], F32)
            maccA = Ring(nc, msc, "modaccA", 6, [128, 256], F32)
            zt = msc.enter_context(nc.sbuf_tensor(_u("zt"), [128, UP], BF16))
            Bz = Buf("zt")
            P.op("dve", lambda e: e.memset(zt[:], 0.0), writes=[Bz])
            for kc in range(KC):
                upv = UPD[kc * 128:(kc + 1) * 128, :]
                P.dma("sp", lambda e, upv=upv: e.dma_start(out=upv[:, 0:UP], in_=zt[:]), reads=[Bz])
                P.dma("sp", lambda e, upv=upv: e.dma_start(out=upv[:, UP + S:UP + S + UP], in_=zt[:]), reads=[Bz])
            for c in layer_casts(0):
                emit_cast(c)
            for nb in range(24):
                mod_task(0, nb, mstg, macc, "sp", "dve", maccA)
            mod_flush()
            for ls in range(1, 2 * depth):
                for nb in range(24):
                    bg_mod.append(lambda ls=ls, nb=nb: mod_task(ls, nb, mstg, macc, "pool", "dve", maccA))
            P.barrier()
            P.emit()
            conv_layer(0, xT, yT if (depth == 1 and stop_mixer) else XS)
        if not (depth == 1 and stop_mixer):
            mlp_pass(0, XS, yT if depth == 1 else XS)
        for i in range(1, depth):
            src = XS
            last = (i == depth - 1)
            mdst = yT if (last and stop_mixer) else XS
            if i % 2 == 0:
                conv_layer(i, src, mdst)
            else:
                attn_layer(i, src, mdst)
            if not (last and stop_mixer):
                mlp_pass(i, XS, yT if last else XS)
    return nc


def _pk(v):
    v = np.asarray(v, dtype=np.float32)
    lead = v.shape[:-1]
    n = v.shape[-1] // 128
    v = v.reshape(*lead, n, 128)
    v = np.moveaxis(v, -1, 0)
    return np.ascontiguousarray(v.reshape(128, -1))


def _rope_tables(S):
    grid_w = 64
    t = np.arange(S)
    row = (t // grid_w).astype(np.float32)
    col = (t % grid_w).astype(np.float32)
    half = 32
    inv = (np.float32(10000.0) ** (-np.arange(half, dtype=np.float32) / np.float32(half))).astype(np.float32)
    ang_r = row[None, :] * inv[:, None]
    ang_c = col[None, :] * inv[:, None]
    cos = np.concatenate([np.cos(ang_r), np.cos(ang_r), np.cos(ang_c), np.cos(ang_c)], axis=0)
    sin = np.concatenate([np.sin(ang_r), np.sin(ang_r), np.sin(ang_c), np.sin(ang_c)], axis=0)
    return np.stack([cos, sin]).astype(np.float32)


def _consts():
    c = np.zeros((128, 256), np.float32)
    c[:, :128] = np.eye(128, dtype=np.float32)
    for m in range(128):
        if m % 64 < 32:
            c[m + 32, 128 + m] = -1.0
        else:
            c[m - 32, 128 + m] = 1.0
    return c


def make_in_maps(inp, S, nb):
    f = lambda k: np.ascontiguousarray(np.asarray(inp[k], dtype=np.float32))
    shared = {
        "cst": _consts(),
        "rope": _rope_tables(S),
        "mod_w": f("mod_w").reshape(8, D, 3 * D),
        "conv_w_in": f("conv_w_in"), "conv_w_out": f("conv_w_out"),
        "attn_w_qkv": f("attn_w_qkv"), "attn_w_out": f("attn_w_out"),
        "mlp_w_in": f("mlp_w_in"), "mlp_w_out": f("mlp_w_out"),
    }
    x = f("x")
    c = f("c")
    cdw = np.asarray(inp["conv_dw"], np.float32)
    maps = []
    for b in range(nb):
        cols = [
            _pk(c[b]),
            _pk(f("mod_b").reshape(8, 3 * D)),
            _pk(f("ln_g").reshape(8, D)),
            _pk(f("ln_b").reshape(8, D)),
            _pk(f("conv_b_in")),
            _pk(cdw),
            _pk(f("conv_dw_b")), _pk(f("conv_ln_g")), _pk(f("conv_ln_b")),
            np.ascontiguousarray(np.asarray(inp["attn_q_norm"], np.float32).T),
            np.ascontiguousarray(np.asarray(inp["attn_k_norm"], np.float32).T),
        ]
        vecs = np.ascontiguousarray(np.concatenate(cols, axis=1))
        assert vecs.shape == (128, NV), vecs.shape
        m = dict(shared)
        m["xT"] = np.ascontiguousarray(x[b, :S].T)
        m["vecs"] = vecs
        maps.append(m)
    return maps


_NC_CACHE = {}


def kernel(**inputs):
    x = np.asarray(inputs["x"])
    B, S, _ = x.shape
    key = (S, DEPTH)
    if key not in _NC_CACHE:
        _NC_CACHE[key] = build(S, DEPTH)
    nc = _NC_CACHE[key]
    maps = make_in_maps(inputs, S, B)
    res = run_bass_kernel_spmd(nc, maps, core_ids=list(range(B)))
    out = np.stack([np.ascontiguousarray(r["yT"].T) for r in res.results], axis=0)
    return out.astype(np.float32)
```
